# Optimizing a Trainium2 kernel written in Bass

```python
import jax, jax.numpy as jnp
from jax import lax
import numpy as np

D_MODEL = 1024
BATCH = 2
SEQ = 8192
DEPTH = 1

GRID_W = 64
CTX_LEN = 256
NA_HEADS = 8
NA_HEAD_DIM = 64
NA_WIN_ROWS = 8
NA_WIN_COLS = 16
GLA_HEADS = 4
GLA_DK = 64
GLA_DV = 128
GLA_GATE_RANK = 16
GLA_GATE_TAU = 16.0
GLA_LOG_ALPHA_MIN = -1.0
GLA_CHUNK = 64
ROPE_THETA = 10000.0
D_FF = ((8 * D_MODEL + 3 * 256 - 1) // (3 * 256)) * 256
N_MOD = 6
NORM_EPS = 1e-6
NEG_INF = -1e30
NA_WIDTH = NA_HEADS * NA_HEAD_DIM
GLA_QK_WIDTH = GLA_HEADS * GLA_DK
GLA_V_WIDTH = GLA_HEADS * GLA_DV
IN_SPLITS = (NA_WIDTH, NA_WIDTH, NA_WIDTH, GLA_QK_WIDTH, GLA_QK_WIDTH, GLA_V_WIDTH, GLA_V_WIDTH,
             2 * GLA_GATE_RANK, D_MODEL, D_MODEL)
IN_OFFSETS = tuple(int(o) for o in np.cumsum(IN_SPLITS)[:-1])
D_IN = sum(IN_SPLITS)

kernel_name = "hybrid_na_gla_diffusion_block"


def rmsnorm(x, g):
    xf = x.astype(jnp.float32)
    y = xf * lax.rsqrt(jnp.mean(xf * xf, axis=-1, keepdims=True) + NORM_EPS)
    return (y * g.astype(jnp.float32)).astype(x.dtype)


def modulate(h, shift, scale):
    return h * (1 + scale) + shift


def heads(t, n):
    return t.reshape(*t.shape[:-1], n, -1)


def flip(t):
    return jnp.flip(t, axis=1)


def axial_rope(x, row_pos, col_pos):
    half = x.shape[-1] // 2
    n = half // 2
    freqs = ROPE_THETA ** (-jnp.arange(n, dtype=jnp.float32) / n)

    def rotate(xa, pos):
        ang = pos.astype(jnp.float32)[:, None] * freqs
        cos = jnp.cos(ang)[None, :, None, :]
        sin = jnp.sin(ang)[None, :, None, :]
        x1 = xa[..., :n].astype(jnp.float32)
        x2 = xa[..., n:].astype(jnp.float32)
        return jnp.concatenate([x1 * cos - x2 * sin, x1 * sin + x2 * cos], axis=-1)

    out = jnp.concatenate([rotate(x[..., :half], row_pos), rotate(x[..., half:], col_pos)], axis=-1)
    return out.astype(x.dtype)


def log_decay(z_lr, w_alpha, b_alpha):
    B, L, _ = z_lr.shape
    z = jnp.einsum('bldr,drk->bldk', z_lr.reshape(B, L, 2, GLA_GATE_RANK), w_alpha) + b_alpha
    la = jnp.maximum(jax.nn.log_sigmoid(z.astype(jnp.float32)) / GLA_GATE_TAU, GLA_LOG_ALPHA_MIN)
    la = la.reshape(B, L, 2, GLA_HEADS, GLA_DK)
    return la[:, :, 0], la[:, :, 1]


def gla_chunked(q, k, v, log_a, s0):
    B, L, H, dk = q.shape
    dv = v.shape[-1]
    n = L // GLA_CHUNK
    f32 = jnp.float32
    qc = q.astype(f32).reshape(B, n, GLA_CHUNK, H, dk)
    kc = k.astype(f32).reshape(B, n, GLA_CHUNK, H, dk)
    vc = v.astype(f32).reshape(B, n, GLA_CHUNK, H, dv)
    b = jnp.cumsum(log_a.astype(f32).reshape(B, n, GLA_CHUNK, H, dk), axis=2)
    b_last = b[:, :, -1:]
    q_dec = qc * jnp.exp(b)
    k_inv = kc * jnp.exp(-b)
    k_end = kc * jnp.exp(b_last - b)
    incl = jnp.tril(jnp.ones((GLA_CHUNK, GLA_CHUNK), dtype=bool))
    att = jnp.where(incl, jnp.einsum('bnihd,bnjhd->bnhij', q_dec, k_inv), 0.0)
    o_intra = jnp.einsum('bnhij,bnjhv->bnihv', att, vc)
    u = jnp.einsum('bnjhd,bnjhv->bnhdv', k_end, vc)
    decay = jnp.exp(b_last[:, :, 0])

    def step(s, inp):
        d_n, u_n = inp
        return d_n[..., None] * s + u_n, s

    _, s_prev = lax.scan(step, s0.astype(f32), (jnp.moveaxis(decay, 1, 0), jnp.moveaxis(u, 1, 0)))
    o_inter = jnp.einsum('bnihd,nbhdv->bnihv', q_dec, s_prev)
    return (o_intra + o_inter).reshape(B, L, H, dv)


def gla_final_state(k, v, log_a):
    b = jnp.cumsum(log_a.astype(jnp.float32), axis=1)
    w = jnp.exp(b[:, -1:] - b)
    return jnp.einsum('blhd,blhv->bhdv', k.astype(jnp.float32) * w, v.astype(jnp.float32))


def gla_bidir(q, k, v, la_f, la_b, s0_f, s0_b):
    o_f = gla_chunked(q, k, v, la_f, s0_f)
    o_b = flip(gla_chunked(flip(q), flip(k), flip(v), flip(la_b), s0_b))
    return o_f + o_b


def neighborhood_attention(q, k, v, k_ctx, v_ctx, rpb):
    B, L, H, Dh = q.shape
    rows = L // GRID_W
    kh = min(NA_WIN_ROWS, rows)
    kw = NA_WIN_COLS
    r = jnp.arange(rows)
    row_start = jnp.clip(r - kh // 2, 0, rows - kh)
    key_rows = row_start[:, None] + jnp.arange(kh)[None, :]
    cidx = jnp.arange(GRID_W)
    col_start = jnp.clip(cidx - kw // 2, 0, GRID_W - kw)
    in_win = (cidx[None, :] >= col_start[:, None]) & (cidx[None, :] < col_start[:, None] + kw)
    mask = jnp.broadcast_to(in_win[:, None, :], (GRID_W, kh, GRID_W)).reshape(GRID_W, kh * GRID_W)
    dr = key_rows - r[:, None]
    dc = jnp.clip(cidx[None, :] - cidx[:, None], -(kw - 1), kw - 1)
    bias = rpb.astype(jnp.float32)[:, dr[:, None, :, None] + NA_WIN_ROWS - 1,
                                   dc[None, :, None, :] + kw - 1]
    bias = bias.reshape(H, rows, GRID_W, kh * GRID_W)

    qg = q.reshape(B, rows, GRID_W, H, Dh)
    kb = k.reshape(B, rows, GRID_W, H, Dh)[:, key_rows].reshape(B, rows, kh * GRID_W, H, Dh)
    vb = v.reshape(B, rows, GRID_W, H, Dh)[:, key_rows].reshape(B, rows, kh * GRID_W, H, Dh)
    scale = NA_HEAD_DIM ** -0.5
    s_loc = jnp.einsum('brqhd,brkhd->bhrqk', qg, kb).astype(jnp.float32) * scale + bias
    s_loc = jnp.where(mask, s_loc, NEG_INF)
    s_ctx = jnp.einsum('brqhd,bkhd->bhrqk', qg, k_ctx).astype(jnp.float32) * scale
    p = jax.nn.softmax(jnp.concatenate([s_loc, s_ctx], axis=-1), axis=-1).astype(v.dtype)
    n_loc = kh * GRID_W
    o = (jnp.einsum('bhrqk,brkhd->brqhd', p[..., :n_loc], vb)
         + jnp.einsum('bhrqk,bkhd->brqhd', p[..., n_loc:], v_ctx))
    return o.reshape(B, L, H * Dh)


def context_attention(q, k, v):
    s = jnp.einsum('bqhd,bkhd->bhqk', q, k).astype(jnp.float32) * NA_HEAD_DIM ** -0.5
    p = jax.nn.softmax(s, axis=-1).astype(v.dtype)
    o = jnp.einsum('bhqk,bkhd->bqhd', p, v)
    return o.reshape(*o.shape[:2], -1)


def merge_branches(o_na, o_gla, og, m_a, m_b, gla_norm_g, w_br_na, w_br_gla, w_o):
    o_gla = rmsnorm(o_gla, gla_norm_g)
    o_gla = o_gla.reshape(*o_gla.shape[:2], -1).astype(og.dtype) * jax.nn.silu(og)
    y = jax.nn.sigmoid(m_a) * (o_na @ w_br_na) + jax.nn.sigmoid(m_b) * (o_gla @ w_br_gla)
    return y @ w_o


def swiglu(h, w_gate, w_up, w_down):
    return (jax.nn.silu(h @ w_gate) * (h @ w_up)) @ w_down


def setup_inputs(seed: int = 0) -> dict:
    key = jax.random.key(seed)
    ks = jax.random.split(key, 21)

    def nrm(k, shape, s):
        return jax.random.normal(k, shape, jnp.float32) * s

    L = DEPTH
    return {
        "x": nrm(ks[0], (BATCH, SEQ, D_MODEL), 1.0),
        "c": nrm(ks[1], (BATCH, D_MODEL), 1.0),
        "ctx": nrm(ks[2], (BATCH, CTX_LEN, D_MODEL), 1.0),
        "c_ctx": nrm(ks[3], (D_MODEL,), 1.0),
        "w_mod": nrm(ks[4], (L, D_MODEL, N_MOD * D_MODEL), 0.5 * D_MODEL ** -0.5),
        "b_mod": nrm(ks[5], (L, N_MOD * D_MODEL), 0.02),
        "norm1_g": 1.0 + nrm(ks[6], (L, D_MODEL), 0.05),
        "norm2_g": 1.0 + nrm(ks[7], (L, D_MODEL), 0.05),
        "w_in": nrm(ks[8], (L, D_MODEL, D_IN), D_MODEL ** -0.5),
        "na_q_norm_g": 1.0 + nrm(ks[9], (L, NA_HEAD_DIM), 0.05),
        "na_k_norm_g": 1.0 + nrm(ks[10], (L, NA_HEAD_DIM), 0.05),
        "na_rpb": nrm(ks[11], (L, NA_HEADS, 2 * NA_WIN_ROWS - 1, 2 * NA_WIN_COLS - 1), 0.5),
        "gla_w_alpha": nrm(ks[12], (L, 2, GLA_GATE_RANK, GLA_QK_WIDTH), GLA_GATE_RANK ** -0.5),
        "gla_b_alpha": 1.0 + nrm(ks[13], (L, 2, GLA_QK_WIDTH), 0.5),
        "gla_norm_g": 1.0 + nrm(ks[14], (L, GLA_DV), 0.05),
        "w_branch_na": nrm(ks[15], (L, NA_WIDTH, D_MODEL), NA_WIDTH ** -0.5),
        "w_branch_gla": nrm(ks[16], (L, GLA_V_WIDTH, D_MODEL), GLA_V_WIDTH ** -0.5),
        "w_out": nrm(ks[17], (L, D_MODEL, D_MODEL), D_MODEL ** -0.5),
        "w_ffn_gate": nrm(ks[18], (L, D_MODEL, D_FF), D_MODEL ** -0.5),
        "w_ffn_up": nrm(ks[19], (L, D_MODEL, D_FF), D_MODEL ** -0.5),
        "w_ffn_down": nrm(ks[20], (L, D_FF, D_MODEL), D_FF ** -0.5),
    }


def reference(x, c, ctx, c_ctx, w_mod, b_mod, norm1_g, norm2_g, w_in, na_q_norm_g, na_k_norm_g, na_rpb,
              gla_w_alpha, gla_b_alpha, gla_norm_g, w_branch_na, w_branch_gla, w_out,
              w_ffn_gate, w_ffn_up, w_ffn_down):
    B, L, _ = x.shape
    t = jnp.arange(L)
    row_pos, col_pos = t // GRID_W, t % GRID_W
    for layer in range(DEPTH):
        mod = jax.nn.silu(c) @ w_mod[layer] + b_mod[layer]
        mod_c = jax.nn.silu(c_ctx) @ w_mod[layer] + b_mod[layer]
        sh1, sc1, g1, sh2, sc2, g2 = jnp.split(mod[:, None, :], N_MOD, axis=-1)
        sh1c, sc1c, g1c, sh2c, sc2c, g2c = jnp.split(mod_c, N_MOD, axis=-1)
        (wq_na, wk_na, wv_na, wq_g, wk_g, wv_g, w_og, w_lr, w_ma, w_mb) = jnp.split(w_in[layer], IN_OFFSETS, axis=1)

        h = modulate(rmsnorm(x, norm1_g[layer]), sh1, sc1)
        hc = modulate(rmsnorm(ctx, norm1_g[layer]), sh1c, sc1c)

        k_na_c = rmsnorm(heads(hc @ wk_na, NA_HEADS), na_k_norm_g[layer])
        v_na_c = heads(hc @ wv_na, NA_HEADS)
        k_g_c = heads(hc @ wk_g, GLA_HEADS)
        v_g_c = heads(hc @ wv_g, GLA_HEADS)
        la_f_c, la_b_c = log_decay(hc @ w_lr, gla_w_alpha[layer], gla_b_alpha[layer])
        s_ctx_f = gla_final_state(k_g_c, v_g_c, la_f_c)
        s_ctx_b = gla_final_state(flip(k_g_c), flip(v_g_c), flip(la_b_c))

        (q_na, k_na, v_na, q_g, k_g, v_g, og, z_lr, m_a, m_b) = jnp.split(h @ w_in[layer], IN_OFFSETS, axis=-1)
        q_na = rmsnorm(heads(q_na, NA_HEADS), na_q_norm_g[layer])
        k_na = rmsnorm(heads(k_na, NA_HEADS), na_k_norm_g[layer])
        o_na = neighborhood_attention(q_na, k_na, heads(v_na, NA_HEADS), k_na_c, v_na_c, na_rpb[layer])
        q_g = axial_rope(heads(q_g, GLA_HEADS), row_pos, col_pos) * GLA_DK ** -0.5
        k_g = axial_rope(heads(k_g, GLA_HEADS), row_pos, col_pos)
        la_f, la_b = log_decay(z_lr, gla_w_alpha[layer], gla_b_alpha[layer])
        o_g = gla_bidir(q_g, k_g, heads(v_g, GLA_HEADS), la_f, la_b, s_ctx_f, s_ctx_b)
        mix = merge_branches(o_na, o_g, og, m_a, m_b, gla_norm_g[layer],
                             w_branch_na[layer], w_branch_gla[layer], w_out[layer])

        if layer + 1 < DEPTH:
            q_na_c = rmsnorm(heads(hc @ wq_na, NA_HEADS), na_q_norm_g[layer])
            o_na_c = context_attention(q_na_c, k_na_c, v_na_c)
            s_zero = jnp.zeros((B, GLA_HEADS, GLA_DK, GLA_DV), jnp.float32)
            q_g_c = heads(hc @ wq_g, GLA_HEADS) * GLA_DK ** -0.5
            o_g_c = gla_bidir(q_g_c, k_g_c, v_g_c, la_f_c, la_b_c, s_zero, s_zero)
            mix_c = merge_branches(o_na_c, o_g_c, hc @ w_og, hc @ w_ma, hc @ w_mb, gla_norm_g[layer],
                                   w_branch_na[layer], w_branch_gla[layer], w_out[layer])
            ctx = ctx + g1c * mix_c
            hc2 = modulate(rmsnorm(ctx, norm2_g[layer]), sh2c, sc2c)
            ctx = ctx + g2c * swiglu(hc2, w_ffn_gate[layer], w_ffn_up[layer], w_ffn_down[layer])

        x = x + g1 * mix
        h2 = modulate(rmsnorm(x, norm2_g[layer]), sh2, sc2)
        x = x + g2 * swiglu(h2, w_ffn_gate[layer], w_ffn_up[layer], w_ffn_down[layer])
    return x
```

```python
import numpy as np
from contextlib import ExitStack
import concourse.bass as bass
import concourse.mybir as mybir
from concourse.bass_utils import run_bass_kernel_spmd

F32 = mybir.dt.float32
BF16 = mybir.dt.bfloat16
AF = mybir.ActivationFunctionType
ALU = mybir.AluOpType

D = 1024
SEQ = 8192
NCORE = 8
TOK = 2048
SA = 512
STG_GAP = 80
NSEG = TOK // SA
CTX = 256
DFF = 2816
NF = DFF // 128
EPS = 1e-6
NEG = -1e30


class Dep:
    __slots__ = ("name", "w", "rs", "sem", "semcnt")

    def __init__(self, name=""):
        self.name = name
        self.w = None
        self.rs = []
        self.sem = None
        self.semcnt = 0


class Op:
    __slots__ = ("eng", "fn", "deps", "dma", "signal", "ev", "idx", "inc")

    def __init__(self, eng, fn, dma, inc):
        self.eng = eng
        self.fn = fn
        self.deps = []
        self.dma = dma
        self.inc = inc
        self.signal = False
        self.ev = None


ENGS = ("pe", "act", "dve", "pool", "sp")


class Sched:
    def __init__(self, nc, same_engine_sync=True):
        self.nc = nc
        self.ops = []
        self.same_engine_sync = same_engine_sync
        self.last = {e: None for e in ENGS}
        self.pending_dma = []
        self.bar_evs = []

    def add(self, eng, fn, reads=(), writes=(), dma=None, ndma=1, inc=16, after_barrier=False, nobar=False):
        deps = []
        if after_barrier:
            deps.extend(getattr(self, "bar_evs", []))
        for r in reads:
            if r.w is not None:
                deps.append(r.w)
        for w in writes:
            if w.w is not None:
                deps.append(w.w)
            deps.extend(w.rs)
        op = Op(eng, fn, dma, ndma * inc)
        seen = set()
        for d in deps:
            if id(d) in seen:
                continue
            seen.add(id(d))
            if d.dma is None and d.eng == eng and (eng == "pe" or not self.same_engine_sync):
                continue
            op.deps.append(d)
        for r in reads:
            r.rs.append(op)
        for w in writes:
            w.w = op
            w.rs = []
        self.ops.append(op)
        if dma is None:
            self.last[eng] = op
        elif not nobar:
            self.pending_dma.append(op)
        return op

    def barrier(self):
        evs = [o for o in self.last.values() if o is not None] + list(self.pending_dma)
        self.pending_dma = []
        self.bar_evs = evs
        for e in ("pe", "act", "dve"):
            op = Op(e, None, None, 0)
            op.deps = [d for d in evs if not (d.dma is None and d.eng == e and e == "pe")]
            self.ops.append(op)

    def emit(self, stack):
        nc = self.nc
        for op in self.ops:
            for d in op.deps:
                d.signal = True
        esem = {}
        for e in ("pe", "act", "dve", "pool"):
            esem[e] = stack.enter_context(nc.semaphore("sem_" + e))
        ecnt = {e: 0 for e in esem}
        nsem = 0
        for op in self.ops:
            if op.dma is not None:
                d = op.dma
                if d.sem is None:
                    d.sem = stack.enter_context(nc.semaphore("dsem%d" % nsem))
                    nsem += 1
                d.semcnt += op.inc
                op.ev = (d.sem, d.semcnt)
            elif op.signal and op.fn is not None:
                ecnt[op.eng] += 1
                op.ev = (esem[op.eng], ecnt[op.eng])
        self.nsem = nsem
        per = {e: [o for o in self.ops if o.eng == e] for e in ENGS}
        block = stack.enter_context(nc.Block())

        def run(engobj, ops):
            waited = {}
            for op in ops:
                for d in op.deps:
                    if d.ev is None:
                        continue
                    sem, val = d.ev
                    k = id(sem)
                    if waited.get(k, 0) >= val:
                        continue
                    waited[k] = val
                    engobj.wait_ge(sem, val)
                if op.fn is None:
                    continue
                if op.dma is not None:
                    op.fn(engobj, op.ev[0])
                else:
                    inst = op.fn(engobj)
                    if op.signal:
                        inst.then_inc(op.ev[0], 1)

        @block.tensor
        def _(e):
            run(e, per["pe"])

        @block.scalar
        def _(e):
            run(e, per["act"])

        @block.vector
        def _(e):
            run(e, per["dve"])

        @block.gpsimd
        def _(e):
            run(e, per["pool"])

        @block.sync
        def _(e):
            run(e, per["sp"])


class T:
    def __init__(self, h, F):
        self.h = h
        self.F = F

    def v(self, c0, dims, p0=0, n=128):
        return bass.AP(self.h, p0 * self.F + c0, [[self.F, n]] + [list(d) for d in dims])

    def s(self, c0, c1, p0=0, n=128):
        return bass.AP(self.h, p0 * self.F + c0, [[self.F, n], [1, c1 - c0]])


def _gmap(Rs, b):
    g = Rs - 4 + b
    if g < 0:
        return {-4: 5, -3: 6, -2: 7, -1: 7}[g]
    if g > 127:
        return 120 if g == 128 else 121
    return g


def _bias_table(rpb, Rs, i):
    tab = np.full((128, 8, 5, 128), NEG, np.float32)
    cq = np.arange(64)
    ck = np.arange(64)
    col_start = np.clip(cq - 8, 0, 48)
    in_win = (ck[None, :] >= col_start[:, None]) & (ck[None, :] < col_start[:, None] + 16)
    dc = np.clip(ck[None, :] - cq[:, None], -15, 15) + 15
    seen = set()
    for kb in range(9):
        b = 2 * i + kb
        g = _gmap(Rs, b)
        first = g not in seen
        seen.add(g)
        t, ph = divmod(kb * 64, 128)
        for ql in range(2):
            rq = Rs + 2 * i + ql
            rs = min(max(rq - 4, 0), 120)
            if not (first and rs <= g < rs + 8):
                continue
            dr = g - rq + 7
            vals = rpb[:, dr, :][:, dc]
            vals = np.where(in_win[None], vals, NEG)
            tab[ph:ph + 64, :, t, ql * 64:(ql + 1) * 64] = np.transpose(vals, (2, 0, 1))
    return tab.reshape(128, 8 * 5 * 128)


def _fm(v):
    return np.ascontiguousarray(v.reshape(-1, 128).T)


def prep_inputs(x, c, ctx, c_ctx, w_mod, b_mod, norm1_g, norm2_g, w_in, na_q_norm_g, na_k_norm_g, na_rpb,
                gla_w_alpha, gla_b_alpha, gla_norm_g, w_branch_na, w_branch_gla, w_out,
                w_ffn_gate, w_ffn_up, w_ffn_down):
    f = lambda a: np.ascontiguousarray(np.asarray(a, dtype=np.float32))
    x, c, ctx, c_ctx = f(x), f(c), f(ctx), f(c_ctx)
    w_in0 = f(w_in)[0]
    offs = np.cumsum([0, 512, 512, 512, 256, 256, 512, 512, 32, 1024, 1024])
    sl = lambda k: w_in0[:, offs[k]:offs[k + 1]]
    wq_na, wk_na, wv_na, wq_g, wk_g, wv_g, w_og, w_lr, w_ma, w_mb = [sl(k) for k in range(10)]
    d = np.arange(256)
    dd = d % 64
    swp = (d // 64) * 64 + np.where(dd % 32 < 16, dd + 16, dd - 16)
    wg1 = np.ascontiguousarray(np.concatenate([wq_g, wq_g[:, swp], wk_g, wk_g[:, swp]], axis=1))
    wal = f(gla_w_alpha)[0]
    walpha = np.zeros((32, 512), np.float32)
    walpha[0:16, 0:256] = wal[0]
    walpha[16:32, 256:512] = wal[1]
    bal = f(gla_b_alpha)[0]
    balpha = np.ascontiguousarray(np.stack([bal[0, 0:128], bal[0, 128:256], bal[1, 0:128], bal[1, 128:256]], axis=1))
    rpb = f(na_rpb)[0]
    gq = np.tile(f(na_q_norm_g)[0], 2).reshape(128, 1)
    gk = np.tile(f(na_k_norm_g)[0], 2).reshape(128, 1)
    gn = f(gla_norm_g)[0].reshape(128, 1)
    pvec = np.ascontiguousarray(np.concatenate([gq, gk, gn], axis=1))
    ident = np.eye(128, dtype=np.float32)
    blk = np.zeros((128, 128), np.float32)
    blk[:64, :64] = 1
    blk[64:, 64:] = 1
    j = np.arange(128)[:, None]
    i = np.arange(128)[None, :]
    same = (j // 64) == (i // 64)
    mf = (same & (j <= i)).astype(np.float32)
    mb = (same & (j >= i)).astype(np.float32)
    consts = np.ascontiguousarray(np.concatenate([ident, blk, mf, mb], axis=1))
    freqs = (10000.0 ** (-np.arange(16, dtype=np.float32) / 16)).astype(np.float32)
    tab_int = _bias_table(rpb, 40, 0)
    shared = {
        "w_mod": f(w_mod)[0], "b_modT": _fm(f(b_mod)[0]),
        "n1g": _fm(f(norm1_g)[0]), "n2g": _fm(f(norm2_g)[0]),
        "wq_na": np.ascontiguousarray(wq_na), "wk_na": np.ascontiguousarray(wk_na),
        "wv_na": np.ascontiguousarray(wv_na), "wg1": wg1, "wv_g": np.ascontiguousarray(wv_g),
        "w_og": np.ascontiguousarray(w_og), "w_lr": np.ascontiguousarray(w_lr),
        "w_ma": np.ascontiguousarray(w_ma), "w_mb": np.ascontiguousarray(w_mb),
        "walpha": walpha, "balpha": balpha, "pvec": pvec,
        "w_br_na": f(w_branch_na)[0], "w_br_gla": f(w_branch_gla)[0], "w_out": f(w_out)[0],
        "w_gate": f(w_ffn_gate)[0], "w_up": f(w_ffn_up)[0], "w_down": f(w_ffn_down)[0],
        "consts": consts,
    }
    maps = []
    for core in range(NCORE):
        bb, r = divmod(core, 4)
        R0 = 32 * r
        xa = x[bb, R0 * 64:(R0 + 32) * 64]
        rows = []
        for s in range(NSEG):
            Rs = R0 + 8 * s
            for b in range(16):
                g = _gmap(Rs, b)
                rows.append(x[bb, g * 64:(g + 1) * 64])
        xh = np.ascontiguousarray(np.concatenate(rows, axis=0))
        cvec = np.ascontiguousarray(np.concatenate([_fm(c[bb]), _fm(c_ctx)], axis=1))
        tabs = [tab_int,
                _bias_table(rpb, R0, 0) if r == 0 else tab_int,
                _bias_table(rpb, R0, 1) if r == 0 else tab_int,
                _bias_table(rpb, R0 + 24, 2) if r == 3 else tab_int,
                _bias_table(rpb, R0 + 24, 3) if r == 3 else tab_int]
        btab = np.ascontiguousarray(np.stack(tabs, axis=0))
        t = np.arange(TOK)
        rowp = (R0 + t // 64).astype(np.float32)
        colp = (t % 64).astype(np.float32)
        dd = np.arange(128) % 64
        fr = freqs[dd % 16][:, None]
        pos = np.where((dd < 32)[:, None], rowp[None, :], colp[None, :]).astype(np.float32)
        ang = (pos * fr).astype(np.float32)
        ropeC = np.ones((128, TOK + CTX), np.float32)
        ropeS = np.zeros((128, TOK + CTX), np.float32)
        ropeC[:, :TOK] = np.cos(ang)
        sgn = np.where((dd % 32) < 16, -1.0, 1.0).astype(np.float32)[:, None]
        ropeS[:, :TOK] = np.sin(ang) * sgn
        cmask = np.zeros((128, 8), np.float32)
        for p in range(4):
            cmask[:, p] = 1.0 if p < r else 0.0
            cmask[:, 4 + p] = 1.0 if p > r else 0.0
        m = dict(shared)
        m.update({"xa": np.ascontiguousarray(xa), "xh": xh, "xc": np.ascontiguousarray(ctx[bb]), "cvec": cvec,
                  "btab": btab, "ropeC": ropeC, "ropeS": ropeS, "cmask": cmask})
        maps.append(m)
    return maps


IN_SHAPES = {
    "xa": [TOK, D], "xh": [NSEG * 1024, D], "xc": [CTX, D], "cvec": [128, 16],
    "w_mod": [D, 6 * D], "b_modT": [128, 48], "n1g": [128, 8], "n2g": [128, 8],
    "wq_na": [D, 512], "wk_na": [D, 512], "wv_na": [D, 512], "wg1": [D, 1024], "wv_g": [D, 512],
    "w_og": [D, 512], "w_lr": [D, 32], "w_ma": [D, D], "w_mb": [D, D],
    "walpha": [32, 512], "balpha": [128, 4], "pvec": [128, 3],
    "w_br_na": [512, D], "w_br_gla": [512, D], "w_out": [D, D],
    "w_gate": [D, DFF], "w_up": [D, DFF], "w_down": [DFF, D],
    "consts": [128, 512], "btab": [5, 128, 5120], "ropeC": [128, TOK + CTX], "ropeS": [128, TOK + CTX],
    "cmask": [128, 8],
}


def build(stop=None, taps=None):
    nc = bass.Bass("TRN2", target_bir_lowering=False)
    I = {k: nc.dram_tensor(k, shp, F32, kind="ExternalInput").ap() for k, shp in IN_SHAPES.items()}
    out = nc.dram_tensor("out", [TOK, D], F32, kind="ExternalOutput").ap()
    cc_in = nc.dram_tensor("cc_in", [128, 520], F32)
    cc_out = nc.dram_tensor("cc_out", [512, 520], F32)
    taps = taps if taps is not None else {}
    tapd = []
    st = ExitStack()
    with st:
        S = Sched(nc)

        uniq = [0]

        def sb(name, F, dt, stack=None):
            uniq[0] += 1
            h = (stack or st).enter_context(nc.sbuf_tensor("s%d_%s" % (uniq[0], name), [128, F], dt))
            return T(h, F)

        def DMA(eng, o, i, reads, writes, dep, n=1, ab=False, nobar=False):
            S.add(eng, lambda e, s: e.dma_start(out=o, in_=i).then_inc(s, 16), reads=reads, writes=writes, dma=dep,
                  after_barrier=ab, nobar=nobar)

        def PE(mms, reads, writes):
            def fn(e):
                r = None
                for m in mms:
                    if m[0] == "T":
                        r = e.transpose(m[1], m[2], m[3])
                    else:
                        r = e.matmul(m[0], lhsT=m[1], rhs=m[2], start=m[3], stop=m[4])
                return r
            S.add("pe", fn, reads, writes)

        def ACT(o, i, func, reads, writes, scale=None, bias=None, accum=None):
            kw = {}
            if scale is not None:
                kw["scale"] = scale
            if bias is not None:
                kw["bias"] = bias
            if accum is not None:
                kw["accum_out"] = accum
            S.add("act", lambda e: e.activation(out=o, in_=i, func=func, **kw), reads, writes)

        def TT(o, a, b, op, reads, writes, eng="dve"):
            S.add(eng, lambda e: e.tensor_tensor(out=o, in0=a, in1=b, op=op), reads, writes)

        def TS(o, a, s1, op0, reads, writes, s2=None, op1=None, eng="dve"):
            if op1 is None:
                S.add(eng, lambda e: e.tensor_scalar(out=o, in0=a, scalar1=s1, scalar2=None, op0=op0), reads, writes)
            else:
                S.add(eng, lambda e: e.tensor_scalar(out=o, in0=a, scalar1=s1, scalar2=s2, op0=op0, op1=op1), reads, writes)

        def STT(o, a, sc, b, op0, op1, reads, writes):
            S.add("dve", lambda e: e.scalar_tensor_tensor(out=o, in0=a, scalar=sc, in1=b, op0=op0, op1=op1), reads, writes)

        def CP(o, i, reads, writes, eng="dve"):
            if eng == "act":
                S.add("act", lambda e: e.copy(out=o, in_=i), reads, writes)
            else:
                S.add(eng, lambda e: e.tensor_copy(out=o, in_=i), reads, writes)

        def MEMSET(o, val, writes, eng="dve"):
            S.add(eng, lambda e: e.memset(o, val), (), writes)

        def tap(name, t, F, dt, deps):
            if name not in taps:
                return
            d = nc.dram_tensor("tap_" + name, [128, F], dt, kind="ExternalOutput").ap()
            dep = Dep("tap_" + name)
            DMA("sp", d, t, deps, [dep], dep)
            tapd.append(dep)

        P = [T(st.enter_context(nc.psum_tensor("ps%d" % i, [128, 512], F32)), 512) for i in range(7)]
        DP = [Dep("ps%d" % i) for i in range(7)]
        PT = T(st.enter_context(nc.psum_tensor("pst", [128, 1024], BF16)), 1024)
        DPT = Dep("pst")

        cst = sb("cst", 512, F32)
        cstb = sb("cstb", 512, BF16)
        Dcst, Dcstb = Dep("cst"), Dep("cstb")
        DMA("sp", cst.s(0, 512), I["consts"], [], [Dcst], Dcst)
        DMA("pool", cstb.s(0, 512), I["consts"], [], [Dcstb], Dcstb)
        identb = cstb.s(0, 128)
        blk32 = cst.s(128, 256)
        blkb = cstb.s(128, 256)
        ident32 = cst.s(0, 128)
        mfb, mbb = cstb.s(256, 384), cstb.s(384, 512)
        kc = sb("kc", 8, F32)
        Dkc = Dep("kc")
        MEMSET(kc.s(0, 1), EPS, [Dkc])
        MEMSET(kc.s(1, 2), 1.0, [Dkc])
        MEMSET(kc.s(2, 3), 0.0, [Dkc])
        ones32 = sb("ones32", 128, F32)
        Dones = Dep("ones")
        MEMSET(ones32.s(0, 128), 1.0, [Dones])
        onesb = sb("onesb", 128, BF16)
        MEMSET(onesb.s(0, 128), 1.0, [Dones])
        onesw = sb("onesw", 512, F32)
        MEMSET(onesw.s(0, 512), 1.0, [Dones])

        prm = sb("prm", 128, F32)
        Dprm = Dep("prm")
        for (c0, key, n) in ((0, "cvec", 16), (16, "b_modT", 48), (64, "n1g", 8), (72, "n2g", 8), (80, "balpha", 4),
                             (84, "pvec", 3), (88, "cmask", 8)):
            DMA("sp", prm.s(c0, c0 + n), I[key], [], [Dprm], Dprm)
        walb = sb("walb", 512, BF16)
        Dwal = Dep("walb")
        DMA("pool", walb.s(0, 512, n=32), I["walpha"], [], [Dwal], Dwal)
        gq, gk, gn = prm.s(84, 85), prm.s(85, 86), prm.s(86, 87)

        NWS = 3
        WS = [sb("ws%d" % i, 4096, BF16) for i in range(NWS)]
        WD = [Dep("ws%d" % i) for i in range(NWS)]
        wctr = [0]

        STG_SPEC = [("w_modl", "w_mod", 1024, 2048, 4096, 128)]
        STG = {}
        stg_chunks = []
        for (sname, src_name, rows, c0_, ncols, rpc) in STG_SPEC:
            t_ = nc.dram_tensor("stg_" + sname, [rows, ncols], BF16).ap()
            deps_ = []
            for r0_ in range(0, rows, rpc):
                r1_ = min(rows, r0_ + rpc)
                d_ = Dep("stg_%s_%d" % (sname, r0_))
                deps_.append((r0_, r1_, d_))
                stg_chunks.append((t_[r0_:r1_, :], I[src_name][r0_:r1_, c0_:c0_ + ncols], d_))
            STG[sname] = (t_, deps_)
        use_stg = [False]

        def stager():
            for (o_, i_, d_) in stg_chunks:
                DMA("pool", o_, i_, [], [d_], d_, nobar=True)
                for _ in range(STG_GAP):
                    yield

        def loadw(name, r0, kt, c0, n, eng="pool"):
            i = wctr[0] % len(WS)
            wctr[0] += 1
            reads = []
            if use_stg[0] and name in STG:
                t_, deps_ = STG[name]
                src = t_[r0:r0 + 128 * kt, c0:c0 + n].rearrange("(k p) n -> p k n", p=128)
                reads = [d_ for (a_, b_, d_) in deps_ if a_ < r0 + 128 * kt and b_ > r0]
            else:
                src = I[name][r0:r0 + 128 * kt, c0:c0 + n].rearrange("(k p) n -> p k n", p=128)
            DMA(eng, WS[i].v(0, [[n, kt], [1, n]]), src, reads, [WD[i]], WD[i])
            return WS[i], WD[i], n

        scb = sb("scb", 16, BF16)
        Dscb = Dep("scb")
        ACT(scb.v(0, [[2, 8]]), prm.s(0, 8), AF.Silu, [Dprm], [Dscb])
        ACT(scb.v(1, [[2, 8]]), prm.s(8, 16), AF.Silu, [Dprm], [Dscb])
        modv = sb("modv", 96, F32)
        Dmod = Dep("modv")
        mv = sb("mv", 64, F32)
        Dmv = Dep("mv")
        Dmv2 = Dep("mv2")
        mcol = lambda ch0, j: modv.v(2 * ch0 + j, [[2, 8]])
        gbc = sb("gbc", 2048, F32)
        Dgbc = Dep("gbc")
        dg = sb("dg", 128, F32)
        Ddg = Dep("dg")

        def mod_groups(g0, g1_, dmod):
            for g in range(g0, g1_):
                if use_stg[0] and g >= 4:
                    ws, wd, n = loadw("w_modl", 0, 8, (g - 4) * 512, 512, eng="sp")
                else:
                    ws, wd, n = loadw("w_mod", 0, 8, g * 512, 512)
                mms = []
                for jj in range(4):
                    ch = g * 4 + jj
                    for kt in range(8):
                        mms.append((P[0].s(2 * ch, 2 * ch + 2), ws.v(kt * 512 + jj * 128, [[1, 128]]), scb.s(2 * kt, 2 * kt + 2),
                                    kt == 0, kt == 7))
                PE(mms, [wd, Dscb], [DP[0]])
            c0, c1 = g0 * 4, g1_ * 4
            TT(modv.v(2 * c0, [[2, c1 - c0], [1, 2]]), P[0].v(2 * c0, [[2, c1 - c0], [1, 2]]), prm.v(16 + c0, [[1, c1 - c0], [0, 2]]),
               ALU.add, [DP[0], Dprm], [dmod])

        def mod_early():
            mod_groups(0, 4, Dmod)
            STT(mv.s(0, 8), mcol(8, 0), 1.0, prm.s(64, 72), ALU.add, ALU.mult, [Dmod, Dprm], [Dmv])
            CP(mv.s(8, 16), mcol(0, 0), [Dmod], [Dmv])
            STT(mv.s(16, 24), mcol(8, 1), 1.0, prm.s(64, 72), ALU.add, ALU.mult, [Dmod, Dprm], [Dmv])
            CP(mv.s(24, 32), mcol(0, 1), [Dmod], [Dmv])

        def mod_late():
            Dm2 = Dep("modv2")
            mod_groups(4, 12, Dm2)
            STT(mv.s(32, 40), mcol(32, 0), 1.0, prm.s(72, 80), ALU.add, ALU.mult, [Dm2, Dprm], [Dmv2])
            CP(mv.s(40, 48), mcol(24, 0), [Dm2], [Dmv2])
            CP(mv.s(48, 56), mcol(16, 0), [Dm2], [Dmv2])
            CP(mv.s(56, 64), mcol(40, 0), [Dm2], [Dmv2])
            for gi in range(2):
                dgt, Ddgt = xt[gi], Dxt[gi]
                TT(dgt.v(0, [[128, 8], [1, 128]]), cst.v(0, [[0, 8], [1, 128]]), mv.v(48 + 8 * gi, [[1, 8], [0, 128]]), ALU.mult,
                   [Dcst, Dmv2], [Ddgt])
                for hf in range(2):
                    b_ = 1 + hf
                    PE([(P[b_].s(0, 512), ones32.s(0, 128), dgt.s(hf * 512, hf * 512 + 512), True, True)], [Ddgt, Dones], [DP[b_]])
                    CP(gbc.s(gi * 1024 + hf * 512, gi * 1024 + hf * 512 + 512), P[b_].s(0, 512), [DP[b_]], [Dgbc],
                       eng="act" if hf == 0 else "dve")
            tap("mv", mv.s(0, 64), 64, F32, [Dmv, Dmv2])
            tap("gbc", gbc.s(0, 2048), 2048, F32, [Dgbc])

        mod_early()
        negb = sb("negb", 4, F32)
        Dnegb = Dep("negb")
        TS(negb.s(0, 4), prm.s(80, 84), -1.0, ALU.mult, [Dprm], [Dnegb])

        xt = [sb("xt%d" % i, 1024, F32) for i in range(2)]
        Dxt = [Dep("xt%d" % i) for i in range(2)]
        xs = [sb("xs%d" % i, 1024, BF16) for i in range(2)]
        Dxs = [Dep("xs%d" % i) for i in range(2)]
        nst = [sb("nst%d" % i, 4, F32) for i in range(2)]
        Dnst = [Dep("nst%d" % i) for i in range(2)]

        PTv = [PT, T(P[6].h.bitcast(BF16), 1024)]
        DPTv = [DPT, DP[6]]

        def interleave(*gens):
            gens = list(gens)
            while gens:
                for g_ in list(gens):
                    try:
                        next(g_)
                    except StopIteration:
                        gens.remove(g_)

        import os

        def norm_T_g(x_t, x_dep, gcol, shcol, dst, dst_deps, dcol, slot, dmv):
            xap = x_t.s(0, 1024) if isinstance(x_t, T) else x_t
            i = slot
            ns, dn = nst[i], Dnst[i]
            pt, dpt = PTv[i], DPTv[i]
            ACT(xs[i].s(0, 1024), xap, AF.Square, [x_dep], [dn, Dxs[i]], accum=ns.s(0, 1))
            yield
            ACT(ns.s(1, 2), ns.s(0, 1), AF.Ln, [dn, Dkc], [dn], scale=1.0 / 1024, bias=kc.s(0, 1))
            yield
            ACT(ns.s(2, 3), ns.s(1, 2), AF.Exp, [dn], [dn], scale=-0.5)
            yield
            TS(xs[i].s(0, 1024), xap, ns.s(2, 3), ALU.mult, [x_dep, dn], [Dxs[i]])
            yield
            PE([("T", pt.s(kt * 128, kt * 128 + 128), xs[i].s(kt * 128, kt * 128 + 128), identb) for kt in range(8)],
               [Dxs[i], Dcstb], [dpt])
            yield
            for kt in range(8):
                o = dst.s(kt * dst.F // 8 + dcol, kt * dst.F // 8 + dcol + 128)
                if slot == 0:
                    ACT(o, pt.s(kt * 128, kt * 128 + 128), AF.Identity, [dpt, dmv], [dst_deps[kt]],
                        scale=mv.s(gcol + kt, gcol + kt + 1), bias=mv.s(shcol + kt, shcol + kt + 1))
                else:
                    TS(o, pt.s(kt * 128, kt * 128 + 128), mv.s(gcol + kt, gcol + kt + 1), ALU.mult, [dpt, dmv], [dst_deps[kt]],
                       s2=mv.s(shcol + kt, shcol + kt + 1), op1=ALU.add)
                if kt % 2 == 1:
                    yield

        def norm_tiles(srcs, gcol, shcol, dst, dst_deps, dmv):
            import os
            STEP = 1 if os.environ.get("NOIL") else 2
            for j in range(0, len(srcs), STEP):
                gs = []
                for slot, (xa_, xd_, dc_, ti_) in enumerate(srcs[j:j + STEP]):
                    gs.append(norm_T_g(xa_, xd_, gcol, shcol, dst, dst_deps[ti_ * 8:(ti_ + 1) * 8], dc_, slot, dmv))
                interleave(*gs)

        xctr = [0]

        def load_x(src_rows):
            i = xctr[0] % 2
            xctr[0] += 1
            DMA("sp", xt[i].s(0, 1024), src_rows, [], [Dxt[i]], Dxt[i])
            return xt[i], Dxt[i]

        hTc = sb("hTc", 8 * CTX, BF16)
        DhTc = [Dep("hTc%d" % i_) for i_ in range(16)]
        hT = sb("hT", 8 * 1024, BF16)
        DhT = [Dep("hT%d" % i_) for i_ in range(64)]
        hT2 = sb("hT2", 8 * 1024, BF16)
        DhT2 = [Dep("hT2_%d" % i_) for i_ in range(64)]
        hTb, DhTb = [hT, hT2], [DhT, DhT2]
        o_loc = sb("o_loc", 4 * TOK, BF16)
        Doloc = [Dep("oloc%d" % a) for a in range(NSEG)]
        qsd = sb("qsd", 4 * TOK, BF16)
        Dqsd = [Dep("qsd%d" % a) for a in range(NSEG)]
        NSUM = NSEG + 1
        Sentbf = sb("Sentbf", NSEG * 2 * 256, BF16)
        DSent = Dep("Sentbf")
        xsc = ExitStack()
        Useg = sb("Useg", NSUM * 2 * 256, F32, xsc)
        Aseg = sb("Aseg", NSUM * 2 * 2, F32, xsc)
        Dsum = [Dep("sum%d" % a) for a in range(NSUM)]

        srcs = []
        for tci in range(2):
            x_t, x_d = load_x(I["xc"][tci * 128:(tci + 1) * 128, :])
            srcs.append((x_t, x_d, tci * 128, tci))
        norm_tiles(srcs, 16, 24, hTc, DhTc, Dmv)
        tap("hTc", hTc.s(0, 8 * CTX), 8 * CTX, BF16, DhTc)
        if stop == "mod":
            S.add("sp", None, reads=tapd)
            S.emit(st)
            return nc

        pa = ExitStack()
        ropeC = sb("ropeC", SA, F32, pa)
        ropeS = sb("ropeS", SA, F32, pa)
        DropeC, DropeS = Dep("ropeC"), Dep("ropeS")
        qr = sb("qr", SA, F32, pa)
        kr = sb("kr", SA, F32, pa)
        Dqr, Dkr = Dep("qr"), Dep("kr")
        tmpf = sb("tmpf", SA, F32, pa)
        Dtmpf = Dep("tmpf")
        vg = sb("vg", 2 * 4 * 512, BF16, pa)
        Dvg = [Dep("vg0"), Dep("vg1")]
        zl = sb("zl", SA, BF16, pa)
        Dzl = Dep("zl")
        la = [sb("la%d" % i_, SA, F32, pa) for i_ in range(2)]
        Dla = [Dep("la0"), Dep("la1")]
        pbuf = [sb("pbuf%d" % i_, SA + 1, F32, pa) for i_ in range(2)]
        Dpb = [Dep("pbuf0"), Dep("pbuf1")]
        for i_ in range(2):
            MEMSET(pbuf[i_].s(0, 1), 0.0, [Dpb[i_]])
        bt = [sb("bt%d" % i_, SA, F32, pa) for i_ in range(2)]
        Dbt = [Dep("bt0"), Dep("bt1")]
        E2 = [sb("E2_%d" % i_, SA, BF16, pa) for i_ in range(2)]
        DE2 = [Dep(), Dep()]
        E1, DE1 = la, Dla
        E3, DE3 = bt, Dbt
        qdec = sb("qdec", 2 * 4 * SA, BF16, pa)
        kinv = sb("kinv", 2 * 4 * SA, BF16, pa)
        Dqdec = [[Dep("qdec%d" % i) for i in range(4)] for _ in range(2)]
        Dkinv = [[Dep("kinv%d" % i) for i in range(4)] for _ in range(2)]
        kend = [sb("kend%d" % i_, SA, BF16, pa) for i_ in range(2)]
        Dkend = [Dep("kend0"), Dep("kend1")]
        kendtm = sb("kendtm", 2 * 4 * 512, BF16, pa)
        Dkendtm = [[Dep("kendtm%d" % i_) for i_ in range(4)] for _ in range(2)]
        decs = sb("decs", 2 * 4 * 8, F32, pa)
        Ddecs = [[Dep("decs%d" % i_) for i_ in range(4)] for _ in range(2)]
        dtmp = [sb("dtmp%d" % i_, 8, F32, pa) for i_ in range(2)]
        Ddtmp = [Dep("dtmp0"), Dep("dtmp1")]
        Sst = sb("Sst", 2 * 256, F32, pa)
        DSst = [Dep("Sst0"), Dep("Sst1")]
        Stmp = sb("Stmp", 2 * 256, F32, pa)
        DStmp = [Dep("Stmp0"), Dep("Stmp1")]
        Sbf = sb("Sbf", 2 * 8 * 256, BF16, pa)
        DSbf = Dep("Sbf")
        Abf = [sb("Abf%d" % i, 256, BF16, pa) for i in range(2)]
        DAbf = [Dep("Abf0"), Dep("Abf1")]
        onesA = onesw

        def roundrobin(gens):
            gens = list(gens)
            while gens:
                for g_ in list(gens):
                    try:
                        next(g_)
                        yield
                    except StopIteration:
                        gens.remove(g_)

        def pa_front(N, hsrc, hcol0, hdep, hstride, tok0, sidx, need_q, pp):
            NCH = N // 64
            NT = N // 128
            hk = lambda kt: hsrc.s(kt * hstride + hcol0, kt * hstride + hcol0 + N)
            DMA("sp", ropeC.s(0, N), I["ropeC"][:, tok0:tok0 + N], [], [DropeC], DropeC)
            DMA("sp", ropeS.s(0, N), I["ropeS"][:, tok0:tok0 + N], [], [DropeS], DropeS)
            ws, wd, n = loadw("wv_g", 0, 8, 0, 512)
            for tt in range(NT):
                b = 5 + (tt % 2)
                PE([(P[b].s(0, 512), hsrc.s(kt * hstride + hcol0 + tt * 128, kt * hstride + hcol0 + tt * 128 + 128),
                     ws.s(kt * 512, kt * 512 + 512), kt == 0, kt == 7) for kt in range(8)], [wd] + hdep, [DP[b]])
                yield
                CP(vg.s(pp * 2048 + tt * 512, pp * 2048 + tt * 512 + 512), P[b].s(0, 512), [DP[b]], [Dvg[pp]], eng="act")
                yield
            ws, wd, n = loadw("w_lr", 0, 8, 0, 32)
            PE([(P[5].s(0, N, n=32), ws.v(kt * 32, [[1, 32]]), hk(kt), kt == 0, kt == 7) for kt in range(8)],
               [wd] + hdep, [DP[5]])
            yield
            CP(zl.s(0, N, n=32), P[5].s(0, N, n=32), [DP[5]], [Dzl], eng="act")
            yield
            wsg, wdg, n = loadw("wg1", 0, 8, 0, 512)
            wsk, wdk, n = loadw("wg1", 0, 8, 512, 512)

            def chain(hp, dr):
                dh = dr * 2 + hp
                la_, pb_, bt_, e1, e2, e3, dt_, ke_ = la[dr], pbuf[dr], bt[dr], E1[dr], E2[dr], E3[dr], dtmp[dr], kend[dr]
                zb = 5 + dr
                PE([(P[zb].s(0, N), walb.v(dh * 128, [[1, 128]], n=32), zl.s(0, N, n=32), True, True)],
                   [Dwal, Dzl], [DP[zb]])
                yield
                ACT(la_.s(0, N), P[zb].s(0, N), AF.Exp, [DP[zb], Dnegb], [Dla[dr]], scale=-1.0, bias=negb.s(dh, dh + 1))
                yield
                ACT(la_.s(0, N), la_.s(0, N), AF.Ln, [Dkc], [Dla[dr]], bias=kc.s(1, 2))
                yield
                TS(la_.s(0, N), la_.s(0, N), -1.0 / 16.0, ALU.mult, [], [Dla[dr]], s2=-1.0, op1=ALU.max)
                yield
                S.add("dve", lambda e: e.tensor_tensor_scan(out=pb_.s(1, N + 1), data0=onesA.s(0, N), data1=la_.s(0, N),
                                                            initial=0.0, op0=ALU.mult, op1=ALU.add),
                      [Dla[dr], Dones], [Dpb[dr]])
                yield
                if dr == 0:
                    TT(bt_.v(0, [[64, NCH], [1, 64]]), pb_.v(1, [[64, NCH], [1, 64]]), pb_.v(0, [[64, NCH], [0, 64]]),
                       ALU.subtract, [Dpb[dr]], [Dbt[dr]])
                else:
                    TT(bt_.v(0, [[64, NCH], [1, 64]]), pb_.v(64, [[64, NCH], [0, 64]]), pb_.v(0, [[64, NCH], [1, 64]]),
                       ALU.subtract, [Dpb[dr]], [Dbt[dr]])
                yield
                if hp == 0 and sidx == 0 and dr == 0:
                    tap("bt", bt_.s(0, N), N, F32, [Dbt[dr]])
                ACT(e1.s(0, N), bt_.s(0, N), AF.Exp, [Dbt[dr], Dla[dr]], [DE1[dr]])
                yield
                ACT(e2.s(0, N), bt_.s(0, N), AF.Exp, [Dbt[dr]], [DE2[dr]], scale=-1.0)
                yield
                if dr == 0:
                    ACT(e3.s(0, N), pb_.s(1, N + 1), AF.Exp, [Dpb[dr]], [DE3[dr]])
                else:
                    ACT(e3.s(0, N), pb_.s(0, N), AF.Exp, [Dpb[dr]], [DE3[dr]], scale=-1.0, bias=pb_.s(N, N + 1))
                yield
                if need_q:
                    STT(qdec.s((pp * 4 + dh) * SA, (pp * 4 + dh) * SA + N), qr.s(0, N), 0.125, e1.s(0, N), ALU.mult, ALU.mult, [Dqr, DE1[dr]],
                        [Dqdec[pp][dh]])
                    yield
                    STT(qsd.s(dh * TOK + tok0, dh * TOK + tok0 + N), qr.s(0, N), 0.125, e3.s(0, N), ALU.mult, ALU.mult,
                        [Dqr, DE3[dr]], [Dqsd[sidx]])
                    yield
                TT(kinv.s((pp * 4 + dh) * SA, (pp * 4 + dh) * SA + N), kr.s(0, N), e2.s(0, N), ALU.mult, [Dkr, DE2[dr]], [Dkinv[pp][dh]])
                yield
                TT(dt_.s(0, NCH), pb_.v(64, [[64, NCH]]), pb_.v(0, [[64, NCH]]), ALU.subtract, [Dpb[dr]], [Ddtmp[dr]])
                yield
                ACT(decs.s(pp * 32 + dh * 8, pp * 32 + dh * 8 + NCH), dt_.s(0, NCH), AF.Exp, [Ddtmp[dr]], [Ddecs[pp][dh]])
                yield
                ACT(Aseg.s((sidx * 2 + dr) * 2 + hp, (sidx * 2 + dr) * 2 + hp + 1), pb_.s(N, N + 1), AF.Exp, [Dpb[dr]],
                    [Dsum[sidx]])
                yield
                TT(ke_.v(0, [[64, NCH], [1, 64]]), kinv.v((pp * 4 + dh) * SA, [[64, NCH], [1, 64]]),
                   decs.v(pp * 32 + dh * 8, [[1, NCH], [0, 64]]), ALU.mult, [Dkinv[pp][dh], Ddecs[pp][dh]], [Dkend[dr]])
                yield
                pt, dpt = T(P[zb].h.bitcast(BF16), 1024), DP[zb]
                PE([("T", pt.s(tt * 128, tt * 128 + 128), ke_.s(tt * 128, tt * 128 + 128), identb) for tt in range(NT)],
                   [Dkend[dr], Dcstb], [dpt])
                yield
                CP(kendtm.v(pp * 2048 + dh * 128, [[512, NT], [1, 128]]), pt.v(0, [[128, NT], [1, 128]]), [dpt], [Dkendtm[pp][dh]],
                   eng="act" if dr == 0 else "dve")
                yield

            for hp in range(2):
                Cs = ropeC.s(0, N)
                Ss = ropeS.s(0, N)
                for (wsx, wdx, dst, ddst, bk, need) in ((wsg, wdg, qr, Dqr, 3, need_q), (wsk, wdk, kr, Dkr, 3, True)):
                    if not need:
                        continue
                    PE([(P[bk].s(0, N), wsx.v(kt * 512 + hp * 128, [[1, 128]]), hk(kt), kt == 0, kt == 7) for kt in range(8)],
                       [wdx] + hdep, [DP[bk]])
                    PE([(P[bk + 1].s(0, N), wsx.v(kt * 512 + 256 + hp * 128, [[1, 128]]), hk(kt), kt == 0, kt == 7)
                        for kt in range(8)], [wdx] + hdep, [DP[bk + 1]])
                    yield
                    TT(tmpf.s(0, N), P[bk].s(0, N), Cs, ALU.mult, [DP[bk], DropeC], [Dtmpf])
                    yield
                    TT(dst.s(0, N), P[bk + 1].s(0, N), Ss, ALU.mult, [DP[bk + 1], DropeS], [ddst])
                    yield
                    TT(dst.s(0, N), dst.s(0, N), tmpf.s(0, N), ALU.add, [Dtmpf], [ddst])
                    yield
                if hp == 0 and sidx == 0:
                    tap("qr", qr.s(0, N), N, F32, [Dqr])
                    tap("kr", kr.s(0, N), N, F32, [Dkr])
                for _ in roundrobin([chain(hp, 0), chain(hp, 1)]):
                    yield

        def pa_back(N, tok0, sidx, need_q, pp):
            NCH = N // 64
            NT = N // 128
            for dr in range(2):
                MEMSET(Sst.s(dr * 256, dr * 256 + 256), 0.0, [DSst[dr]])
            for nstep in range(NCH):
                for dr in range(2):
                    cn = nstep if dr == 0 else NCH - 1 - nstep
                    tt, half = divmod(cn, 2)
                    bk = 0
                    mms = []
                    for hp in range(2):
                        for hh in range(2):
                            h = 2 * hp + hh
                            mms.append((P[bk].s(dr * 256 + hp * 128, dr * 256 + hp * 128 + 128, p0=64 * hh, n=64),
                                        kendtm.s(pp * 2048 + tt * 512 + (dr * 2 + hp) * 128 + 64 * hh, pp * 2048 + tt * 512 + (dr * 2 + hp) * 128 + 64 * hh + 64,
                                                 p0=64 * half, n=64),
                                        vg.s(pp * 2048 + tt * 512 + h * 128, pp * 2048 + tt * 512 + h * 128 + 128, p0=64 * half, n=64), True, True))
                    PE(mms, Dkendtm[pp] + [Dvg[pp]], [DP[bk]])
                    yield
                    TT(Stmp.v(dr * 256, [[128, 2], [1, 128]]), Sst.v(dr * 256, [[128, 2], [1, 128]]),
                       decs.v(pp * 32 + dr * 16 + cn, [[8, 2], [0, 128]]), ALU.mult, [DSst[dr]] + Ddecs[pp], [DStmp[dr]])
                    TT(Sst.s(dr * 256, dr * 256 + 256), Stmp.s(dr * 256, dr * 256 + 256), P[bk].s(dr * 256, dr * 256 + 256),
                       ALU.add, [DStmp[dr], DP[bk]], [DSst[dr]])
                    yield
                    nxt = cn + 1 if dr == 0 else cn - 1
                    if 0 <= nxt < NCH and need_q:
                        CP(Sbf.s((dr * 8 + nxt) * 256, (dr * 8 + nxt) * 256 + 256), Sst.s(dr * 256, dr * 256 + 256),
                           [DSst[dr]], [DSbf], eng="act")
            for dr in range(2):
                CP(Useg.s((sidx * 2 + dr) * 256, (sidx * 2 + dr) * 256 + 256), Sst.s(dr * 256, dr * 256 + 256), [DSst[dr]],
                   [Dsum[sidx]])
            if not need_q:
                return
            for h in range(4):
                hp, hh = divmod(h, 2)
                ob = 2
                for tt in range(NT):
                    cs = slice(tt * 128, tt * 128 + 128)
                    ab = 1
                    ao = (tt % 2) * 256
                    mms = []
                    for dr in range(2):
                        dh = dr * 2 + hp
                        mms.append((P[ab].s(ao + dr * 128, ao + dr * 128 + 128),
                                    kinv.s((pp * 4 + dh) * SA + tt * 128, (pp * 4 + dh) * SA + tt * 128 + 128, p0=64 * hh, n=64),
                                    qdec.s((pp * 4 + dh) * SA + tt * 128, (pp * 4 + dh) * SA + tt * 128 + 128, p0=64 * hh, n=64), True, True))
                    PE(mms, [Dkinv[pp][hp], Dkinv[pp][2 + hp], Dqdec[pp][hp], Dqdec[pp][2 + hp]], [DP[ab]])
                    yield
                    ai = tt % 2
                    TT(Abf[ai].s(0, 128), P[ab].s(ao, ao + 128), mfb, ALU.mult, [DP[ab], Dcstb], [DAbf[ai]])
                    TT(Abf[ai].s(128, 256), P[ab].s(ao + 128, ao + 256), mbb, ALU.mult, [DP[ab], Dcstb], [DAbf[ai]])
                    mms = [(P[ob].s(tt * 128, tt * 128 + 128), vg.s(pp * 2048 + tt * 512 + h * 128, pp * 2048 + tt * 512 + h * 128 + 128),
                            Abf[ai].s(0, 128), True, False),
                           (P[ob].s(tt * 128, tt * 128 + 128), vg.s(pp * 2048 + tt * 512 + h * 128, pp * 2048 + tt * 512 + h * 128 + 128),
                            Abf[ai].s(128, 256), False, False)]
                    for cn in (2 * tt, 2 * tt + 1):
                        for dr in range(2):
                            if (dr == 0 and cn == 0) or (dr == 1 and cn == NCH - 1):
                                continue
                            dh = dr * 2 + hp
                            mms.append((P[ob].s(cn * 64, cn * 64 + 64),
                                        Sbf.s((dr * 8 + cn) * 256 + hp * 128, (dr * 8 + cn) * 256 + hp * 128 + 128, p0=64 * hh, n=64),
                                        qdec.s((pp * 4 + dh) * SA + cn * 64, (pp * 4 + dh) * SA + cn * 64 + 64, p0=64 * hh, n=64), False, False))
                    mms[-1] = mms[-1][:4] + (True,)
                    PE(mms, [Dvg[pp], DAbf[ai], DSbf, Dqdec[pp][hp], Dqdec[pp][2 + hp]], [DP[ob]])
                    yield
                CP(o_loc.s(h * TOK + tok0, h * TOK + tok0 + N), P[ob].s(0, N), [DP[ob]], [Doloc[sidx]], eng="act")

        def drain(g_):
            for _ in g_:
                pass

        def prep_a(a):
            hb = a % 2
            for t2 in range(SA // 128):
                x_t, x_d = load_x(I["xa"][a * SA + t2 * 128: a * SA + t2 * 128 + 128, :])
                ti_ = hb * 4 + t2
                for _ in norm_T_g(x_t, x_d, 0, 8, hT, DhT[ti_ * 8:(ti_ + 1) * 8], hb * 512 + t2 * 128, 0, Dmv):
                    yield

        drain(roundrobin([pa_front(CTX, hTc, 0, DhTc, CTX, TOK, NSEG, False, 0), prep_a(0)]))

        def fr(a):
            hb = a % 2
            return pa_front(SA, hT, hb * 512, DhT[hb * 32:hb * 32 + 32], 1024, a * SA, a, True, (a + 1) % 2)

        bg = stager()

        def drain_bg(gens):
            for _ in roundrobin(gens):
                try:
                    next(bg)
                except StopIteration:
                    pass

        gens = [pa_back(CTX, TOK, NSEG, False, 0), fr(0)]
        if NSEG > 1:
            gens.append(prep_a(1))
        drain_bg(gens)
        for a in range(NSEG):
            gens = [pa_back(SA, a * SA, a, True, (a + 1) % 2)]
            if a + 1 < NSEG:
                gens.append(fr(a + 1))
            if a + 2 < NSEG:
                gens.append(prep_a(a + 2))
            drain_bg(gens)
        drain(bg)
        use_stg[0] = True
        tap("oloc", o_loc.s(0, 4 * TOK), 4 * TOK, BF16, Doloc)
        tap("qsd", qsd.s(0, 4 * TOK), 4 * TOK, BF16, Dqsd)
        tap("Useg", Useg.s(0, NSUM * 512), NSUM * 512, F32, Dsum)
        tap("Aseg", Aseg.s(0, NSUM * 4), NSUM * 4, F32, Dsum)
        S.barrier()
        pa.close()
        if stop in ("a0", "a"):
            xsc.close()
            S.add("sp", None, reads=tapd)
            S.emit(st)
            return nc

        snd = sb("snd", 520, F32, xsc)
        gath = sb("gath", 4 * 520, F32, xsc)
        Dsnd, Dgath = Dep("snd"), Dep("gath")
        Dccin, Dccout = Dep("ccin"), Dep("ccout")
        U_ = lambda a, d: Useg.s((a * 2 + d) * 256, (a * 2 + d) * 256 + 256)
        U3 = lambda a, d: Useg.v((a * 2 + d) * 256, [[128, 2], [1, 128]])
        Ab = lambda a, d: Aseg.v((a * 2 + d) * 2, [[1, 2], [0, 128]])
        A2 = lambda a, d: Aseg.s((a * 2 + d) * 2, (a * 2 + d) * 2 + 2)
        MEMSET(snd.s(512, 520), 0.0, [Dsnd])
        CP(snd.s(0, 256), U_(0, 0), Dsum, [Dsnd])
        CP(snd.s(512, 514), A2(0, 0), Dsum, [Dsnd])
        for a in range(1, NSEG):
            TT(snd.v(0, [[128, 2], [1, 128]]), snd.v(0, [[128, 2], [1, 128]]), Ab(a, 0), ALU.mult, Dsum, [Dsnd])
            TT(snd.s(0, 256), snd.s(0, 256), U_(a, 0), ALU.add, Dsum, [Dsnd])
            TT(snd.s(512, 514), snd.s(512, 514), A2(a, 0), ALU.mult, Dsum, [Dsnd])
        CP(snd.s(256, 512), U_(NSEG - 1, 1), Dsum, [Dsnd])
        CP(snd.s(514, 516), A2(NSEG - 1, 1), Dsum, [Dsnd])
        for a in range(NSEG - 2, -1, -1):
            TT(snd.v(256, [[128, 2], [1, 128]]), snd.v(256, [[128, 2], [1, 128]]), Ab(a, 1), ALU.mult, Dsum, [Dsnd])
            TT(snd.s(256, 512), snd.s(256, 512), U_(a, 1), ALU.add, Dsum, [Dsnd])
            TT(snd.s(514, 516), snd.s(514, 516), A2(a, 1), ALU.mult, Dsum, [Dsnd])
        DMA("sp", cc_in.ap(), snd.s(0, 520), [Dsnd], [Dccin], Dccin)
        S.add("pool", lambda e, s_: e.collective_compute("AllGather", ALU.bypass, replica_groups=[[0, 1, 2, 3], [4, 5, 6, 7]],
                                                         ins=[cc_in.ap().opt()], outs=[cc_out.ap().opt()]).then_inc(s_, 1),
              reads=[Dccin], writes=[Dccout], dma=Dccout, inc=1)
        mod_late()
        DMA("sp", gath.v(0, [[520, 4], [1, 520]]), cc_out.ap().rearrange("(r p) c -> p r c", p=128), [Dccout], [Dgath], Dgath)
        Sc = sb("Sc", 2 * 256, F32, xsc)
        Sct2 = sb("Sct", 2 * 256, F32, xsc)
        DSc, DSct = [Dep("Sc0"), Dep("Sc1")], [Dep("Sct0"), Dep("Sct1")]
        DSent2 = [Dep("Sent0"), Dep("Sent1")]

        def compose(d):
            Sd = Sc.s(d * 256, d * 256 + 256)
            Sd3 = Sc.v(d * 256, [[128, 2], [1, 128]])
            St = Sct2.s(d * 256, d * 256 + 256)
            St3 = Sct2.v(d * 256, [[128, 2], [1, 128]])
            CP(Sd, U_(NSEG, d), Dsum, [DSc[d]])
            yield
            order = range(4) if d == 0 else range(3, -1, -1)
            for p in order:
                TT(St3, Sd3, gath.v(p * 520 + 512 + 2 * d, [[1, 2], [0, 128]]), ALU.mult, [DSc[d], Dgath], [DSct[d]])
                yield
                TT(St, St, gath.s(p * 520 + d * 256, p * 520 + d * 256 + 256), ALU.add, [Dgath], [DSct[d]])
                yield
                TT(St, St, Sd, ALU.subtract, [DSc[d]], [DSct[d]])
                yield
                STT(Sd, St, prm.s(88 + 4 * d + p, 88 + 4 * d + p + 1), Sd, ALU.mult, ALU.add, [DSct[d], Dprm], [DSc[d]])
                yield
            segs = range(NSEG) if d == 0 else range(NSEG - 1, -1, -1)
            for a in segs:
                CP(Sentbf.s((a * 2 + d) * 256, (a * 2 + d) * 256 + 256), Sd, [DSc[d]], [DSent2[d]], eng="act")
                yield
                TT(Sd3, Sd3, Ab(a, d), ALU.mult, Dsum, [DSc[d]])
                yield
                TT(Sd, Sd, U_(a, d), ALU.add, Dsum, [DSc[d]])
                yield

        drain(roundrobin([compose(0), compose(1)]))
        tap("Sent", Sentbf.s(0, NSEG * 512), NSEG * 512, BF16, DSent2)
        S.barrier()
        xsc.close()
        if stop == "x":
            S.add("sp", None, reads=tapd)
            S.emit(st)
            return nc

        pb = ExitStack()
        WS.append(sb("ws3", 4096, BF16, pb))
        WD.append(Dep("ws3"))
        kctx = sb("kctx", 4 * CTX, BF16, pb)
        vctx = sb("vctx", 2 * 520, BF16, pb)
        Dvctx = Dep("vctx")
        gq8 = sb("gq8", 1, F32, pb)
        Dgq8 = Dep("gq8")
        TS(gq8.s(0, 1), gq, 0.125, ALU.mult, [Dprm], [Dgq8])
        PTf = T(PT.h.bitcast(F32), 512)

        def make_qk_norm(stack, nslot=2):
            sqb = [sb("sqb%d" % i_, 512, BF16, stack) for i_ in range(nslot)]
            rsb = [sb("rsb%d" % i_, 512, F32, stack) for i_ in range(nslot)]
            Dsqb, Drsb = [Dep() for _ in range(nslot)], [Dep() for _ in range(nslot)]
            if nslot == 2:
                banks = [(P[0], DP[0], P[2], DP[2]), (P[1], DP[1], P[3], DP[3])]
            else:
                banks = [(P[0], DP[0], P[4], DP[4]), (P[1], DP[1], P[5], DP[5]), (P[2], DP[2], P[6], DP[6]),
                         (P[3], DP[3], PTf, DPT)]

            def qk_norm_g(proj_fn, slot, N, g_ap, g_dep, dst_ap, dst_dep):
                pb_, dpb_, sb_, dsb_ = banks[slot]
                proj_fn(pb_, dpb_)
                yield
                ACT(sqb[slot].s(0, N), pb_.s(0, N), AF.Square, [dpb_], [Dsqb[slot]])
                yield
                PE([(sb_.s(0, N), blkb, sqb[slot].s(0, N), True, True)], [Dsqb[slot], Dcstb], [dsb_])
                yield
                ACT(rsb[slot].s(0, N), sb_.s(0, N), AF.Ln, [dsb_, Dkc], [Drsb[slot]], scale=1.0 / 64, bias=kc.s(0, 1))
                yield
                ACT(rsb[slot].s(0, N), rsb[slot].s(0, N), AF.Exp, [], [Drsb[slot]], scale=-0.5)
                yield
                STT(dst_ap, pb_.s(0, N), g_ap, rsb[slot].s(0, N), ALU.mult, ALU.mult, [dpb_, Drsb[slot], g_dep], [dst_dep])
                yield
            return qk_norm_g

        def run_groups(gens, k):
            for j in range(0, len(gens), k):
                drain(roundrobin(gens[j:j + k]))

        def run_pairs(gens):
            for j in range(0, len(gens), 2):
                drain(roundrobin(gens[j:j + 2]))

        MEMSET(vctx.s(0, 2 * 520), 1.0, [Dvctx])
        scs = ExitStack()
        qk_norm_g = make_qk_norm(scs)
        ws, wd, n = loadw("wk_na", 0, 8, 0, 512)
        gens = []
        Dkctx_l = [Dep("kctx%d" % i_) for i_ in range(4)]
        for hp in range(4):
            def pj(pb_, dpb_, hp=hp, ws=ws, wd=wd):
                PE([(pb_.s(0, CTX), ws.v(kt * 512 + hp * 128, [[1, 128]]), hTc.s(kt * CTX, kt * CTX + CTX), kt == 0, kt == 7)
                    for kt in range(8)], [wd] + DhTc, [dpb_])
            gens.append(qk_norm_g(pj, hp % 2, CTX, gk, Dprm, kctx.s(hp * CTX, hp * CTX + CTX), Dkctx_l[hp]))
        run_groups(gens, 2)
        ws, wd, n = loadw("wv_na", 0, 8, 0, 512)
        for tci in range(2):
            b = 4 + tci
            PE([(P[b].s(0, 512), hTc.s(kt * CTX + tci * 128, kt * CTX + tci * 128 + 128), ws.s(kt * 512, kt * 512 + 512),
                 kt == 0, kt == 7) for kt in range(8)], [wd] + DhTc, [DP[b]])
            CP(vctx.v(tci * 520, [[65, 8], [1, 64]]), P[b].v(0, [[64, 8], [1, 64]]), [DP[b]], [Dvctx], eng="act")
        S.barrier()
        scs.close()
        tap("kctx", kctx.s(0, 4 * CTX), 4 * CTX, BF16, Dkctx_l)
        tap("vctx", vctx.s(0, 1040), 1040, BF16, [Dvctx])

        ost = [sb("ost%d" % i, 512, F32, pb) for i in range(2)]
        Dost = [Dep("ost0"), Dep("ost1")]
        Dout = [Dep("out0"), Dep("out1")]
        octr = [0]
        SPEC = {(0, 0): 1, (0, 1): 2, (NSEG - 1, 2): 3, (NSEG - 1, 3): 4}

        def halo_prep(sn):
            hTn, DhTn = hTb[sn % 2], DhTb[sn % 2]
            for tt in range(0, 8, 2):
                gs = []
                for slot, t2 in enumerate((tt, tt + 1)):
                    x_t, x_d = load_x(I["xh"][sn * 1024 + t2 * 128: sn * 1024 + t2 * 128 + 128, :])
                    gs.append(norm_T_g(x_t, x_d, 0, 8, hTn, DhTn[t2 * 8:(t2 + 1) * 8], t2 * 128, slot, Dmv))
                for _ in roundrobin(gs):
                    yield

        for s_i in range(NSEG):
            sg = ExitStack()
            o_naT = sb("o_naT", 4 * 512, BF16, sg)
            Do_naT = Dep("o_naT")
            hT, DhT = hTb[s_i % 2], DhTb[s_i % 2]
            if s_i == 0:
                drain(halo_prep(0))
            hown = lambda kt, hT=hT: hT.s(kt * 1024 + 256, kt * 1024 + 768)
            b1 = ExitStack()
            tabI = sb("tabI", 5120, BF16, b1)
            tabS = sb("tabS", 5120, BF16, b1)
            DtabI, DtabS = Dep("tabI"), Dep("tabS")
            DMA("pool", tabI.s(0, 5120), I["btab"][0], [], [DtabI], DtabI, ab=True)
            kn = sb("kn", 4 * 1024, BF16, b1)
            qn = sb("qn", 4 * 512, BF16, b1)
            vaug = sb("vaug", 8 * 520, BF16, b1)
            Dkn = [Dep("kn%d" % i_) for i_ in range(8)]
            Dqn = [Dep("qn%d" % i_) for i_ in range(4)]
            Dvaug = Dep("vaug")
            ptA = [sb("ptA%d" % i, 512, BF16, b1) for i in range(2)]
            ptB = [sb("ptB%d" % i, 384, BF16, b1) for i in range(2)]
            DptA = [Dep("ptA0"), Dep("ptA1")]
            DptB = [Dep("ptB0"), Dep("ptB1")]
            otm = sb("otm", 512, BF16, b1)
            Dotm = Dep("otm")
            rcp = sb("rcp", 8, F32, b1)
            Drcp = Dep("rcp")
            MEMSET(vaug.s(0, 8 * 520), 1.0, [Dvaug])
            qk_norm_g = make_qk_norm(b1, 4)
            ws, wd, n = loadw("wq_na", 0, 8, 0, 512)
            gens = []
            for hp in range(4):
                def pj(pb_, dpb_, hp=hp, ws=ws, wd=wd):
                    PE([(pb_.s(0, 512), ws.v(kt * 512 + hp * 128, [[1, 128]]), hown(kt), kt == 0, kt == 7) for kt in range(8)],
                       [wd] + DhT, [dpb_])
                gens.append(qk_norm_g(pj, hp % 4, 512, gq8.s(0, 1), Dgq8, qn.s(hp * 512, hp * 512 + 512), Dqn[hp]))
            ws, wd, n = loadw("wk_na", 0, 8, 0, 512)
            for hp in range(4):
                for tg in range(2):
                    def pj(pb_, dpb_, hp=hp, tg=tg, ws=ws, wd=wd):
                        PE([(pb_.s(0, 512), ws.v(kt * 512 + hp * 128, [[1, 128]]),
                             hT.s(kt * 1024 + tg * 512, kt * 1024 + tg * 512 + 512), kt == 0, kt == 7) for kt in range(8)],
                           [wd] + DhT, [dpb_])
                    gens.append(qk_norm_g(pj, (hp * 2 + tg) % 4, 512, gk, Dprm, kn.s(hp * 1024 + tg * 512, hp * 1024 + tg * 512 + 512),
                                          Dkn[hp * 2 + tg]))
            run_groups(gens, 4)
            ws, wd, n = loadw("wv_na", 0, 8, 0, 512)
            for tt in range(8):
                b = 4 + tt % 2
                PE([(P[b].s(0, 512), hT.s(kt * 1024 + tt * 128, kt * 1024 + tt * 128 + 128), ws.s(kt * 512, kt * 512 + 512),
                     kt == 0, kt == 7) for kt in range(8)], [wd] + DhT, [DP[b]])
                CP(vaug.v(tt * 520, [[65, 8], [1, 64]]), P[b].v(0, [[64, 8], [1, 64]]), [DP[b]], [Dvaug], eng="act")
            if s_i == 0:
                tap("qn", qn.s(0, 2048), 2048, BF16, Dqn)
                tap("kn", kn.s(0, 4096), 4096, BF16, Dkn)
                tap("vaug", vaug.s(0, 8 * 520), 8 * 520, BF16, [Dvaug])
            for i in range(4):
                if stop == "b1a" or (stop in ("b1b", "b1c", "b1d", "b1e") and i == 1):
                    break
                sp_ = SPEC.get((s_i, i))
                if sp_ is None:
                    tab, Dtab = tabI, DtabI
                else:
                    DMA("pool", tabS.s(0, 5120), I["btab"][sp_], [], [DtabS], DtabS, ab=True)
                    tab, Dtab = tabS, DtabS

                def ST(h, i=i, tab=tab, Dtab=Dtab):
                    hp, hh = divmod(h, 2)
                    ba, bb_ = (0, 1) if h % 2 == 0 else (2, 3)
                    qap = qn.s(hp * 512 + i * 128, hp * 512 + i * 128 + 128, p0=64 * hh, n=64)
                    mms = []
                    for t_ in range(4):
                        o_ = P[ba].s(t_ * 128, t_ * 128 + 128)
                        mms.append((o_, kn.s(hp * 1024 + (i + t_) * 128, hp * 1024 + (i + t_) * 128 + 128, p0=64 * hh, n=64), qap, True, False))
                        mms.append((o_, identb, tab.s(h * 640 + t_ * 128, h * 640 + t_ * 128 + 128), False, True))
                    PE(mms, Dkn + Dqn + [Dtab, Dcstb], [DP[ba]])
                    mms = []
                    o_ = P[bb_].s(0, 128)
                    mms.append((o_, kn.s(hp * 1024 + (i + 4) * 128, hp * 1024 + (i + 4) * 128 + 128, p0=64 * hh, n=64), qap, True, False))
                    mms.append((o_, identb, tab.s(h * 640 + 512, h * 640 + 640), False, True))
                    for c_ in range(2):
                        mms.append((P[bb_].s(128 + c_ * 128, 256 + c_ * 128),
                                    kctx.s(hp * CTX + c_ * 128, hp * CTX + c_ * 128 + 128, p0=64 * hh, n=64), qap, True, True))
                    PE(mms, Dkn + Dqn + [Dtab, Dcstb] + Dkctx_l, [DP[bb_]])

                def EXPV(h, i=i):
                    j = h % 2
                    ba, bb_ = (0, 1) if h % 2 == 0 else (2, 3)
                    ACT(ptA[j].s(0, 512), P[ba].s(0, 512), AF.Exp, [DP[ba]], [DptA[j]])
                    ACT(ptB[j].s(0, 384), P[bb_].s(0, 384), AF.Exp, [DP[bb_]], [DptB[j]])
                    ob = 4 + h // 4
                    oc = (h % 4) * 65
                    o_ = P[ob].s(oc, oc + 65)
                    mms = []
                    for t_ in range(4):
                        mms.append((o_, ptA[j].s(t_ * 128, t_ * 128 + 128), vaug.s((i + t_) * 520 + h * 65, (i + t_) * 520 + h * 65 + 65),
                                    t_ == 0, False))
                    mms.append((o_, ptB[j].s(0, 128), vaug.s((i + 4) * 520 + h * 65, (i + 4) * 520 + h * 65 + 65), False, False))
                    for c_ in range(2):
                        mms.append((o_, ptB[j].s(128 + c_ * 128, 256 + c_ * 128), vctx.s(c_ * 520 + h * 65, c_ * 520 + h * 65 + 65),
                                    False, c_ == 1))
                    PE(mms, [DptA[j], DptB[j], Dvaug, Dvctx], [DP[ob]])

                if stop == "b1e":
                    break
                ST(0)
                if stop == "b1c":
                    break
                if stop == "b1d":
                    EXPV(0)
                    break
                for h in range(8):
                    if h + 1 < 8:
                        ST(h + 1)
                    EXPV(h)
                    if h % 4 == 3:
                        ob = 4 + h // 4
                        g4 = h // 4
                        S.add("dve", lambda e, ob=ob, g4=g4: e.reciprocal(out=rcp.s(g4 * 4, g4 * 4 + 4), in_=P[ob].v(64, [[65, 4]])),
                              [DP[ob]], [Drcp])
                        TT(otm.v(g4 * 256, [[64, 4], [1, 64]]), P[ob].v(0, [[65, 4], [1, 64]]), rcp.v(g4 * 4, [[1, 4], [0, 64]]),
                           ALU.mult, [DP[ob], Drcp], [Dotm])
                PE([("T", PT.s(k_ * 128, k_ * 128 + 128), otm.s(k_ * 128, k_ * 128 + 128), identb) for k_ in range(4)],
                   [Dotm, Dcstb], [DPT])
                CP(o_naT.v(i * 128, [[512, 4], [1, 128]]), PT.v(0, [[128, 4], [1, 128]]), [DPT], [Do_naT])
            if s_i == 0:
                tap("o_naT", o_naT.s(0, 2048), 2048, BF16, [Do_naT])
            S.barrier()
            b1.close()
            if stop in ("b1", "b1a", "b1b", "b1c", "b1d", "b1e"):
                sg.close()
                break
            sg2 = ExitStack()
            x1 = sb("x1", 4 * 1024, F32, sg2)
            Dx1 = [Dep("x1_%d" % t_) for t_ in range(4)]
            h2T = sb("h2T", 8 * 512, BF16, sg2)
            Dh2T = [Dep("h2T%d" % i_) for i_ in range(32)]
            b2 = ExitStack()
            sog = [sb("sog%d" % i_, 512, F32, b2) for i_ in range(2)]
            ogt = [sb("ogt%d" % i_, 512, F32, b2) for i_ in range(2)]
            gt2 = [sb("gt2_%d" % i_, 512, F32, b2) for i_ in range(2)]
            gsq = [sb("gsq%d" % i_, 512, BF16, b2) for i_ in range(2)]
            Dgsq = [Dep(), Dep()]
            Dsog, Dogt, Dgt2 = [Dep(), Dep()], [Dep(), Dep()], [Dep(), Dep()]
            o_gT = sb("o_gT", 4 * 512, BF16, b2)
            Do_gT = [Dep("o_gT%d" % i_) for i_ in range(4)]
            sig = [sb("sig%d" % i, 512, F32, b2) for i in range(2)]
            Dsig = [Dep("sig0"), Dep("sig1")]
            ytmp = sb("ytmp", 4 * 512, F32, b2)
            Dytmp = Dep("ytmp")
            yT = sb("yT", 8 * 512, BF16, b2)
            DyT = Dep("yT")
            mtmp, Dmtmp = ogt[0], Dogt[0]
            for tt in range(4):
                DMA("sp", x1.s(tt * 1024, tt * 1024 + 1024), I["xa"][s_i * 512 + tt * 128: s_i * 512 + tt * 128 + 128, :], [], [Dx1[tt]],
                    Dx1[tt], ab=True)
            wso, wdo, n = loadw("w_og", 0, 8, 0, 512)

            def gla_fin(h, sl):
                hp, hh = divmod(h, 2)
                b0, b1_, b2_ = (0, 1, 2) if sl == 0 else (3, 4, 5)
                sog_, ogt_, gt2_ = sog[sl], ogt[sl], gt2[sl]
                PE([(P[b0].s(0, 512), wso.v(kt * 512 + h * 128, [[1, 128]]), hown(kt), kt == 0, kt == 7) for kt in range(8)],
                   [wdo] + DhT, [DP[b0]])
                yield
                ACT(sog_.s(0, 512), P[b0].s(0, 512), AF.Silu, [DP[b0]], [Dsog[sl]])
                yield
                PE([(P[b1_].s(0, 512), Sentbf.s((s_i * 2 + 0) * 256 + hp * 128, (s_i * 2 + 0) * 256 + hp * 128 + 128, p0=64 * hh, n=64),
                     qsd.s((0 * 2 + hp) * TOK + s_i * 512, (0 * 2 + hp) * TOK + s_i * 512 + 512, p0=64 * hh, n=64), True, False),
                    (P[b1_].s(0, 512), Sentbf.s((s_i * 2 + 1) * 256 + hp * 128, (s_i * 2 + 1) * 256 + hp * 128 + 128, p0=64 * hh, n=64),
                     qsd.s((1 * 2 + hp) * TOK + s_i * 512, (1 * 2 + hp) * TOK + s_i * 512 + 512, p0=64 * hh, n=64), False, True)],
                   DSent2 + [Dqsd[s_i]], [DP[b1_]])
                yield
                TT(ogt_.s(0, 512), P[b1_].s(0, 512), o_loc.s(h * TOK + s_i * 512, h * TOK + s_i * 512 + 512), ALU.add,
                   [DP[b1_], Doloc[s_i]], [Dogt[sl]])
                yield
                if s_i == 0 and h == 0:
                    tap("ogt", ogt_.s(0, 512), 512, F32, [Dogt[sl]])
                ACT(gsq[sl].s(0, 512), ogt_.s(0, 512), AF.Square, [Dogt[sl]], [Dgsq[sl]])
                yield
                PE([(P[b2_].s(0, 512), onesb.s(0, 128), gsq[sl].s(0, 512), True, True)], [Dgsq[sl], Dones], [DP[b2_]])
                yield
                ACT(gt2_.s(0, 512), P[b2_].s(0, 512), AF.Ln, [DP[b2_], Dkc], [Dgt2[sl]], scale=1.0 / 128, bias=kc.s(0, 1))
                yield
                ACT(gt2_.s(0, 512), gt2_.s(0, 512), AF.Exp, [], [Dgt2[sl]], scale=-0.5)
                yield
                STT(ogt_.s(0, 512), ogt_.s(0, 512), gn, gt2_.s(0, 512), ALU.mult, ALU.mult, [Dgt2[sl], Dprm], [Dogt[sl]])
                yield
                TT(o_gT.s(h * 512, h * 512 + 512), ogt_.s(0, 512), sog_.s(0, 512), ALU.mult, [Dogt[sl], Dsog[sl]], [Do_gT[h]])
                yield

            run_groups([gla_fin(h, h % 2) for h in range(4)], 2)
            if s_i == 0:
                tap("o_gT", o_gT.s(0, 2048), 2048, BF16, Do_gT)
            for half in range(2):
                for br in range(2):
                    wsb, wdb, n = loadw("w_br_na" if br == 0 else "w_br_gla", 0, 4, half * 512, 512)
                    wsm, wdm, n = loadw("w_ma" if br == 0 else "w_mb", 0, 8, half * 512, 512)
                    src, dsrc = (o_naT, [Do_naT]) if br == 0 else (o_gT, Do_gT)
                    for cj in range(4):
                        c = half * 4 + cj
                        pb_, mb_ = (0, 1) if cj % 2 == 0 else (2, 3)
                        PE([(P[pb_].s(0, 512), wsb.v(k_ * 512 + cj * 128, [[1, 128]]), src.s(k_ * 512, k_ * 512 + 512), k_ == 0, k_ == 3)
                            for k_ in range(4)], [wdb] + dsrc, [DP[pb_]])
                        PE([(P[mb_].s(0, 512), wsm.v(kt * 512 + cj * 128, [[1, 128]]), hown(kt), kt == 0, kt == 7) for kt in range(8)],
                           [wdm] + DhT, [DP[mb_]])
                        j = cj % 2
                        ACT(sig[j].s(0, 512), P[mb_].s(0, 512), AF.Sigmoid, [DP[mb_]], [Dsig[j]])
                        if br == 0:
                            TT(ytmp.s(cj * 512, cj * 512 + 512), P[pb_].s(0, 512), sig[j].s(0, 512), ALU.mult, [DP[pb_], Dsig[j]],
                               [Dytmp])
                        else:
                            TT(mtmp.s(0, 512), P[pb_].s(0, 512), sig[j].s(0, 512), ALU.mult, [DP[pb_], Dsig[j]], [Dmtmp])
                            TT(yT.s(c * 512, c * 512 + 512), mtmp.s(0, 512), ytmp.s(cj * 512, cj * 512 + 512), ALU.add,
                               [Dmtmp, Dytmp], [DyT])
            if s_i == 0:
                tap("yT", yT.s(0, 4096), 4096, BF16, [DyT])
            for half in range(2):
                ws, wd, n = loadw("w_out", 0, 8, half * 512, 512)
                for tt in range(4):
                    b = 4 + tt % 2
                    PE([(P[b].s(0, 512), yT.s(c * 512 + tt * 128, c * 512 + tt * 128 + 128), ws.s(c * 512, c * 512 + 512), c == 0, c == 7)
                        for c in range(8)], [wd, DyT], [DP[b]])
                    TT(mtmp.s(0, 512), P[b].s(0, 512), gbc.s(half * 512, half * 512 + 512), ALU.mult, [DP[b], Dgbc], [Dmtmp])
                    TT(x1.s(tt * 1024 + half * 512, tt * 1024 + half * 512 + 512), x1.s(tt * 1024 + half * 512, tt * 1024 + half * 512 + 512),
                       mtmp.s(0, 512), ALU.add, [Dmtmp], [Dx1[tt]])
            norm_tiles([(x1.s(tt * 1024, tt * 1024 + 1024), Dx1[tt], tt * 128, tt) for tt in range(4)], 32, 40, h2T, Dh2T, Dmv2)
            if s_i == 0:
                tap("x1", x1.s(0, 4096), 4096, F32, Dx1)
                tap("h2T", h2T.s(0, 4096), 4096, BF16, Dh2T)
            S.barrier()
            b2.close()
            b3 = ExitStack()
            actT = sb("actT", NF * 512, BF16, b3)
            Dact = Dep("actT")
            sgl = [sb("sgl%d" % i, 512, F32, b3) for i in range(2)]
            Dsgl = [Dep("sgl0"), Dep("sgl1")]
            def ffn_gu():
              for f2 in range(NF // 2):
                  i_ = wctr[0] % len(WS)
                  wctr[0] += 1
                  wsf, wdf = WS[i_], WD[i_]

                  def ld(e, s_, f2=f2, wsf=wsf):
                      e.dma_start(out=wsf.v(0, [[512, 8], [1, 256]]),
                                  in_=I["w_gate"][:, f2 * 256:f2 * 256 + 256].rearrange("(k p) n -> p k n", p=128)).then_inc(s_, 16)
                      e.dma_start(out=wsf.v(256, [[512, 8], [1, 256]]),
                                  in_=I["w_up"][:, f2 * 256:f2 * 256 + 256].rearrange("(k p) n -> p k n", p=128)).then_inc(s_, 16)
                  S.add("pool", ld, [], [wdf], dma=wdf, ndma=2)
                  for ff in range(2):
                      f = 2 * f2 + ff
                      bg, bu = (0, 1) if ff == 0 else (2, 3)
                      PE([(P[bg].s(0, 512), wsf.v(kt * 512 + ff * 128, [[1, 128]]), h2T.s(kt * 512, kt * 512 + 512), kt == 0, kt == 7)
                          for kt in range(8)], [wdf] + Dh2T, [DP[bg]])
                      yield
                      PE([(P[bu].s(0, 512), wsf.v(kt * 512 + 256 + ff * 128, [[1, 128]]), h2T.s(kt * 512, kt * 512 + 512), kt == 0, kt == 7)
                          for kt in range(8)], [wdf] + Dh2T, [DP[bu]])
                      yield
                      ACT(sgl[ff].s(0, 512), P[bg].s(0, 512), AF.Silu, [DP[bg]], [Dsgl[ff]])
                      yield
                      TT(actT.s(f * 512, f * 512 + 512), P[bu].s(0, 512), sgl[ff].s(0, 512), ALU.mult, [DP[bu], Dsgl[ff]], [Dact])
                      yield
            gens = [ffn_gu()]
            if s_i + 1 < NSEG:
                gens.append(halo_prep(s_i + 1))
            drain(roundrobin(gens))
            for half in range(2):
                banks = (0, 1, 2, 3) if half == 0 else (4, 5, 6, 3)
                nf4 = (NF + 3) // 4
                for f4 in range(nf4):
                    kk = min(4, NF - 4 * f4)
                    ws, wd, n = loadw("w_down", f4 * 512, kk, half * 512, 512)
                    for tt in range(4):
                        PE([(P[banks[tt]].s(0, 512), actT.s((4 * f4 + k_) * 512 + tt * 128, (4 * f4 + k_) * 512 + tt * 128 + 128),
                             ws.s(k_ * 512, k_ * 512 + 512), (f4 == 0 and k_ == 0), (f4 == nf4 - 1 and k_ == kk - 1)) for k_ in range(kk)],
                           [wd, Dact], [DP[banks[tt]]])
                for tt in range(4):
                    oi = octr[0] % 2
                    octr[0] += 1
                    b = banks[tt]
                    TT(ost[oi].s(0, 512), P[b].s(0, 512), gbc.s(1024 + half * 512, 1024 + half * 512 + 512), ALU.mult,
                       [DP[b], Dgbc, Dout[oi]], [Dost[oi]])
                    TT(ost[oi].s(0, 512), ost[oi].s(0, 512), x1.s(tt * 1024 + half * 512, tt * 1024 + half * 512 + 512), ALU.add,
                       [Dx1[tt]], [Dost[oi]])
                    DMA("sp", out[s_i * 512 + tt * 128: s_i * 512 + tt * 128 + 128, half * 512: half * 512 + 512], ost[oi].s(0, 512),
                        [Dost[oi]], [Dout[oi]], Dout[oi], nobar=True)
            S.barrier()
            b3.close()
            sg2.close()
            sg.close()
            if stop == "b3":
                break
        S.add("sp", None, reads=Dout + tapd)
        pb.close()
        S.emit(st)
    return nc


_NC_CACHE = {}


def kernel(**inputs):
    maps = prep_inputs(**inputs)
    if "nc" not in _NC_CACHE:
        _NC_CACHE["nc"] = build()
    nc = _NC_CACHE["nc"]
    res = run_bass_kernel_spmd(nc, maps, core_ids=list(range(NCORE)))
    out = np.empty((2, SEQ, D), np.float32)
    for core in range(NCORE):
        bb, r = divmod(core, 4)
        out[bb, r * TOK:(r + 1) * TOK] = np.asarray(res.results[core]["out"], dtype=np.float32)
    return out
```

```python
import numpy as np
from contextlib import ExitStack
import concourse.bass as bass
import concourse.mybir as mybir
from concourse.bass_utils import run_bass_kernel_spmd

F32 = mybir.dt.float32
BF16 = mybir.dt.bfloat16
AF = mybir.ActivationFunctionType
ALU = mybir.AluOpType

D = 1024
SEQ = 8192
NCORE = 8
TOK = 2048
SA = 512
WAW_RELAX = True
STG_GAP = 80
NSEG = TOK // SA
CTX = 256
DFF = 2816
NF = DFF // 128
EPS = 1e-6
NEG = -1e30


class Dep:
    __slots__ = ("name", "w", "rs", "sem", "semcnt")

    def __init__(self, name=""):
        self.name = name
        self.w = None
        self.rs = []
        self.sem = None
        self.semcnt = 0


class Op:
    __slots__ = ("eng", "fn", "deps", "dma", "signal", "ev", "idx", "inc")

    def __init__(self, eng, fn, dma, inc):
        self.eng = eng
        self.fn = fn
        self.deps = []
        self.dma = dma
        self.inc = inc
        self.signal = False
        self.ev = None


ENGS = ("pe", "act", "dve", "pool", "sp")


class Sched:
    def __init__(self, nc, same_engine_sync=True):
        self.nc = nc
        self.ops = []
        self.same_engine_sync = same_engine_sync
        self.last = {e: None for e in ENGS}
        self.pending_dma = []
        self.bar_evs = []

    def add(self, eng, fn, reads=(), writes=(), dma=None, ndma=1, inc=16, after_barrier=False, nobar=False):
        deps = []
        if after_barrier:
            deps.extend(getattr(self, "bar_evs", []))
        for r in reads:
            if r.w is not None:
                deps.append(r.w)
        for w in writes:
            if w.w is not None:
                deps.append(w.w)
            deps.extend(w.rs)
        op = Op(eng, fn, dma, ndma * inc)
        raw = set(id(r.w) for r in reads if r.w is not None)
        seen = set()
        for d in deps:
            if id(d) in seen:
                continue
            seen.add(id(d))
            if d.dma is None and d.eng == eng and (eng == "pe" or not self.same_engine_sync):
                continue
            if WAW_RELAX and dma is None and d.dma is None and d.eng == eng and id(d) not in raw:
                continue
            op.deps.append(d)
        for r in reads:
            r.rs.append(op)
        for w in writes:
            w.w = op
            w.rs = []
        self.ops.append(op)
        if dma is None:
            self.last[eng] = op
        elif not nobar:
            self.pending_dma.append(op)
        return op

    def barrier(self):
        evs = [o for o in self.last.values() if o is not None] + list(self.pending_dma)
        self.pending_dma = []
        self.bar_evs = evs
        for e in ("pe", "act", "dve"):
            op = Op(e, None, None, 0)
            op.deps = [d for d in evs if not (d.dma is None and d.eng == e and e == "pe")]
            self.ops.append(op)

    def emit(self, stack):
        nc = self.nc
        for op in self.ops:
            for d in op.deps:
                d.signal = True
        esem = {}
        for e in ("pe", "act", "dve", "pool"):
            esem[e] = stack.enter_context(nc.semaphore("sem_" + e))
        ecnt = {e: 0 for e in esem}
        nsem = 0
        for op in self.ops:
            if op.dma is not None:
                d = op.dma
                if d.sem is None:
                    d.sem = stack.enter_context(nc.semaphore("dsem%d" % nsem))
                    nsem += 1
                d.semcnt += op.inc
                op.ev = (d.sem, d.semcnt)
            elif op.signal and op.fn is not None:
                ecnt[op.eng] += 1
                op.ev = (esem[op.eng], ecnt[op.eng])
        self.nsem = nsem
        per = {e: [o for o in self.ops if o.eng == e] for e in ENGS}
        block = stack.enter_context(nc.Block())

        def run(engobj, ops):
            waited = {}
            for op in ops:
                for d in op.deps:
                    if d.ev is None:
                        continue
                    sem, val = d.ev
                    k = id(sem)
                    if waited.get(k, 0) >= val:
                        continue
                    waited[k] = val
                    engobj.wait_ge(sem, val)
                if op.fn is None:
                    continue
                if op.dma is not None:
                    op.fn(engobj, op.ev[0])
                else:
                    inst = op.fn(engobj)
                    if op.signal:
                        inst.then_inc(op.ev[0], 1)

        @block.tensor
        def _(e):
            run(e, per["pe"])

        @block.scalar
        def _(e):
            run(e, per["act"])

        @block.vector
        def _(e):
            run(e, per["dve"])

        @block.gpsimd
        def _(e):
            run(e, per["pool"])

        @block.sync
        def _(e):
            run(e, per["sp"])


class T:
    def __init__(self, h, F):
        self.h = h
        self.F = F

    def v(self, c0, dims, p0=0, n=128):
        return bass.AP(self.h, p0 * self.F + c0, [[self.F, n]] + [list(d) for d in dims])

    def s(self, c0, c1, p0=0, n=128):
        return bass.AP(self.h, p0 * self.F + c0, [[self.F, n], [1, c1 - c0]])


def _gmap(Rs, b):
    g = Rs - 4 + b
    if g < 0:
        return {-4: 5, -3: 6, -2: 7, -1: 7}[g]
    if g > 127:
        return 120 if g == 128 else 121
    return g


def _bias_table(rpb, Rs, i):
    tab = np.full((128, 8, 5, 128), NEG, np.float32)
    cq = np.arange(64)
    ck = np.arange(64)
    col_start = np.clip(cq - 8, 0, 48)
    in_win = (ck[None, :] >= col_start[:, None]) & (ck[None, :] < col_start[:, None] + 16)
    dc = np.clip(ck[None, :] - cq[:, None], -15, 15) + 15
    seen = set()
    for kb in range(9):
        b = 2 * i + kb
        g = _gmap(Rs, b)
        first = g not in seen
        seen.add(g)
        t, ph = divmod(kb * 64, 128)
        for ql in range(2):
            rq = Rs + 2 * i + ql
            rs = min(max(rq - 4, 0), 120)
            if not (first and rs <= g < rs + 8):
                continue
            dr = g - rq + 7
            vals = rpb[:, dr, :][:, dc]
            vals = np.where(in_win[None], vals, NEG)
            tab[ph:ph + 64, :, t, ql * 64:(ql + 1) * 64] = np.transpose(vals, (2, 0, 1))
    return tab.reshape(128, 8 * 5 * 128)


def _fm(v):
    return np.ascontiguousarray(v.reshape(-1, 128).T)


def prep_inputs(x, c, ctx, c_ctx, w_mod, b_mod, norm1_g, norm2_g, w_in, na_q_norm_g, na_k_norm_g, na_rpb,
                gla_w_alpha, gla_b_alpha, gla_norm_g, w_branch_na, w_branch_gla, w_out,
                w_ffn_gate, w_ffn_up, w_ffn_down):
    f = lambda a: np.ascontiguousarray(np.asarray(a, dtype=np.float32))
    x, c, ctx, c_ctx = f(x), f(c), f(ctx), f(c_ctx)
    w_in0 = f(w_in)[0]
    offs = np.cumsum([0, 512, 512, 512, 256, 256, 512, 512, 32, 1024, 1024])
    sl = lambda k: w_in0[:, offs[k]:offs[k + 1]]
    wq_na, wk_na, wv_na, wq_g, wk_g, wv_g, w_og, w_lr, w_ma, w_mb = [sl(k) for k in range(10)]
    d = np.arange(256)
    dd = d % 64
    swp = (d // 64) * 64 + np.where(dd % 32 < 16, dd + 16, dd - 16)
    wg1 = np.ascontiguousarray(np.concatenate([wq_g, wq_g[:, swp], wk_g, wk_g[:, swp]], axis=1))
    wal = f(gla_w_alpha)[0]
    walpha = np.zeros((32, 512), np.float32)
    walpha[0:16, 0:256] = wal[0]
    walpha[16:32, 256:512] = wal[1]
    bal = f(gla_b_alpha)[0]
    balpha = np.ascontiguousarray(np.stack([bal[0, 0:128], bal[0, 128:256], bal[1, 0:128], bal[1, 128:256]], axis=1))
    rpb = f(na_rpb)[0]
    gq = np.tile(f(na_q_norm_g)[0], 2).reshape(128, 1)
    gk = np.tile(f(na_k_norm_g)[0], 2).reshape(128, 1)
    gn = f(gla_norm_g)[0].reshape(128, 1)
    pvec = np.ascontiguousarray(np.concatenate([gq, gk, gn], axis=1))
    ident = np.eye(128, dtype=np.float32)
    blk = np.zeros((128, 128), np.float32)
    blk[:64, :64] = 1
    blk[64:, 64:] = 1
    j = np.arange(128)[:, None]
    i = np.arange(128)[None, :]
    same = (j // 64) == (i // 64)
    mf = (same & (j <= i)).astype(np.float32)
    mb = (same & (j >= i)).astype(np.float32)
    consts = np.ascontiguousarray(np.concatenate([ident, blk, mf, mb], axis=1))
    freqs = (10000.0 ** (-np.arange(16, dtype=np.float32) / 16)).astype(np.float32)
    tab_int = _bias_table(rpb, 40, 0)
    shared = {
        "w_mod": f(w_mod)[0], "b_modT": _fm(f(b_mod)[0]),
        "n1g": _fm(f(norm1_g)[0]), "n2g": _fm(f(norm2_g)[0]),
        "wq_na": np.ascontiguousarray(wq_na), "wk_na": np.ascontiguousarray(wk_na),
        "wv_na": np.ascontiguousarray(wv_na), "wg1": wg1, "wv_g": np.ascontiguousarray(wv_g),
        "w_og": np.ascontiguousarray(w_og), "w_lr": np.ascontiguousarray(w_lr),
        "w_ma": np.ascontiguousarray(w_ma), "w_mb": np.ascontiguousarray(w_mb),
        "walpha": walpha, "balpha": balpha, "pvec": pvec,
        "w_br_na": f(w_branch_na)[0], "w_br_gla": f(w_branch_gla)[0], "w_out": f(w_out)[0],
        "w_gate": f(w_ffn_gate)[0], "w_up": f(w_ffn_up)[0], "w_down": f(w_ffn_down)[0],
        "consts": consts,
    }
    maps = []
    for core in range(NCORE):
        bb, r = divmod(core, 4)
        R0 = 32 * r
        xa = x[bb, R0 * 64:(R0 + 32) * 64]
        rows = []
        for s in range(NSEG):
            Rs = R0 + 8 * s
            for b in range(16):
                g = _gmap(Rs, b)
                rows.append(x[bb, g * 64:(g + 1) * 64])
        xh = np.ascontiguousarray(np.concatenate(rows, axis=0))
        cvec = np.ascontiguousarray(np.concatenate([_fm(c[bb]), _fm(c_ctx)], axis=1))
        tabs = [tab_int,
                _bias_table(rpb, R0, 0) if r == 0 else tab_int,
                _bias_table(rpb, R0, 1) if r == 0 else tab_int,
                _bias_table(rpb, R0 + 24, 2) if r == 3 else tab_int,
                _bias_table(rpb, R0 + 24, 3) if r == 3 else tab_int]
        btab = np.ascontiguousarray(np.stack(tabs, axis=0))
        t = np.arange(TOK)
        rowp = (R0 + t // 64).astype(np.float32)
        colp = (t % 64).astype(np.float32)
        dd = np.arange(128) % 64
        fr = freqs[dd % 16][:, None]
        pos = np.where((dd < 32)[:, None], rowp[None, :], colp[None, :]).astype(np.float32)
        ang = (pos * fr).astype(np.float32)
        ropeC = np.ones((128, TOK + CTX), np.float32)
        ropeS = np.zeros((128, TOK + CTX), np.float32)
        ropeC[:, :TOK] = np.cos(ang)
        sgn = np.where((dd % 32) < 16, -1.0, 1.0).astype(np.float32)[:, None]
        ropeS[:, :TOK] = np.sin(ang) * sgn
        cmask = np.zeros((128, 8), np.float32)
        for p in range(4):
            cmask[:, p] = 1.0 if p < r else 0.0
            cmask[:, 4 + p] = 1.0 if p > r else 0.0
        m = dict(shared)
        m.update({"xa": np.ascontiguousarray(xa), "xh": xh, "xc": np.ascontiguousarray(ctx[bb]), "cvec": cvec,
                  "btab": btab, "ropeC": ropeC, "ropeS": ropeS, "cmask": cmask})
        maps.append(m)
    return maps


IN_SHAPES = {
    "xa": [TOK, D], "xh": [NSEG * 1024, D], "xc": [CTX, D], "cvec": [128, 16],
    "w_mod": [D, 6 * D], "b_modT": [128, 48], "n1g": [128, 8], "n2g": [128, 8],
    "wq_na": [D, 512], "wk_na": [D, 512], "wv_na": [D, 512], "wg1": [D, 1024], "wv_g": [D, 512],
    "w_og": [D, 512], "w_lr": [D, 32], "w_ma": [D, D], "w_mb": [D, D],
    "walpha": [32, 512], "balpha": [128, 4], "pvec": [128, 3],
    "w_br_na": [512, D], "w_br_gla": [512, D], "w_out": [D, D],
    "w_gate": [D, DFF], "w_up": [D, DFF], "w_down": [DFF, D],
    "consts": [128, 512], "btab": [5, 128, 5120], "ropeC": [128, TOK + CTX], "ropeS": [128, TOK + CTX],
    "cmask": [128, 8],
}


def build(stop=None, taps=None):
    nc = bass.Bass("TRN2", target_bir_lowering=False)
    I = {k: nc.dram_tensor(k, shp, F32, kind="ExternalInput").ap() for k, shp in IN_SHAPES.items()}
    out = nc.dram_tensor("out", [TOK, D], F32, kind="ExternalOutput").ap()
    cc_in = nc.dram_tensor("cc_in", [128, 520], F32)
    cc_out = nc.dram_tensor("cc_out", [512, 520], F32)
    taps = taps if taps is not None else {}
    tapd = []
    st = ExitStack()
    with st:
        S = Sched(nc)

        uniq = [0]

        def sb(name, F, dt, stack=None):
            uniq[0] += 1
            h = (stack or st).enter_context(nc.sbuf_tensor("s%d_%s" % (uniq[0], name), [128, F], dt))
            return T(h, F)

        def DMA(eng, o, i, reads, writes, dep, n=1, ab=False, nobar=False):
            S.add(eng, lambda e, s: e.dma_start(out=o, in_=i).then_inc(s, 16), reads=reads, writes=writes, dma=dep,
                  after_barrier=ab, nobar=nobar)

        def PE(mms, reads, writes):
            def fn(e):
                r = None
                for m in mms:
                    if m[0] == "T":
                        r = e.transpose(m[1], m[2], m[3])
                    else:
                        r = e.matmul(m[0], lhsT=m[1], rhs=m[2], start=m[3], stop=m[4])
                return r
            S.add("pe", fn, reads, writes)

        def ACT(o, i, func, reads, writes, scale=None, bias=None, accum=None):
            kw = {}
            if scale is not None:
                kw["scale"] = scale
            if bias is not None:
                kw["bias"] = bias
            if accum is not None:
                kw["accum_out"] = accum
            S.add("act", lambda e: e.activation(out=o, in_=i, func=func, **kw), reads, writes)

        def TT(o, a, b, op, reads, writes, eng="dve"):
            S.add(eng, lambda e: e.tensor_tensor(out=o, in0=a, in1=b, op=op), reads, writes)

        def TS(o, a, s1, op0, reads, writes, s2=None, op1=None, eng="dve"):
            if op1 is None:
                S.add(eng, lambda e: e.tensor_scalar(out=o, in0=a, scalar1=s1, scalar2=None, op0=op0), reads, writes)
            else:
                S.add(eng, lambda e: e.tensor_scalar(out=o, in0=a, scalar1=s1, scalar2=s2, op0=op0, op1=op1), reads, writes)

        def STT(o, a, sc, b, op0, op1, reads, writes):
            S.add("dve", lambda e: e.scalar_tensor_tensor(out=o, in0=a, scalar=sc, in1=b, op0=op0, op1=op1), reads, writes)

        def CP(o, i, reads, writes, eng="dve"):
            if eng == "act":
                S.add("act", lambda e: e.copy(out=o, in_=i), reads, writes)
            else:
                S.add(eng, lambda e: e.tensor_copy(out=o, in_=i), reads, writes)

        def MEMSET(o, val, writes, eng="dve"):
            S.add(eng, lambda e: e.memset(o, val), (), writes)

        def tap(name, t, F, dt, deps):
            if name not in taps:
                return
            d = nc.dram_tensor("tap_" + name, [128, F], dt, kind="ExternalOutput").ap()
            dep = Dep("tap_" + name)
            DMA("sp", d, t, deps, [dep], dep)
            tapd.append(dep)

        P = [T(st.enter_context(nc.psum_tensor("ps%d" % i, [128, 512], F32)), 512) for i in range(7)]
        DP = [Dep("ps%d" % i) for i in range(7)]
        PT = T(st.enter_context(nc.psum_tensor("pst", [128, 1024], BF16)), 1024)
        DPT = Dep("pst")

        cst = sb("cst", 512, F32)
        cstb = sb("cstb", 512, BF16)
        Dcst, Dcstb = Dep("cst"), Dep("cstb")
        DMA("sp", cst.s(0, 512), I["consts"], [], [Dcst], Dcst)
        DMA("pool", cstb.s(0, 512), I["consts"], [], [Dcstb], Dcstb)
        identb = cstb.s(0, 128)
        blk32 = cst.s(128, 256)
        blkb = cstb.s(128, 256)
        ident32 = cst.s(0, 128)
        mfb, mbb = cstb.s(256, 384), cstb.s(384, 512)
        kc = sb("kc", 8, F32)
        Dkc = Dep("kc")
        MEMSET(kc.s(0, 1), EPS, [Dkc])
        MEMSET(kc.s(1, 2), 1.0, [Dkc])
        MEMSET(kc.s(2, 3), 0.0, [Dkc])
        ones32 = sb("ones32", 128, F32)
        Dones = Dep("ones")
        MEMSET(ones32.s(0, 128), 1.0, [Dones])
        onesb = sb("onesb", 128, BF16)
        MEMSET(onesb.s(0, 128), 1.0, [Dones])
        onesw = sb("onesw", 512, F32)
        MEMSET(onesw.s(0, 512), 1.0, [Dones])

        prm = sb("prm", 128, F32)
        Dprm = Dep("prm")
        for (c0, key, n) in ((0, "cvec", 16), (16, "b_modT", 48), (64, "n1g", 8), (72, "n2g", 8), (80, "balpha", 4),
                             (84, "pvec", 3), (88, "cmask", 8)):
            DMA("sp", prm.s(c0, c0 + n), I[key], [], [Dprm], Dprm)
        walb = sb("walb", 512, BF16)
        Dwal = Dep("walb")
        DMA("pool", walb.s(0, 512, n=32), I["walpha"], [], [Dwal], Dwal)
        gq, gk, gn = prm.s(84, 85), prm.s(85, 86), prm.s(86, 87)

        NWS = 3
        WS = [sb("ws%d" % i, 4096, BF16) for i in range(NWS)]
        WD = [Dep("ws%d" % i) for i in range(NWS)]
        wctr = [0]

        STG_SPEC = [("w_modl", "w_mod", 1024, 2048, 4096, 128)]
        STG = {}
        stg_chunks = []
        for (sname, src_name, rows, c0_, ncols, rpc) in STG_SPEC:
            t_ = nc.dram_tensor("stg_" + sname, [rows, ncols], BF16).ap()
            deps_ = []
            for r0_ in range(0, rows, rpc):
                r1_ = min(rows, r0_ + rpc)
                d_ = Dep("stg_%s_%d" % (sname, r0_))
                deps_.append((r0_, r1_, d_))
                stg_chunks.append((t_[r0_:r1_, :], I[src_name][r0_:r1_, c0_:c0_ + ncols], d_))
            STG[sname] = (t_, deps_)
        use_stg = [False]

        def stager():
            for (o_, i_, d_) in stg_chunks:
                DMA("pool", o_, i_, [], [d_], d_, nobar=True)
                for _ in range(STG_GAP):
                    yield

        def loadw(name, r0, kt, c0, n):
            i = wctr[0] % len(WS)
            wctr[0] += 1
            reads = []
            if use_stg[0] and name in STG:
                t_, deps_ = STG[name]
                src = t_[r0:r0 + 128 * kt, c0:c0 + n].rearrange("(k p) n -> p k n", p=128)
                reads = [d_ for (a_, b_, d_) in deps_ if a_ < r0 + 128 * kt and b_ > r0]
            else:
                src = I[name][r0:r0 + 128 * kt, c0:c0 + n].rearrange("(k p) n -> p k n", p=128)
            DMA("pool", WS[i].v(0, [[n, kt], [1, n]]), src, reads, [WD[i]], WD[i])
            return WS[i], WD[i], n

        scb = sb("scb", 16, BF16)
        Dscb = Dep("scb")
        ACT(scb.v(0, [[2, 8]]), prm.s(0, 8), AF.Silu, [Dprm], [Dscb])
        ACT(scb.v(1, [[2, 8]]), prm.s(8, 16), AF.Silu, [Dprm], [Dscb])
        modv = sb("modv", 96, F32)
        Dmod = Dep("modv")
        mv = sb("mv", 64, F32)
        Dmv = Dep("mv")
        Dmv2 = Dep("mv2")
        mcol = lambda ch0, j: modv.v(2 * ch0 + j, [[2, 8]])
        gbc = sb("gbc", 2048, F32)
        Dgbc = Dep("gbc")
        dg = sb("dg", 128, F32)
        Ddg = Dep("dg")

        def mod_groups(g0, g1_, dmod):
            for g in range(g0, g1_):
                if use_stg[0] and g >= 4:
                    ws, wd, n = loadw("w_modl", 0, 8, (g - 4) * 512, 512)
                else:
                    ws, wd, n = loadw("w_mod", 0, 8, g * 512, 512)
                mms = []
                for jj in range(4):
                    ch = g * 4 + jj
                    for kt in range(8):
                        mms.append((P[0].s(2 * ch, 2 * ch + 2), ws.v(kt * 512 + jj * 128, [[1, 128]]), scb.s(2 * kt, 2 * kt + 2),
                                    kt == 0, kt == 7))
                PE(mms, [wd, Dscb], [DP[0]])
            c0, c1 = g0 * 4, g1_ * 4
            TT(modv.v(2 * c0, [[2, c1 - c0], [1, 2]]), P[0].v(2 * c0, [[2, c1 - c0], [1, 2]]), prm.v(16 + c0, [[1, c1 - c0], [0, 2]]),
               ALU.add, [DP[0], Dprm], [dmod])

        def mod_early():
            mod_groups(0, 4, Dmod)
            STT(mv.s(0, 8), mcol(8, 0), 1.0, prm.s(64, 72), ALU.add, ALU.mult, [Dmod, Dprm], [Dmv])
            CP(mv.s(8, 16), mcol(0, 0), [Dmod], [Dmv])
            STT(mv.s(16, 24), mcol(8, 1), 1.0, prm.s(64, 72), ALU.add, ALU.mult, [Dmod, Dprm], [Dmv])
            CP(mv.s(24, 32), mcol(0, 1), [Dmod], [Dmv])

        def mod_late():
            Dm2 = Dep("modv2")
            mod_groups(4, 12, Dm2)
            STT(mv.s(32, 40), mcol(32, 0), 1.0, prm.s(72, 80), ALU.add, ALU.mult, [Dm2, Dprm], [Dmv2])
            CP(mv.s(40, 48), mcol(24, 0), [Dm2], [Dmv2])
            CP(mv.s(48, 56), mcol(16, 0), [Dm2], [Dmv2])
            CP(mv.s(56, 64), mcol(40, 0), [Dm2], [Dmv2])
            for gi in range(2):
                dgt, Ddgt = xt[gi], Dxt[gi]
                TT(dgt.v(0, [[128, 8], [1, 128]]), cst.v(0, [[0, 8], [1, 128]]), mv.v(48 + 8 * gi, [[1, 8], [0, 128]]), ALU.mult,
                   [Dcst, Dmv2], [Ddgt])
                for hf in range(2):
                    b_ = 1 + hf
                    PE([(P[b_].s(0, 512), ones32.s(0, 128), dgt.s(hf * 512, hf * 512 + 512), True, True)], [Ddgt, Dones], [DP[b_]])
                    CP(gbc.s(gi * 1024 + hf * 512, gi * 1024 + hf * 512 + 512), P[b_].s(0, 512), [DP[b_]], [Dgbc],
                       eng="act" if hf == 0 else "dve")
            tap("mv", mv.s(0, 64), 64, F32, [Dmv, Dmv2])
            tap("gbc", gbc.s(0, 2048), 2048, F32, [Dgbc])

        mod_early()
        negb = sb("negb", 4, F32)
        Dnegb = Dep("negb")
        TS(negb.s(0, 4), prm.s(80, 84), -1.0, ALU.mult, [Dprm], [Dnegb])

        xt = [sb("xt%d" % i, 1024, F32) for i in range(2)]
        Dxt = [Dep("xt%d" % i) for i in range(2)]
        xs = [sb("xs%d" % i, 1024, BF16) for i in range(2)]
        Dxs = [Dep("xs%d" % i) for i in range(2)]
        nst = [sb("nst%d" % i, 4, F32) for i in range(2)]
        Dnst = [Dep("nst%d" % i) for i in range(2)]

        PTv = [PT, T(P[6].h.bitcast(BF16), 1024)]
        DPTv = [DPT, DP[6]]

        def interleave(*gens):
            gens = list(gens)
            while gens:
                for g_ in list(gens):
                    try:
                        next(g_)
                    except StopIteration:
                        gens.remove(g_)

        import os

        def norm_T_g(x_t, x_dep, gcol, shcol, dst, dst_deps, dcol, slot, dmv):
            xap = x_t.s(0, 1024) if isinstance(x_t, T) else x_t
            i = slot
            ns, dn = nst[i], Dnst[i]
            pt, dpt = PTv[i], DPTv[i]
            ACT(xs[i].s(0, 1024), xap, AF.Square, [x_dep], [dn, Dxs[i]], accum=ns.s(0, 1))
            yield
            ACT(ns.s(1, 2), ns.s(0, 1), AF.Ln, [dn, Dkc], [dn], scale=1.0 / 1024, bias=kc.s(0, 1))
            yield
            ACT(ns.s(2, 3), ns.s(1, 2), AF.Exp, [dn], [dn], scale=-0.5)
            yield
            TS(xs[i].s(0, 1024), xap, ns.s(2, 3), ALU.mult, [x_dep, dn], [Dxs[i]])
            yield
            PE([("T", pt.s(kt * 128, kt * 128 + 128), xs[i].s(kt * 128, kt * 128 + 128), identb) for kt in range(8)],
               [Dxs[i], Dcstb], [dpt])
            yield
            for kt in range(8):
                o = dst.s(kt * dst.F // 8 + dcol, kt * dst.F // 8 + dcol + 128)
                if slot == 0:
                    ACT(o, pt.s(kt * 128, kt * 128 + 128), AF.Identity, [dpt, dmv], [dst_deps[kt]],
                        scale=mv.s(gcol + kt, gcol + kt + 1), bias=mv.s(shcol + kt, shcol + kt + 1))
                else:
                    TS(o, pt.s(kt * 128, kt * 128 + 128), mv.s(gcol + kt, gcol + kt + 1), ALU.mult, [dpt, dmv], [dst_deps[kt]],
                       s2=mv.s(shcol + kt, shcol + kt + 1), op1=ALU.add)
                if kt % 2 == 1:
                    yield

        def norm_tiles(srcs, gcol, shcol, dst, dst_deps, dmv):
            import os
            STEP = 1 if os.environ.get("NOIL") else 2
            for j in range(0, len(srcs), STEP):
                gs = []
                for slot, (xa_, xd_, dc_, ti_) in enumerate(srcs[j:j + STEP]):
                    gs.append(norm_T_g(xa_, xd_, gcol, shcol, dst, dst_deps[ti_ * 8:(ti_ + 1) * 8], dc_, slot, dmv))
                interleave(*gs)

        xctr = [0]

        def load_x(src_rows):
            i = xctr[0] % 2
            xctr[0] += 1
            DMA("sp", xt[i].s(0, 1024), src_rows, [], [Dxt[i]], Dxt[i])
            return xt[i], Dxt[i]

        hTc = sb("hTc", 8 * CTX, BF16)
        DhTc = [Dep("hTc%d" % i_) for i_ in range(16)]
        hT = sb("hT", 8 * 1024, BF16)
        DhT = [Dep("hT%d" % i_) for i_ in range(64)]
        hT2 = sb("hT2", 8 * 1024, BF16)
        DhT2 = [Dep("hT2_%d" % i_) for i_ in range(64)]
        hTb, DhTb = [hT, hT2], [DhT, DhT2]
        o_loc = sb("o_loc", 4 * TOK, BF16)
        Doloc = [Dep("oloc%d" % a) for a in range(NSEG)]
        qsd = sb("qsd", 4 * TOK, BF16)
        Dqsd = [Dep("qsd%d" % a) for a in range(NSEG)]
        NSUM = NSEG + 1
        Sentbf = sb("Sentbf", NSEG * 2 * 256, BF16)
        DSent = Dep("Sentbf")
        xsc = ExitStack()
        Useg = sb("Useg", NSUM * 2 * 256, F32, xsc)
        Aseg = sb("Aseg", NSUM * 2 * 2, F32, xsc)
        Dsum = [Dep("sum%d" % a) for a in range(NSUM)]

        srcs = []
        for tci in range(2):
            x_t, x_d = load_x(I["xc"][tci * 128:(tci + 1) * 128, :])
            srcs.append((x_t, x_d, tci * 128, tci))
        norm_tiles(srcs, 16, 24, hTc, DhTc, Dmv)
        tap("hTc", hTc.s(0, 8 * CTX), 8 * CTX, BF16, DhTc)
        if stop == "mod":
            S.add("sp", None, reads=tapd)
            S.emit(st)
            return nc

        pa = ExitStack()
        ropeC = sb("ropeC", SA, F32, pa)
        ropeS = sb("ropeS", SA, F32, pa)
        DropeC, DropeS = Dep("ropeC"), Dep("ropeS")
        qr = sb("qr", SA, F32, pa)
        kr = sb("kr", SA, F32, pa)
        Dqr, Dkr = Dep("qr"), Dep("kr")
        tmpf = sb("tmpf", SA, F32, pa)
        Dtmpf = Dep("tmpf")
        vg = sb("vg", 2 * 4 * 512, BF16, pa)
        Dvg = [Dep("vg0"), Dep("vg1")]
        zl = sb("zl", SA, BF16, pa)
        Dzl = Dep("zl")
        la = [sb("la%d" % i_, SA, F32, pa) for i_ in range(2)]
        Dla = [Dep("la0"), Dep("la1")]
        pbuf = [sb("pbuf%d" % i_, SA + 1, F32, pa) for i_ in range(2)]
        Dpb = [Dep("pbuf0"), Dep("pbuf1")]
        for i_ in range(2):
            MEMSET(pbuf[i_].s(0, 1), 0.0, [Dpb[i_]])
        bt = [sb("bt%d" % i_, SA, F32, pa) for i_ in range(2)]
        Dbt = [Dep("bt0"), Dep("bt1")]
        E2 = [sb("E2_%d" % i_, SA, BF16, pa) for i_ in range(2)]
        DE2 = [Dep(), Dep()]
        E1, DE1 = la, Dla
        E3, DE3 = bt, Dbt
        qdec = sb("qdec", 2 * 4 * SA, BF16, pa)
        kinv = sb("kinv", 2 * 4 * SA, BF16, pa)
        Dqdec = [[Dep("qdec%d" % i) for i in range(4)] for _ in range(2)]
        Dkinv = [[Dep("kinv%d" % i) for i in range(4)] for _ in range(2)]
        kend = [sb("kend%d" % i_, SA, BF16, pa) for i_ in range(2)]
        Dkend = [Dep("kend0"), Dep("kend1")]
        kendtm = sb("kendtm", 2 * 4 * 512, BF16, pa)
        Dkendtm = [[Dep("kendtm%d" % i_) for i_ in range(4)] for _ in range(2)]
        decs = sb("decs", 2 * 4 * 8, F32, pa)
        Ddecs = [[Dep("decs%d" % i_) for i_ in range(4)] for _ in range(2)]
        dtmp = [sb("dtmp%d" % i_, 8, F32, pa) for i_ in range(2)]
        Ddtmp = [Dep("dtmp0"), Dep("dtmp1")]
        Sst = sb("Sst", 2 * 256, F32, pa)
        DSst = [Dep("Sst0"), Dep("Sst1")]
        Stmp = sb("Stmp", 2 * 256, F32, pa)
        DStmp = [Dep("Stmp0"), Dep("Stmp1")]
        Sbf = sb("Sbf", 2 * 8 * 256, BF16, pa)
        DSbf = Dep("Sbf")
        Abf = [sb("Abf%d" % i, 256, BF16, pa) for i in range(2)]
        DAbf = [Dep("Abf0"), Dep("Abf1")]
        onesA = onesw

        def roundrobin(gens):
            gens = list(gens)
            while gens:
                for g_ in list(gens):
                    try:
                        next(g_)
                        yield
                    except StopIteration:
                        gens.remove(g_)

        def pa_front(N, hsrc, hcol0, hdep, hstride, tok0, sidx, need_q, pp):
            NCH = N // 64
            NT = N // 128
            hk = lambda kt: hsrc.s(kt * hstride + hcol0, kt * hstride + hcol0 + N)
            DMA("sp", ropeC.s(0, N), I["ropeC"][:, tok0:tok0 + N], [], [DropeC], DropeC)
            DMA("sp", ropeS.s(0, N), I["ropeS"][:, tok0:tok0 + N], [], [DropeS], DropeS)
            ws, wd, n = loadw("wv_g", 0, 8, 0, 512)
            for tt in range(NT):
                b = 5 + (tt % 2)
                PE([(P[b].s(0, 512), hsrc.s(kt * hstride + hcol0 + tt * 128, kt * hstride + hcol0 + tt * 128 + 128),
                     ws.s(kt * 512, kt * 512 + 512), kt == 0, kt == 7) for kt in range(8)], [wd] + hdep, [DP[b]])
                yield
                CP(vg.s(pp * 2048 + tt * 512, pp * 2048 + tt * 512 + 512), P[b].s(0, 512), [DP[b]], [Dvg[pp]], eng="act")
                yield
            ws, wd, n = loadw("w_lr", 0, 8, 0, 32)
            PE([(P[5].s(0, N, n=32), ws.v(kt * 32, [[1, 32]]), hk(kt), kt == 0, kt == 7) for kt in range(8)],
               [wd] + hdep, [DP[5]])
            yield
            CP(zl.s(0, N, n=32), P[5].s(0, N, n=32), [DP[5]], [Dzl], eng="act")
            yield
            wsg, wdg, n = loadw("wg1", 0, 8, 0, 512)
            wsk, wdk, n = loadw("wg1", 0, 8, 512, 512)

            def chain(hp, dr):
                dh = dr * 2 + hp
                la_, pb_, bt_, e1, e2, e3, dt_, ke_ = la[dr], pbuf[dr], bt[dr], E1[dr], E2[dr], E3[dr], dtmp[dr], kend[dr]
                zb = 5 + dr
                PE([(P[zb].s(0, N), walb.v(dh * 128, [[1, 128]], n=32), zl.s(0, N, n=32), True, True)],
                   [Dwal, Dzl], [DP[zb]])
                yield
                ACT(la_.s(0, N), P[zb].s(0, N), AF.Exp, [DP[zb], Dnegb], [Dla[dr]], scale=-1.0, bias=negb.s(dh, dh + 1))
                yield
                ACT(la_.s(0, N), la_.s(0, N), AF.Ln, [Dkc], [Dla[dr]], bias=kc.s(1, 2))
                yield
                TS(la_.s(0, N), la_.s(0, N), -1.0 / 16.0, ALU.mult, [], [Dla[dr]], s2=-1.0, op1=ALU.max)
                yield
                S.add("dve", lambda e: e.tensor_tensor_scan(out=pb_.s(1, N + 1), data0=onesA.s(0, N), data1=la_.s(0, N),
                                                            initial=0.0, op0=ALU.mult, op1=ALU.add),
                      [Dla[dr], Dones], [Dpb[dr]])
                yield
                if dr == 0:
                    TT(bt_.v(0, [[64, NCH], [1, 64]]), pb_.v(1, [[64, NCH], [1, 64]]), pb_.v(0, [[64, NCH], [0, 64]]),
                       ALU.subtract, [Dpb[dr]], [Dbt[dr]])
                else:
                    TT(bt_.v(0, [[64, NCH], [1, 64]]), pb_.v(64, [[64, NCH], [0, 64]]), pb_.v(0, [[64, NCH], [1, 64]]),
                       ALU.subtract, [Dpb[dr]], [Dbt[dr]])
                yield
                if hp == 0 and sidx == 0 and dr == 0:
                    tap("bt", bt_.s(0, N), N, F32, [Dbt[dr]])
                ACT(e1.s(0, N), bt_.s(0, N), AF.Exp, [Dbt[dr], Dla[dr]], [DE1[dr]])
                yield
                ACT(e2.s(0, N), bt_.s(0, N), AF.Exp, [Dbt[dr]], [DE2[dr]], scale=-1.0)
                yield
                if dr == 0:
                    ACT(e3.s(0, N), pb_.s(1, N + 1), AF.Exp, [Dpb[dr]], [DE3[dr]])
                else:
                    ACT(e3.s(0, N), pb_.s(0, N), AF.Exp, [Dpb[dr]], [DE3[dr]], scale=-1.0, bias=pb_.s(N, N + 1))
                yield
                if need_q:
                    STT(qdec.s((pp * 4 + dh) * SA, (pp * 4 + dh) * SA + N), qr.s(0, N), 0.125, e1.s(0, N), ALU.mult, ALU.mult, [Dqr, DE1[dr]],
                        [Dqdec[pp][dh]])
                    yield
                    STT(qsd.s(dh * TOK + tok0, dh * TOK + tok0 + N), qr.s(0, N), 0.125, e3.s(0, N), ALU.mult, ALU.mult,
                        [Dqr, DE3[dr]], [Dqsd[sidx]])
                    yield
                TT(kinv.s((pp * 4 + dh) * SA, (pp * 4 + dh) * SA + N), kr.s(0, N), e2.s(0, N), ALU.mult, [Dkr, DE2[dr]], [Dkinv[pp][dh]])
                yield
                TT(dt_.s(0, NCH), pb_.v(64, [[64, NCH]]), pb_.v(0, [[64, NCH]]), ALU.subtract, [Dpb[dr]], [Ddtmp[dr]])
                yield
                ACT(decs.s(pp * 32 + dh * 8, pp * 32 + dh * 8 + NCH), dt_.s(0, NCH), AF.Exp, [Ddtmp[dr]], [Ddecs[pp][dh]])
                yield
                ACT(Aseg.s((sidx * 2 + dr) * 2 + hp, (sidx * 2 + dr) * 2 + hp + 1), pb_.s(N, N + 1), AF.Exp, [Dpb[dr]],
                    [Dsum[sidx]])
                yield
                TT(ke_.v(0, [[64, NCH], [1, 64]]), kinv.v((pp * 4 + dh) * SA, [[64, NCH], [1, 64]]),
                   decs.v(pp * 32 + dh * 8, [[1, NCH], [0, 64]]), ALU.mult, [Dkinv[pp][dh], Ddecs[pp][dh]], [Dkend[dr]])
                yield
                pt, dpt = T(P[zb].h.bitcast(BF16), 1024), DP[zb]
                PE([("T", pt.s(tt * 128, tt * 128 + 128), ke_.s(tt * 128, tt * 128 + 128), identb) for tt in range(NT)],
                   [Dkend[dr], Dcstb], [dpt])
                yield
                CP(kendtm.v(pp * 2048 + dh * 128, [[512, NT], [1, 128]]), pt.v(0, [[128, NT], [1, 128]]), [dpt], [Dkendtm[pp][dh]],
                   eng="act" if dr == 0 else "dve")
                yield

            for hp in range(2):
                Cs = ropeC.s(0, N)
                Ss = ropeS.s(0, N)
                for (wsx, wdx, dst, ddst, bk, need) in ((wsg, wdg, qr, Dqr, 3, need_q), (wsk, wdk, kr, Dkr, 3, True)):
                    if not need:
                        continue
                    PE([(P[bk].s(0, N), wsx.v(kt * 512 + hp * 128, [[1, 128]]), hk(kt), kt == 0, kt == 7) for kt in range(8)],
                       [wdx] + hdep, [DP[bk]])
                    PE([(P[bk + 1].s(0, N), wsx.v(kt * 512 + 256 + hp * 128, [[1, 128]]), hk(kt), kt == 0, kt == 7)
                        for kt in range(8)], [wdx] + hdep, [DP[bk + 1]])
                    yield
                    TT(tmpf.s(0, N), P[bk].s(0, N), Cs, ALU.mult, [DP[bk], DropeC], [Dtmpf])
                    yield
                    TT(dst.s(0, N), P[bk + 1].s(0, N), Ss, ALU.mult, [DP[bk + 1], DropeS], [ddst])
                    yield
                    TT(dst.s(0, N), dst.s(0, N), tmpf.s(0, N), ALU.add, [Dtmpf], [ddst])
                    yield
                if hp == 0 and sidx == 0:
                    tap("qr", qr.s(0, N), N, F32, [Dqr])
                    tap("kr", kr.s(0, N), N, F32, [Dkr])
                for _ in roundrobin([chain(hp, 0), chain(hp, 1)]):
                    yield

        def pa_back(N, tok0, sidx, need_q, pp):
            NCH = N // 64
            NT = N // 128
            for dr in range(2):
                MEMSET(Sst.s(dr * 256, dr * 256 + 256), 0.0, [DSst[dr]])
            for nstep in range(NCH):
                for dr in range(2):
                    cn = nstep if dr == 0 else NCH - 1 - nstep
                    tt, half = divmod(cn, 2)
                    bk = 0
                    mms = []
                    for hp in range(2):
                        for hh in range(2):
                            h = 2 * hp + hh
                            mms.append((P[bk].s(dr * 256 + hp * 128, dr * 256 + hp * 128 + 128, p0=64 * hh, n=64),
                                        kendtm.s(pp * 2048 + tt * 512 + (dr * 2 + hp) * 128 + 64 * hh, pp * 2048 + tt * 512 + (dr * 2 + hp) * 128 + 64 * hh + 64,
                                                 p0=64 * half, n=64),
                                        vg.s(pp * 2048 + tt * 512 + h * 128, pp * 2048 + tt * 512 + h * 128 + 128, p0=64 * half, n=64), True, True))
                    PE(mms, Dkendtm[pp] + [Dvg[pp]], [DP[bk]])
                    yield
                    TT(Stmp.v(dr * 256, [[128, 2], [1, 128]]), Sst.v(dr * 256, [[128, 2], [1, 128]]),
                       decs.v(pp * 32 + dr * 16 + cn, [[8, 2], [0, 128]]), ALU.mult, [DSst[dr]] + Ddecs[pp], [DStmp[dr]])
                    TT(Sst.s(dr * 256, dr * 256 + 256), Stmp.s(dr * 256, dr * 256 + 256), P[bk].s(dr * 256, dr * 256 + 256),
                       ALU.add, [DStmp[dr], DP[bk]], [DSst[dr]])
                    yield
                    nxt = cn + 1 if dr == 0 else cn - 1
                    if 0 <= nxt < NCH and need_q:
                        CP(Sbf.s((dr * 8 + nxt) * 256, (dr * 8 + nxt) * 256 + 256), Sst.s(dr * 256, dr * 256 + 256),
                           [DSst[dr]], [DSbf], eng="act")
            for dr in range(2):
                CP(Useg.s((sidx * 2 + dr) * 256, (sidx * 2 + dr) * 256 + 256), Sst.s(dr * 256, dr * 256 + 256), [DSst[dr]],
                   [Dsum[sidx]])
            if not need_q:
                return
            for h in range(4):
                hp, hh = divmod(h, 2)
                ob = 2
                for tt in range(NT):
                    cs = slice(tt * 128, tt * 128 + 128)
                    ab = 1
                    ao = (tt % 2) * 256
                    mms = []
                    for dr in range(2):
                        dh = dr * 2 + hp
                        mms.append((P[ab].s(ao + dr * 128, ao + dr * 128 + 128),
                                    kinv.s((pp * 4 + dh) * SA + tt * 128, (pp * 4 + dh) * SA + tt * 128 + 128, p0=64 * hh, n=64),
                                    qdec.s((pp * 4 + dh) * SA + tt * 128, (pp * 4 + dh) * SA + tt * 128 + 128, p0=64 * hh, n=64), True, True))
                    PE(mms, [Dkinv[pp][hp], Dkinv[pp][2 + hp], Dqdec[pp][hp], Dqdec[pp][2 + hp]], [DP[ab]])
                    yield
                    ai = tt % 2
                    TT(Abf[ai].s(0, 128), P[ab].s(ao, ao + 128), mfb, ALU.mult, [DP[ab], Dcstb], [DAbf[ai]])
                    TT(Abf[ai].s(128, 256), P[ab].s(ao + 128, ao + 256), mbb, ALU.mult, [DP[ab], Dcstb], [DAbf[ai]])
                    mms = [(P[ob].s(tt * 128, tt * 128 + 128), vg.s(pp * 2048 + tt * 512 + h * 128, pp * 2048 + tt * 512 + h * 128 + 128),
                            Abf[ai].s(0, 128), True, False),
                           (P[ob].s(tt * 128, tt * 128 + 128), vg.s(pp * 2048 + tt * 512 + h * 128, pp * 2048 + tt * 512 + h * 128 + 128),
                            Abf[ai].s(128, 256), False, False)]
                    for cn in (2 * tt, 2 * tt + 1):
                        for dr in range(2):
                            if (dr == 0 and cn == 0) or (dr == 1 and cn == NCH - 1):
                                continue
                            dh = dr * 2 + hp
                            mms.append((P[ob].s(cn * 64, cn * 64 + 64),
                                        Sbf.s((dr * 8 + cn) * 256 + hp * 128, (dr * 8 + cn) * 256 + hp * 128 + 128, p0=64 * hh, n=64),
                                        qdec.s((pp * 4 + dh) * SA + cn * 64, (pp * 4 + dh) * SA + cn * 64 + 64, p0=64 * hh, n=64), False, False))
                    mms[-1] = mms[-1][:4] + (True,)
                    PE(mms, [Dvg[pp], DAbf[ai], DSbf, Dqdec[pp][hp], Dqdec[pp][2 + hp]], [DP[ob]])
                    yield
                CP(o_loc.s(h * TOK + tok0, h * TOK + tok0 + N), P[ob].s(0, N), [DP[ob]], [Doloc[sidx]], eng="act")

        def drain(g_):
            for _ in g_:
                pass

        def prep_a(a):
            hb = a % 2
            for t2 in range(SA // 128):
                x_t, x_d = load_x(I["xa"][a * SA + t2 * 128: a * SA + t2 * 128 + 128, :])
                ti_ = hb * 4 + t2
                for _ in norm_T_g(x_t, x_d, 0, 8, hT, DhT[ti_ * 8:(ti_ + 1) * 8], hb * 512 + t2 * 128, 0, Dmv):
                    yield

        drain(roundrobin([pa_front(CTX, hTc, 0, DhTc, CTX, TOK, NSEG, False, 0), prep_a(0)]))

        def fr(a):
            hb = a % 2
            return pa_front(SA, hT, hb * 512, DhT[hb * 32:hb * 32 + 32], 1024, a * SA, a, True, (a + 1) % 2)

        bg = stager()

        def drain_bg(gens):
            for _ in roundrobin(gens):
                try:
                    next(bg)
                except StopIteration:
                    pass

        gens = [pa_back(CTX, TOK, NSEG, False, 0), fr(0)]
        if NSEG > 1:
            gens.append(prep_a(1))
        drain_bg(gens)
        for a in range(NSEG):
            gens = [pa_back(SA, a * SA, a, True, (a + 1) % 2)]
            if a + 1 < NSEG:
                gens.append(fr(a + 1))
            if a + 2 < NSEG:
                gens.append(prep_a(a + 2))
            drain_bg(gens)
        drain(bg)
        use_stg[0] = True
        tap("oloc", o_loc.s(0, 4 * TOK), 4 * TOK, BF16, Doloc)
        tap("qsd", qsd.s(0, 4 * TOK), 4 * TOK, BF16, Dqsd)
        tap("Useg", Useg.s(0, NSUM * 512), NSUM * 512, F32, Dsum)
        tap("Aseg", Aseg.s(0, NSUM * 4), NSUM * 4, F32, Dsum)
        S.barrier()
        pa.close()
        if stop in ("a0", "a"):
            xsc.close()
            S.add("sp", None, reads=tapd)
            S.emit(st)
            return nc

        snd = sb("snd", 520, F32, xsc)
        gath = sb("gath", 4 * 520, F32, xsc)
        Dsnd, Dgath = Dep("snd"), Dep("gath")
        Dccin, Dccout = Dep("ccin"), Dep("ccout")
        U_ = lambda a, d: Useg.s((a * 2 + d) * 256, (a * 2 + d) * 256 + 256)
        U3 = lambda a, d: Useg.v((a * 2 + d) * 256, [[128, 2], [1, 128]])
        Ab = lambda a, d: Aseg.v((a * 2 + d) * 2, [[1, 2], [0, 128]])
        A2 = lambda a, d: Aseg.s((a * 2 + d) * 2, (a * 2 + d) * 2 + 2)
        MEMSET(snd.s(512, 520), 0.0, [Dsnd])
        CP(snd.s(0, 256), U_(0, 0), Dsum, [Dsnd])
        CP(snd.s(512, 514), A2(0, 0), Dsum, [Dsnd])
        for a in range(1, NSEG):
            TT(snd.v(0, [[128, 2], [1, 128]]), snd.v(0, [[128, 2], [1, 128]]), Ab(a, 0), ALU.mult, Dsum, [Dsnd])
            TT(snd.s(0, 256), snd.s(0, 256), U_(a, 0), ALU.add, Dsum, [Dsnd])
            TT(snd.s(512, 514), snd.s(512, 514), A2(a, 0), ALU.mult, Dsum, [Dsnd])
        CP(snd.s(256, 512), U_(NSEG - 1, 1), Dsum, [Dsnd])
        CP(snd.s(514, 516), A2(NSEG - 1, 1), Dsum, [Dsnd])
        for a in range(NSEG - 2, -1, -1):
            TT(snd.v(256, [[128, 2], [1, 128]]), snd.v(256, [[128, 2], [1, 128]]), Ab(a, 1), ALU.mult, Dsum, [Dsnd])
            TT(snd.s(256, 512), snd.s(256, 512), U_(a, 1), ALU.add, Dsum, [Dsnd])
            TT(snd.s(514, 516), snd.s(514, 516), A2(a, 1), ALU.mult, Dsum, [Dsnd])
        DMA("sp", cc_in.ap(), snd.s(0, 520), [Dsnd], [Dccin], Dccin)
        S.add("pool", lambda e, s_: e.collective_compute("AllGather", ALU.bypass, replica_groups=[[0, 1, 2, 3], [4, 5, 6, 7]],
                                                         ins=[cc_in.ap().opt()], outs=[cc_out.ap().opt()]).then_inc(s_, 1),
              reads=[Dccin], writes=[Dccout], dma=Dccout, inc=1)
        DMA("sp", gath.v(0, [[520, 4], [1, 520]]), cc_out.ap().rearrange("(r p) c -> p r c", p=128), [Dccout], [Dgath], Dgath)
        mod_late()
        Sc = sb("Sc", 2 * 256, F32, xsc)
        Sct2 = sb("Sct", 2 * 256, F32, xsc)
        DSc, DSct = [Dep("Sc0"), Dep("Sc1")], [Dep("Sct0"), Dep("Sct1")]
        DSent2 = [Dep("Sent0"), Dep("Sent1")]

        def compose(d):
            Sd = Sc.s(d * 256, d * 256 + 256)
            Sd3 = Sc.v(d * 256, [[128, 2], [1, 128]])
            St = Sct2.s(d * 256, d * 256 + 256)
            St3 = Sct2.v(d * 256, [[128, 2], [1, 128]])
            CP(Sd, U_(NSEG, d), Dsum, [DSc[d]])
            yield
            order = range(4) if d == 0 else range(3, -1, -1)
            for p in order:
                TT(St3, Sd3, gath.v(p * 520 + 512 + 2 * d, [[1, 2], [0, 128]]), ALU.mult, [DSc[d], Dgath], [DSct[d]])
                yield
                TT(St, St, gath.s(p * 520 + d * 256, p * 520 + d * 256 + 256), ALU.add, [Dgath], [DSct[d]])
                yield
                TT(St, St, Sd, ALU.subtract, [DSc[d]], [DSct[d]])
                yield
                STT(Sd, St, prm.s(88 + 4 * d + p, 88 + 4 * d + p + 1), Sd, ALU.mult, ALU.add, [DSct[d], Dprm], [DSc[d]])
                yield
            segs = range(NSEG) if d == 0 else range(NSEG - 1, -1, -1)
            for a in segs:
                CP(Sentbf.s((a * 2 + d) * 256, (a * 2 + d) * 256 + 256), Sd, [DSc[d]], [DSent2[d]], eng="act")
                yield
                TT(Sd3, Sd3, Ab(a, d), ALU.mult, Dsum, [DSc[d]])
                yield
                TT(Sd, Sd, U_(a, d), ALU.add, Dsum, [DSc[d]])
                yield

        drain(roundrobin([compose(0), compose(1)]))
        tap("Sent", Sentbf.s(0, NSEG * 512), NSEG * 512, BF16, DSent2)
        S.barrier()
        xsc.close()
        if stop == "x":
            S.add("sp", None, reads=tapd)
            S.emit(st)
            return nc

        pb = ExitStack()
        WS.append(sb("ws3", 4096, BF16, pb))
        WD.append(Dep("ws3"))
        kctx = sb("kctx", 4 * CTX, BF16, pb)
        vctx = sb("vctx", 2 * 520, BF16, pb)
        Dvctx = Dep("vctx")
        gq8 = sb("gq8", 1, F32, pb)
        Dgq8 = Dep("gq8")
        TS(gq8.s(0, 1), gq, 0.125, ALU.mult, [Dprm], [Dgq8])
        PTf = T(PT.h.bitcast(F32), 512)

        def make_qk_norm(stack, nslot=2):
            sqb = [sb("sqb%d" % i_, 512, BF16, stack) for i_ in range(nslot)]
            rsb = [sb("rsb%d" % i_, 512, F32, stack) for i_ in range(nslot)]
            Dsqb, Drsb = [Dep() for _ in range(nslot)], [Dep() for _ in range(nslot)]
            if nslot == 2:
                banks = [(P[0], DP[0], P[2], DP[2]), (P[1], DP[1], P[3], DP[3])]
            else:
                banks = [(P[0], DP[0], P[4], DP[4]), (P[1], DP[1], P[5], DP[5]), (P[2], DP[2], P[6], DP[6]),
                         (P[3], DP[3], PTf, DPT)]

            def qk_norm_g(proj_fn, slot, N, g_ap, g_dep, dst_ap, dst_dep):
                pb_, dpb_, sb_, dsb_ = banks[slot]
                proj_fn(pb_, dpb_)
                yield
                ACT(sqb[slot].s(0, N), pb_.s(0, N), AF.Square, [dpb_], [Dsqb[slot]])
                yield
                PE([(sb_.s(0, N), blkb, sqb[slot].s(0, N), True, True)], [Dsqb[slot], Dcstb], [dsb_])
                yield
                ACT(rsb[slot].s(0, N), sb_.s(0, N), AF.Ln, [dsb_, Dkc], [Drsb[slot]], scale=1.0 / 64, bias=kc.s(0, 1))
                yield
                ACT(rsb[slot].s(0, N), rsb[slot].s(0, N), AF.Exp, [], [Drsb[slot]], scale=-0.5)
                yield
                STT(dst_ap, pb_.s(0, N), g_ap, rsb[slot].s(0, N), ALU.mult, ALU.mult, [dpb_, Drsb[slot], g_dep], [dst_dep])
                yield
            return qk_norm_g

        def run_groups(gens, k):
            for j in range(0, len(gens), k):
                drain(roundrobin(gens[j:j + k]))

        def run_pairs(gens):
            for j in range(0, len(gens), 2):
                drain(roundrobin(gens[j:j + 2]))

        MEMSET(vctx.s(0, 2 * 520), 1.0, [Dvctx])
        scs = ExitStack()
        qk_norm_g = make_qk_norm(scs)
        ws, wd, n = loadw("wk_na", 0, 8, 0, 512)
        gens = []
        Dkctx_l = [Dep("kctx%d" % i_) for i_ in range(4)]
        for hp in range(4):
            def pj(pb_, dpb_, hp=hp, ws=ws, wd=wd):
                PE([(pb_.s(0, CTX), ws.v(kt * 512 + hp * 128, [[1, 128]]), hTc.s(kt * CTX, kt * CTX + CTX), kt == 0, kt == 7)
                    for kt in range(8)], [wd] + DhTc, [dpb_])
            gens.append(qk_norm_g(pj, hp % 2, CTX, gk, Dprm, kctx.s(hp * CTX, hp * CTX + CTX), Dkctx_l[hp]))
        run_groups(gens, 2)
        ws, wd, n = loadw("wv_na", 0, 8, 0, 512)
        for tci in range(2):
            b = 4 + tci
            PE([(P[b].s(0, 512), hTc.s(kt * CTX + tci * 128, kt * CTX + tci * 128 + 128), ws.s(kt * 512, kt * 512 + 512),
                 kt == 0, kt == 7) for kt in range(8)], [wd] + DhTc, [DP[b]])
            CP(vctx.v(tci * 520, [[65, 8], [1, 64]]), P[b].v(0, [[64, 8], [1, 64]]), [DP[b]], [Dvctx], eng="act")
        S.barrier()
        scs.close()
        tap("kctx", kctx.s(0, 4 * CTX), 4 * CTX, BF16, Dkctx_l)
        tap("vctx", vctx.s(0, 1040), 1040, BF16, [Dvctx])

        ost = [sb("ost%d" % i, 512, F32, pb) for i in range(2)]
        Dost = [Dep("ost0"), Dep("ost1")]
        Dout = [Dep("out0"), Dep("out1")]
        octr = [0]
        SPEC = {(0, 0): 1, (0, 1): 2, (NSEG - 1, 2): 3, (NSEG - 1, 3): 4}

        def halo_prep(sn):
            hTn, DhTn = hTb[sn % 2], DhTb[sn % 2]
            for tt in range(0, 8, 2):
                gs = []
                for slot, t2 in enumerate((tt, tt + 1)):
                    x_t, x_d = load_x(I["xh"][sn * 1024 + t2 * 128: sn * 1024 + t2 * 128 + 128, :])
                    gs.append(norm_T_g(x_t, x_d, 0, 8, hTn, DhTn[t2 * 8:(t2 + 1) * 8], t2 * 128, slot, Dmv))
                for _ in roundrobin(gs):
                    yield

        for s_i in range(NSEG):
            sg = ExitStack()
            o_naT = sb("o_naT", 4 * 512, BF16, sg)
            Do_naT = Dep("o_naT")
            hT, DhT = hTb[s_i % 2], DhTb[s_i % 2]
            if s_i == 0:
                drain(halo_prep(0))
            hown = lambda kt, hT=hT: hT.s(kt * 1024 + 256, kt * 1024 + 768)
            b1 = ExitStack()
            tabI = sb("tabI", 5120, BF16, b1)
            tabS = sb("tabS", 5120, BF16, b1)
            DtabI, DtabS = Dep("tabI"), Dep("tabS")
            DMA("pool", tabI.s(0, 5120), I["btab"][0], [], [DtabI], DtabI, ab=True)
            kn = sb("kn", 4 * 1024, BF16, b1)
            qn = sb("qn", 4 * 512, BF16, b1)
            vaug = sb("vaug", 8 * 520, BF16, b1)
            Dkn = [Dep("kn%d" % i_) for i_ in range(8)]
            Dqn = [Dep("qn%d" % i_) for i_ in range(4)]
            Dvaug = Dep("vaug")
            ptA = [sb("ptA%d" % i, 512, BF16, b1) for i in range(2)]
            ptB = [sb("ptB%d" % i, 384, BF16, b1) for i in range(2)]
            DptA = [Dep("ptA0"), Dep("ptA1")]
            DptB = [Dep("ptB0"), Dep("ptB1")]
            otm = sb("otm", 512, BF16, b1)
            Dotm = Dep("otm")
            rcp = sb("rcp", 8, F32, b1)
            Drcp = Dep("rcp")
            MEMSET(vaug.s(0, 8 * 520), 1.0, [Dvaug])
            qk_norm_g = make_qk_norm(b1, 4)
            ws, wd, n = loadw("wq_na", 0, 8, 0, 512)
            gens = []
            for hp in range(4):
                def pj(pb_, dpb_, hp=hp, ws=ws, wd=wd):
                    PE([(pb_.s(0, 512), ws.v(kt * 512 + hp * 128, [[1, 128]]), hown(kt), kt == 0, kt == 7) for kt in range(8)],
                       [wd] + DhT, [dpb_])
                gens.append(qk_norm_g(pj, hp % 4, 512, gq8.s(0, 1), Dgq8, qn.s(hp * 512, hp * 512 + 512), Dqn[hp]))
            ws, wd, n = loadw("wk_na", 0, 8, 0, 512)
            for hp in range(4):
                for tg in range(2):
                    def pj(pb_, dpb_, hp=hp, tg=tg, ws=ws, wd=wd):
                        PE([(pb_.s(0, 512), ws.v(kt * 512 + hp * 128, [[1, 128]]),
                             hT.s(kt * 1024 + tg * 512, kt * 1024 + tg * 512 + 512), kt == 0, kt == 7) for kt in range(8)],
                           [wd] + DhT, [dpb_])
                    gens.append(qk_norm_g(pj, (hp * 2 + tg) % 4, 512, gk, Dprm, kn.s(hp * 1024 + tg * 512, hp * 1024 + tg * 512 + 512),
                                          Dkn[hp * 2 + tg]))
            run_groups(gens, 4)
            ws, wd, n = loadw("wv_na", 0, 8, 0, 512)
            for tt in range(8):
                b = 4 + tt % 2
                PE([(P[b].s(0, 512), hT.s(kt * 1024 + tt * 128, kt * 1024 + tt * 128 + 128), ws.s(kt * 512, kt * 512 + 512),
                     kt == 0, kt == 7) for kt in range(8)], [wd] + DhT, [DP[b]])
                CP(vaug.v(tt * 520, [[65, 8], [1, 64]]), P[b].v(0, [[64, 8], [1, 64]]), [DP[b]], [Dvaug], eng="act")
            if s_i == 0:
                tap("qn", qn.s(0, 2048), 2048, BF16, Dqn)
                tap("kn", kn.s(0, 4096), 4096, BF16, Dkn)
                tap("vaug", vaug.s(0, 8 * 520), 8 * 520, BF16, [Dvaug])
            for i in range(4):
                if stop == "b1a" or (stop in ("b1b", "b1c", "b1d", "b1e") and i == 1):
                    break
                sp_ = SPEC.get((s_i, i))
                if sp_ is None:
                    tab, Dtab = tabI, DtabI
                else:
                    DMA("pool", tabS.s(0, 5120), I["btab"][sp_], [], [DtabS], DtabS, ab=True)
                    tab, Dtab = tabS, DtabS

                def ST(h, i=i, tab=tab, Dtab=Dtab):
                    hp, hh = divmod(h, 2)
                    ba, bb_ = (0, 1) if h % 2 == 0 else (2, 3)
                    qap = qn.s(hp * 512 + i * 128, hp * 512 + i * 128 + 128, p0=64 * hh, n=64)
                    mms = []
                    for t_ in range(4):
                        o_ = P[ba].s(t_ * 128, t_ * 128 + 128)
                        mms.append((o_, kn.s(hp * 1024 + (i + t_) * 128, hp * 1024 + (i + t_) * 128 + 128, p0=64 * hh, n=64), qap, True, False))
                        mms.append((o_, identb, tab.s(h * 640 + t_ * 128, h * 640 + t_ * 128 + 128), False, True))
                    PE(mms, Dkn + Dqn + [Dtab, Dcstb], [DP[ba]])
                    mms = []
                    o_ = P[bb_].s(0, 128)
                    mms.append((o_, kn.s(hp * 1024 + (i + 4) * 128, hp * 1024 + (i + 4) * 128 + 128, p0=64 * hh, n=64), qap, True, False))
                    mms.append((o_, identb, tab.s(h * 640 + 512, h * 640 + 640), False, True))
                    for c_ in range(2):
                        mms.append((P[bb_].s(128 + c_ * 128, 256 + c_ * 128),
                                    kctx.s(hp * CTX + c_ * 128, hp * CTX + c_ * 128 + 128, p0=64 * hh, n=64), qap, True, True))
                    PE(mms, Dkn + Dqn + [Dtab, Dcstb] + Dkctx_l, [DP[bb_]])

                def EXPV(h, i=i):
                    j = h % 2
                    ba, bb_ = (0, 1) if h % 2 == 0 else (2, 3)
                    ACT(ptA[j].s(0, 512), P[ba].s(0, 512), AF.Exp, [DP[ba]], [DptA[j]])
                    ACT(ptB[j].s(0, 384), P[bb_].s(0, 384), AF.Exp, [DP[bb_]], [DptB[j]])
                    ob = 4 + h // 4
                    oc = (h % 4) * 65
                    o_ = P[ob].s(oc, oc + 65)
                    mms = []
                    for t_ in range(4):
                        mms.append((o_, ptA[j].s(t_ * 128, t_ * 128 + 128), vaug.s((i + t_) * 520 + h * 65, (i + t_) * 520 + h * 65 + 65),
                                    t_ == 0, False))
                    mms.append((o_, ptB[j].s(0, 128), vaug.s((i + 4) * 520 + h * 65, (i + 4) * 520 + h * 65 + 65), False, False))
                    for c_ in range(2):
                        mms.append((o_, ptB[j].s(128 + c_ * 128, 256 + c_ * 128), vctx.s(c_ * 520 + h * 65, c_ * 520 + h * 65 + 65),
                                    False, c_ == 1))
                    PE(mms, [DptA[j], DptB[j], Dvaug, Dvctx], [DP[ob]])

                if stop == "b1e":
                    break
                ST(0)
                if stop == "b1c":
                    break
                if stop == "b1d":
                    EXPV(0)
                    break
                for h in range(8):
                    if h + 1 < 8:
                        ST(h + 1)
                    EXPV(h)
                    if h % 4 == 3:
                        ob = 4 + h // 4
                        g4 = h // 4
                        S.add("dve", lambda e, ob=ob, g4=g4: e.reciprocal(out=rcp.s(g4 * 4, g4 * 4 + 4), in_=P[ob].v(64, [[65, 4]])),
                              [DP[ob]], [Drcp])
                        TT(otm.v(g4 * 256, [[64, 4], [1, 64]]), P[ob].v(0, [[65, 4], [1, 64]]), rcp.v(g4 * 4, [[1, 4], [0, 64]]),
                           ALU.mult, [DP[ob], Drcp], [Dotm])
                PE([("T", PT.s(k_ * 128, k_ * 128 + 128), otm.s(k_ * 128, k_ * 128 + 128), identb) for k_ in range(4)],
                   [Dotm, Dcstb], [DPT])
                CP(o_naT.v(i * 128, [[512, 4], [1, 128]]), PT.v(0, [[128, 4], [1, 128]]), [DPT], [Do_naT])
            if s_i == 0:
                tap("o_naT", o_naT.s(0, 2048), 2048, BF16, [Do_naT])
            S.barrier()
            b1.close()
            if stop in ("b1", "b1a", "b1b", "b1c", "b1d", "b1e"):
                sg.close()
                break
            sg2 = ExitStack()
            x1 = sb("x1", 4 * 1024, F32, sg2)
            Dx1 = [Dep("x1_%d" % t_) for t_ in range(4)]
            h2T = sb("h2T", 8 * 512, BF16, sg2)
            Dh2T = [Dep("h2T%d" % i_) for i_ in range(32)]
            b2 = ExitStack()
            sog = [sb("sog%d" % i_, 512, F32, b2) for i_ in range(2)]
            ogt = [sb("ogt%d" % i_, 512, F32, b2) for i_ in range(2)]
            gt2 = [sb("gt2_%d" % i_, 512, F32, b2) for i_ in range(2)]
            gsq = [sb("gsq%d" % i_, 512, BF16, b2) for i_ in range(2)]
            Dgsq = [Dep(), Dep()]
            Dsog, Dogt, Dgt2 = [Dep(), Dep()], [Dep(), Dep()], [Dep(), Dep()]
            o_gT = sb("o_gT", 4 * 512, BF16, b2)
            Do_gT = [Dep("o_gT%d" % i_) for i_ in range(4)]
            sig = [sb("sig%d" % i, 512, F32, b2) for i in range(2)]
            Dsig = [Dep("sig0"), Dep("sig1")]
            ytmp = sb("ytmp", 4 * 512, F32, b2)
            Dytmp = Dep("ytmp")
            yT = sb("yT", 8 * 512, BF16, b2)
            DyT = Dep("yT")
            mtmp, Dmtmp = ogt[0], Dogt[0]
            for tt in range(4):
                DMA("sp", x1.s(tt * 1024, tt * 1024 + 1024), I["xa"][s_i * 512 + tt * 128: s_i * 512 + tt * 128 + 128, :], [], [Dx1[tt]],
                    Dx1[tt], ab=True)
            wso, wdo, n = loadw("w_og", 0, 8, 0, 512)

            def gla_fin(h, sl):
                hp, hh = divmod(h, 2)
                b0, b1_, b2_ = (0, 1, 2) if sl == 0 else (3, 4, 5)
                sog_, ogt_, gt2_ = sog[sl], ogt[sl], gt2[sl]
                PE([(P[b0].s(0, 512), wso.v(kt * 512 + h * 128, [[1, 128]]), hown(kt), kt == 0, kt == 7) for kt in range(8)],
                   [wdo] + DhT, [DP[b0]])
                yield
                ACT(sog_.s(0, 512), P[b0].s(0, 512), AF.Silu, [DP[b0]], [Dsog[sl]])
                yield
                PE([(P[b1_].s(0, 512), Sentbf.s((s_i * 2 + 0) * 256 + hp * 128, (s_i * 2 + 0) * 256 + hp * 128 + 128, p0=64 * hh, n=64),
                     qsd.s((0 * 2 + hp) * TOK + s_i * 512, (0 * 2 + hp) * TOK + s_i * 512 + 512, p0=64 * hh, n=64), True, False),
                    (P[b1_].s(0, 512), Sentbf.s((s_i * 2 + 1) * 256 + hp * 128, (s_i * 2 + 1) * 256 + hp * 128 + 128, p0=64 * hh, n=64),
                     qsd.s((1 * 2 + hp) * TOK + s_i * 512, (1 * 2 + hp) * TOK + s_i * 512 + 512, p0=64 * hh, n=64), False, True)],
                   DSent2 + [Dqsd[s_i]], [DP[b1_]])
                yield
                TT(ogt_.s(0, 512), P[b1_].s(0, 512), o_loc.s(h * TOK + s_i * 512, h * TOK + s_i * 512 + 512), ALU.add,
                   [DP[b1_], Doloc[s_i]], [Dogt[sl]])
                yield
                if s_i == 0 and h == 0:
                    tap("ogt", ogt_.s(0, 512), 512, F32, [Dogt[sl]])
                ACT(gsq[sl].s(0, 512), ogt_.s(0, 512), AF.Square, [Dogt[sl]], [Dgsq[sl]])
                yield
                PE([(P[b2_].s(0, 512), onesb.s(0, 128), gsq[sl].s(0, 512), True, True)], [Dgsq[sl], Dones], [DP[b2_]])
                yield
                ACT(gt2_.s(0, 512), P[b2_].s(0, 512), AF.Ln, [DP[b2_], Dkc], [Dgt2[sl]], scale=1.0 / 128, bias=kc.s(0, 1))
                yield
                ACT(gt2_.s(0, 512), gt2_.s(0, 512), AF.Exp, [], [Dgt2[sl]], scale=-0.5)
                yield
                STT(ogt_.s(0, 512), ogt_.s(0, 512), gn, gt2_.s(0, 512), ALU.mult, ALU.mult, [Dgt2[sl], Dprm], [Dogt[sl]])
                yield
                TT(o_gT.s(h * 512, h * 512 + 512), ogt_.s(0, 512), sog_.s(0, 512), ALU.mult, [Dogt[sl], Dsog[sl]], [Do_gT[h]])
                yield

            run_groups([gla_fin(h, h % 2) for h in range(4)], 2)
            if s_i == 0:
                tap("o_gT", o_gT.s(0, 2048), 2048, BF16, Do_gT)
            for half in range(2):
                for br in range(2):
                    wsb, wdb, n = loadw("w_br_na" if br == 0 else "w_br_gla", 0, 4, half * 512, 512)
                    wsm, wdm, n = loadw("w_ma" if br == 0 else "w_mb", 0, 8, half * 512, 512)
                    src, dsrc = (o_naT, [Do_naT]) if br == 0 else (o_gT, Do_gT)
                    for cj in range(4):
                        c = half * 4 + cj
                        pb_, mb_ = (0, 1) if cj % 2 == 0 else (2, 3)
                        PE([(P[pb_].s(0, 512), wsb.v(k_ * 512 + cj * 128, [[1, 128]]), src.s(k_ * 512, k_ * 512 + 512), k_ == 0, k_ == 3)
                            for k_ in range(4)], [wdb] + dsrc, [DP[pb_]])
                        PE([(P[mb_].s(0, 512), wsm.v(kt * 512 + cj * 128, [[1, 128]]), hown(kt), kt == 0, kt == 7) for kt in range(8)],
                           [wdm] + DhT, [DP[mb_]])
                        j = cj % 2
                        ACT(sig[j].s(0, 512), P[mb_].s(0, 512), AF.Sigmoid, [DP[mb_]], [Dsig[j]])
                        if br == 0:
                            TT(ytmp.s(cj * 512, cj * 512 + 512), P[pb_].s(0, 512), sig[j].s(0, 512), ALU.mult, [DP[pb_], Dsig[j]],
                               [Dytmp])
                        else:
                            TT(mtmp.s(0, 512), P[pb_].s(0, 512), sig[j].s(0, 512), ALU.mult, [DP[pb_], Dsig[j]], [Dmtmp])
                            TT(yT.s(c * 512, c * 512 + 512), mtmp.s(0, 512), ytmp.s(cj * 512, cj * 512 + 512), ALU.add,
                               [Dmtmp, Dytmp], [DyT])
            if s_i == 0:
                tap("yT", yT.s(0, 4096), 4096, BF16, [DyT])
            for half in range(2):
                ws, wd, n = loadw("w_out", 0, 8, half * 512, 512)
                for tt in range(4):
                    b = 4 + tt % 2
                    PE([(P[b].s(0, 512), yT.s(c * 512 + tt * 128, c * 512 + tt * 128 + 128), ws.s(c * 512, c * 512 + 512), c == 0, c == 7)
                        for c in range(8)], [wd, DyT], [DP[b]])
                    TT(mtmp.s(0, 512), P[b].s(0, 512), gbc.s(half * 512, half * 512 + 512), ALU.mult, [DP[b], Dgbc], [Dmtmp])
                    TT(x1.s(tt * 1024 + half * 512, tt * 1024 + half * 512 + 512), x1.s(tt * 1024 + half * 512, tt * 1024 + half * 512 + 512),
                       mtmp.s(0, 512), ALU.add, [Dmtmp], [Dx1[tt]])
            norm_tiles([(x1.s(tt * 1024, tt * 1024 + 1024), Dx1[tt], tt * 128, tt) for tt in range(4)], 32, 40, h2T, Dh2T, Dmv2)
            if s_i == 0:
                tap("x1", x1.s(0, 4096), 4096, F32, Dx1)
                tap("h2T", h2T.s(0, 4096), 4096, BF16, Dh2T)
            S.barrier()
            b2.close()
            b3 = ExitStack()
            actT = sb("actT", NF * 512, BF16, b3)
            Dact = Dep("actT")
            sgl = [sb("sgl%d" % i, 512, F32, b3) for i in range(2)]
            Dsgl = [Dep("sgl0"), Dep("sgl1")]
            def ffn_gu():
              for f2 in range(NF // 2):
                  i_ = wctr[0] % len(WS)
                  wctr[0] += 1
                  wsf, wdf = WS[i_], WD[i_]

                  def ld(e, s_, f2=f2, wsf=wsf):
                      e.dma_start(out=wsf.v(0, [[512, 8], [1, 256]]),
                                  in_=I["w_gate"][:, f2 * 256:f2 * 256 + 256].rearrange("(k p) n -> p k n", p=128)).then_inc(s_, 16)
                      e.dma_start(out=wsf.v(256, [[512, 8], [1, 256]]),
                                  in_=I["w_up"][:, f2 * 256:f2 * 256 + 256].rearrange("(k p) n -> p k n", p=128)).then_inc(s_, 16)
                  S.add("pool", ld, [], [wdf], dma=wdf, ndma=2)
                  for ff in range(2):
                      f = 2 * f2 + ff
                      bg, bu = (0, 1) if ff == 0 else (2, 3)
                      PE([(P[bg].s(0, 512), wsf.v(kt * 512 + ff * 128, [[1, 128]]), h2T.s(kt * 512, kt * 512 + 512), kt == 0, kt == 7)
                          for kt in range(8)], [wdf] + Dh2T, [DP[bg]])
                      yield
                      PE([(P[bu].s(0, 512), wsf.v(kt * 512 + 256 + ff * 128, [[1, 128]]), h2T.s(kt * 512, kt * 512 + 512), kt == 0, kt == 7)
                          for kt in range(8)], [wdf] + Dh2T, [DP[bu]])
                      yield
                      ACT(sgl[ff].s(0, 512), P[bg].s(0, 512), AF.Silu, [DP[bg]], [Dsgl[ff]])
                      yield
                      TT(actT.s(f * 512, f * 512 + 512), P[bu].s(0, 512), sgl[ff].s(0, 512), ALU.mult, [DP[bu], Dsgl[ff]], [Dact])
                      yield
            gens = [ffn_gu()]
            if s_i + 1 < NSEG:
                gens.append(halo_prep(s_i + 1))
            drain(roundrobin(gens))
            for half in range(2):
                banks = (0, 1, 2, 3) if half == 0 else (4, 5, 6, 3)
                nf4 = (NF + 3) // 4
                for f4 in range(nf4):
                    kk = min(4, NF - 4 * f4)
                    ws, wd, n = loadw("w_down", f4 * 512, kk, half * 512, 512)
                    for tt in range(4):
                        PE([(P[banks[tt]].s(0, 512), actT.s((4 * f4 + k_) * 512 + tt * 128, (4 * f4 + k_) * 512 + tt * 128 + 128),
                             ws.s(k_ * 512, k_ * 512 + 512), (f4 == 0 and k_ == 0), (f4 == nf4 - 1 and k_ == kk - 1)) for k_ in range(kk)],
                           [wd, Dact], [DP[banks[tt]]])
                for tt in range(4):
                    oi = octr[0] % 2
                    octr[0] += 1
                    b = banks[tt]
                    TT(ost[oi].s(0, 512), P[b].s(0, 512), gbc.s(1024 + half * 512, 1024 + half * 512 + 512), ALU.mult,
                       [DP[b], Dgbc, Dout[oi]], [Dost[oi]])
                    TT(ost[oi].s(0, 512), ost[oi].s(0, 512), x1.s(tt * 1024 + half * 512, tt * 1024 + half * 512 + 512), ALU.add,
                       [Dx1[tt]], [Dost[oi]])
                    DMA("sp", out[s_i * 512 + tt * 128: s_i * 512 + tt * 128 + 128, half * 512: half * 512 + 512], ost[oi].s(0, 512),
                        [Dost[oi]], [Dout[oi]], Dout[oi], nobar=True)
            S.barrier()
            b3.close()
            sg2.close()
            sg.close()
            if stop == "b3":
                break
        S.add("sp", None, reads=Dout + tapd)
        pb.close()
        S.emit(st)
    return nc


_NC_CACHE = {}


def kernel(**inputs):
    maps = prep_inputs(**inputs)
    if "nc" not in _NC_CACHE:
        _NC_CACHE["nc"] = build()
    nc = _NC_CACHE["nc"]
    res = run_bass_kernel_spmd(nc, maps, core_ids=list(range(NCORE)))
    out = np.empty((2, SEQ, D), np.float32)
    for core in range(NCORE):
        bb, r = divmod(core, 4)
        out[bb, r * TOK:(r + 1) * TOK] = np.asarray(res.results[core]["out"], dtype=np.float32)
    return out
```

```python
import numpy as np
from contextlib import ExitStack
import concourse.bass as bass
import concourse.mybir as mybir
from concourse.bass_utils import run_bass_kernel_spmd

F32 = mybir.dt.float32
BF16 = mybir.dt.bfloat16
AF = mybir.ActivationFunctionType
ALU = mybir.AluOpType

D = 1024
SEQ = 8192
NCORE = 8
TOK = 2048
SA = 512
WAW_RELAX = True
STG_GAP = 80
NSEG = TOK // SA
CTX = 256
DFF = 2816
NF = DFF // 128
EPS = 1e-6
NEG = -1e30


class Dep:
    __slots__ = ("name", "w", "rs", "sem", "semcnt")

    def __init__(self, name=""):
        self.name = name
        self.w = None
        self.rs = []
        self.sem = None
        self.semcnt = 0


class Op:
    __slots__ = ("eng", "fn", "deps", "dma", "signal", "ev", "idx", "inc")

    def __init__(self, eng, fn, dma, inc):
        self.eng = eng
        self.fn = fn
        self.deps = []
        self.dma = dma
        self.inc = inc
        self.signal = False
        self.ev = None


ENGS = ("pe", "act", "dve", "pool", "sp")


class Sched:
    def __init__(self, nc, same_engine_sync=True):
        self.nc = nc
        self.ops = []
        self.same_engine_sync = same_engine_sync
        self.last = {e: None for e in ENGS}
        self.pending_dma = []
        self.bar_evs = []

    def add(self, eng, fn, reads=(), writes=(), dma=None, ndma=1, inc=16, after_barrier=False, nobar=False):
        deps = []
        if after_barrier:
            deps.extend(getattr(self, "bar_evs", []))
        for r in reads:
            if r.w is not None:
                deps.append(r.w)
        for w in writes:
            if w.w is not None:
                deps.append(w.w)
            deps.extend(w.rs)
        op = Op(eng, fn, dma, ndma * inc)
        raw = set(id(r.w) for r in reads if r.w is not None)
        seen = set()
        for d in deps:
            if id(d) in seen:
                continue
            seen.add(id(d))
            if d.dma is None and d.eng == eng and (eng == "pe" or not self.same_engine_sync):
                continue
            if WAW_RELAX and dma is None and d.dma is None and d.eng == eng and id(d) not in raw:
                continue
            op.deps.append(d)
        for r in reads:
            r.rs.append(op)
        for w in writes:
            w.w = op
            w.rs = []
        self.ops.append(op)
        if dma is None:
            self.last[eng] = op
        elif not nobar:
            self.pending_dma.append(op)
        return op

    def barrier(self):
        evs = [o for o in self.last.values() if o is not None] + list(self.pending_dma)
        self.pending_dma = []
        self.bar_evs = evs
        for e in ("pe", "act", "dve"):
            op = Op(e, None, None, 0)
            op.deps = [d for d in evs if not (d.dma is None and d.eng == e and e == "pe")]
            self.ops.append(op)

    def emit(self, stack):
        nc = self.nc
        for op in self.ops:
            for d in op.deps:
                d.signal = True
        esem = {}
        for e in ("pe", "act", "dve", "pool"):
            esem[e] = stack.enter_context(nc.semaphore("sem_" + e))
        ecnt = {e: 0 for e in esem}
        nsem = 0
        for op in self.ops:
            if op.dma is not None:
                d = op.dma
                if d.sem is None:
                    d.sem = stack.enter_context(nc.semaphore("dsem%d" % nsem))
                    nsem += 1
                d.semcnt += op.inc
                op.ev = (d.sem, d.semcnt)
            elif op.signal and op.fn is not None:
                ecnt[op.eng] += 1
                op.ev = (esem[op.eng], ecnt[op.eng])
        self.nsem = nsem
        per = {e: [o for o in self.ops if o.eng == e] for e in ENGS}
        block = stack.enter_context(nc.Block())

        def run(engobj, ops):
            waited = {}
            for op in ops:
                for d in op.deps:
                    if d.ev is None:
                        continue
                    sem, val = d.ev
                    k = id(sem)
                    if waited.get(k, 0) >= val:
                        continue
                    waited[k] = val
                    engobj.wait_ge(sem, val)
                if op.fn is None:
                    continue
                if op.dma is not None:
                    op.fn(engobj, op.ev[0])
                else:
                    inst = op.fn(engobj)
                    if op.signal:
                        inst.then_inc(op.ev[0], 1)

        @block.tensor
        def _(e):
            run(e, per["pe"])

        @block.scalar
        def _(e):
            run(e, per["act"])

        @block.vector
        def _(e):
            run(e, per["dve"])

        @block.gpsimd
        def _(e):
            run(e, per["pool"])

        @block.sync
        def _(e):
            run(e, per["sp"])


class T:
    def __init__(self, h, F):
        self.h = h
        self.F = F

    def v(self, c0, dims, p0=0, n=128):
        return bass.AP(self.h, p0 * self.F + c0, [[self.F, n]] + [list(d) for d in dims])

    def s(self, c0, c1, p0=0, n=128):
        return bass.AP(self.h, p0 * self.F + c0, [[self.F, n], [1, c1 - c0]])


def _gmap(Rs, b):
    g = Rs - 4 + b
    if g < 0:
        return {-4: 5, -3: 6, -2: 7, -1: 7}[g]
    if g > 127:
        return 120 if g == 128 else 121
    return g


def _bias_table(rpb, Rs, i):
    tab = np.full((128, 8, 5, 128), NEG, np.float32)
    cq = np.arange(64)
    ck = np.arange(64)
    col_start = np.clip(cq - 8, 0, 48)
    in_win = (ck[None, :] >= col_start[:, None]) & (ck[None, :] < col_start[:, None] + 16)
    dc = np.clip(ck[None, :] - cq[:, None], -15, 15) + 15
    seen = set()
    for kb in range(9):
        b = 2 * i + kb
        g = _gmap(Rs, b)
        first = g not in seen
        seen.add(g)
        t, ph = divmod(kb * 64, 128)
        for ql in range(2):
            rq = Rs + 2 * i + ql
            rs = min(max(rq - 4, 0), 120)
            if not (first and rs <= g < rs + 8):
                continue
            dr = g - rq + 7
            vals = rpb[:, dr, :][:, dc]
            vals = np.where(in_win[None], vals, NEG)
            tab[ph:ph + 64, :, t, ql * 64:(ql + 1) * 64] = np.transpose(vals, (2, 0, 1))
    return tab.reshape(128, 8 * 5 * 128)


def _fm(v):
    return np.ascontiguousarray(v.reshape(-1, 128).T)


def prep_inputs(x, c, ctx, c_ctx, w_mod, b_mod, norm1_g, norm2_g, w_in, na_q_norm_g, na_k_norm_g, na_rpb,
                gla_w_alpha, gla_b_alpha, gla_norm_g, w_branch_na, w_branch_gla, w_out,
                w_ffn_gate, w_ffn_up, w_ffn_down):
    f = lambda a: np.ascontiguousarray(np.asarray(a, dtype=np.float32))
    x, c, ctx, c_ctx = f(x), f(c), f(ctx), f(c_ctx)
    w_in0 = f(w_in)[0]
    offs = np.cumsum([0, 512, 512, 512, 256, 256, 512, 512, 32, 1024, 1024])
    sl = lambda k: w_in0[:, offs[k]:offs[k + 1]]
    wq_na, wk_na, wv_na, wq_g, wk_g, wv_g, w_og, w_lr, w_ma, w_mb = [sl(k) for k in range(10)]
    d = np.arange(256)
    dd = d % 64
    swp = (d // 64) * 64 + np.where(dd % 32 < 16, dd + 16, dd - 16)
    wg1 = np.ascontiguousarray(np.concatenate([wq_g, wq_g[:, swp], wk_g, wk_g[:, swp]], axis=1))
    wal = f(gla_w_alpha)[0]
    walpha = np.zeros((32, 512), np.float32)
    walpha[0:16, 0:256] = wal[0]
    walpha[16:32, 256:512] = wal[1]
    bal = f(gla_b_alpha)[0]
    balpha = np.ascontiguousarray(np.stack([bal[0, 0:128], bal[0, 128:256], bal[1, 0:128], bal[1, 128:256]], axis=1))
    rpb = f(na_rpb)[0]
    gq = np.tile(f(na_q_norm_g)[0], 2).reshape(128, 1)
    gk = np.tile(f(na_k_norm_g)[0], 2).reshape(128, 1)
    gn = f(gla_norm_g)[0].reshape(128, 1)
    pvec = np.ascontiguousarray(np.concatenate([gq, gk, gn], axis=1))
    ident = np.eye(128, dtype=np.float32)
    blk = np.zeros((128, 128), np.float32)
    blk[:64, :64] = 1
    blk[64:, 64:] = 1
    j = np.arange(128)[:, None]
    i = np.arange(128)[None, :]
    same = (j // 64) == (i // 64)
    mf = (same & (j <= i)).astype(np.float32)
    mb = (same & (j >= i)).astype(np.float32)
    consts = np.ascontiguousarray(np.concatenate([ident, blk, mf, mb], axis=1))
    freqs = (10000.0 ** (-np.arange(16, dtype=np.float32) / 16)).astype(np.float32)
    tab_int = _bias_table(rpb, 40, 0)
    shared = {
        "w_mod": f(w_mod)[0], "b_modT": _fm(f(b_mod)[0]),
        "n1g": _fm(f(norm1_g)[0]), "n2g": _fm(f(norm2_g)[0]),
        "wq_na": np.ascontiguousarray(wq_na), "wk_na": np.ascontiguousarray(wk_na),
        "wv_na": np.ascontiguousarray(wv_na), "wg1": wg1, "wv_g": np.ascontiguousarray(wv_g),
        "w_og": np.ascontiguousarray(w_og), "w_lr": np.ascontiguousarray(w_lr),
        "w_ma": np.ascontiguousarray(w_ma), "w_mb": np.ascontiguousarray(w_mb),
        "walpha": walpha, "balpha": balpha, "pvec": pvec,
        "w_br_na": f(w_branch_na)[0], "w_br_gla": f(w_branch_gla)[0], "w_out": f(w_out)[0],
        "w_gate": f(w_ffn_gate)[0], "w_up": f(w_ffn_up)[0], "w_down": f(w_ffn_down)[0],
        "consts": consts,
    }
    maps = []
    for core in range(NCORE):
        bb, r = divmod(core, 4)
        R0 = 32 * r
        xa = x[bb, R0 * 64:(R0 + 32) * 64]
        rows = []
        for s in range(NSEG):
            Rs = R0 + 8 * s
            for b in range(16):
                g = _gmap(Rs, b)
                rows.append(x[bb, g * 64:(g + 1) * 64])
        xh = np.ascontiguousarray(np.concatenate(rows, axis=0))
        cvec = np.ascontiguousarray(np.concatenate([_fm(c[bb]), _fm(c_ctx)], axis=1))
        tabs = [tab_int,
                _bias_table(rpb, R0, 0) if r == 0 else tab_int,
                _bias_table(rpb, R0, 1) if r == 0 else tab_int,
                _bias_table(rpb, R0 + 24, 2) if r == 3 else tab_int,
                _bias_table(rpb, R0 + 24, 3) if r == 3 else tab_int]
        btab = np.ascontiguousarray(np.stack(tabs, axis=0))
        t = np.arange(TOK)
        rowp = (R0 + t // 64).astype(np.float32)
        colp = (t % 64).astype(np.float32)
        dd = np.arange(128) % 64
        fr = freqs[dd % 16][:, None]
        pos = np.where((dd < 32)[:, None], rowp[None, :], colp[None, :]).astype(np.float32)
        ang = (pos * fr).astype(np.float32)
        ropeC = np.ones((128, TOK + CTX), np.float32)
        ropeS = np.zeros((128, TOK + CTX), np.float32)
        ropeC[:, :TOK] = np.cos(ang)
        sgn = np.where((dd % 32) < 16, -1.0, 1.0).astype(np.float32)[:, None]
        ropeS[:, :TOK] = np.sin(ang) * sgn
        cmask = np.zeros((128, 8), np.float32)
        for p in range(4):
            cmask[:, p] = 1.0 if p < r else 0.0
            cmask[:, 4 + p] = 1.0 if p > r else 0.0
        m = dict(shared)
        m.update({"xa": np.ascontiguousarray(xa), "xh": xh, "xc": np.ascontiguousarray(ctx[bb]), "cvec": cvec,
                  "btab": btab, "ropeC": ropeC, "ropeS": ropeS, "cmask": cmask})
        maps.append(m)
    return maps


IN_SHAPES = {
    "xa": [TOK, D], "xh": [NSEG * 1024, D], "xc": [CTX, D], "cvec": [128, 16],
    "w_mod": [D, 6 * D], "b_modT": [128, 48], "n1g": [128, 8], "n2g": [128, 8],
    "wq_na": [D, 512], "wk_na": [D, 512], "wv_na": [D, 512], "wg1": [D, 1024], "wv_g": [D, 512],
    "w_og": [D, 512], "w_lr": [D, 32], "w_ma": [D, D], "w_mb": [D, D],
    "walpha": [32, 512], "balpha": [128, 4], "pvec": [128, 3],
    "w_br_na": [512, D], "w_br_gla": [512, D], "w_out": [D, D],
    "w_gate": [D, DFF], "w_up": [D, DFF], "w_down": [DFF, D],
    "consts": [128, 512], "btab": [5, 128, 5120], "ropeC": [128, TOK + CTX], "ropeS": [128, TOK + CTX],
    "cmask": [128, 8],
}


def build(stop=None, taps=None):
    nc = bass.Bass("TRN2", target_bir_lowering=False)
    I = {k: nc.dram_tensor(k, shp, F32, kind="ExternalInput").ap() for k, shp in IN_SHAPES.items()}
    out = nc.dram_tensor("out", [TOK, D], F32, kind="ExternalOutput").ap()
    cc_in = nc.dram_tensor("cc_in", [128, 520], F32)
    cc_out = nc.dram_tensor("cc_out", [512, 520], F32)
    taps = taps if taps is not None else {}
    tapd = []
    st = ExitStack()
    with st:
        S = Sched(nc)

        uniq = [0]

        def sb(name, F, dt, stack=None):
            uniq[0] += 1
            h = (stack or st).enter_context(nc.sbuf_tensor("s%d_%s" % (uniq[0], name), [128, F], dt))
            return T(h, F)

        def DMA(eng, o, i, reads, writes, dep, n=1, ab=False, nobar=False):
            S.add(eng, lambda e, s: e.dma_start(out=o, in_=i).then_inc(s, 16), reads=reads, writes=writes, dma=dep,
                  after_barrier=ab, nobar=nobar)

        def PE(mms, reads, writes):
            def fn(e):
                r = None
                for m in mms:
                    if m[0] == "T":
                        r = e.transpose(m[1], m[2], m[3])
                    else:
                        r = e.matmul(m[0], lhsT=m[1], rhs=m[2], start=m[3], stop=m[4])
                return r
            S.add("pe", fn, reads, writes)

        def ACT(o, i, func, reads, writes, scale=None, bias=None, accum=None):
            kw = {}
            if scale is not None:
                kw["scale"] = scale
            if bias is not None:
                kw["bias"] = bias
            if accum is not None:
                kw["accum_out"] = accum
            S.add("act", lambda e: e.activation(out=o, in_=i, func=func, **kw), reads, writes)

        def TT(o, a, b, op, reads, writes, eng="dve"):
            S.add(eng, lambda e: e.tensor_tensor(out=o, in0=a, in1=b, op=op), reads, writes)

        def TS(o, a, s1, op0, reads, writes, s2=None, op1=None, eng="dve"):
            if op1 is None:
                S.add(eng, lambda e: e.tensor_scalar(out=o, in0=a, scalar1=s1, scalar2=None, op0=op0), reads, writes)
            else:
                S.add(eng, lambda e: e.tensor_scalar(out=o, in0=a, scalar1=s1, scalar2=s2, op0=op0, op1=op1), reads, writes)

        def STT(o, a, sc, b, op0, op1, reads, writes):
            S.add("dve", lambda e: e.scalar_tensor_tensor(out=o, in0=a, scalar=sc, in1=b, op0=op0, op1=op1), reads, writes)

        def CP(o, i, reads, writes, eng="dve"):
            if eng == "act":
                S.add("act", lambda e: e.copy(out=o, in_=i), reads, writes)
            else:
                S.add(eng, lambda e: e.tensor_copy(out=o, in_=i), reads, writes)

        def MEMSET(o, val, writes, eng="dve"):
            S.add(eng, lambda e: e.memset(o, val), (), writes)

        def tap(name, t, F, dt, deps):
            if name not in taps:
                return
            d = nc.dram_tensor("tap_" + name, [128, F], dt, kind="ExternalOutput").ap()
            dep = Dep("tap_" + name)
            DMA("sp", d, t, deps, [dep], dep)
            tapd.append(dep)

        P = [T(st.enter_context(nc.psum_tensor("ps%d" % i, [128, 512], F32)), 512) for i in range(7)]
        DP = [Dep("ps%d" % i) for i in range(7)]
        PT = T(st.enter_context(nc.psum_tensor("pst", [128, 1024], BF16)), 1024)
        DPT = Dep("pst")

        cst = sb("cst", 512, F32)
        cstb = sb("cstb", 512, BF16)
        Dcst, Dcstb = Dep("cst"), Dep("cstb")
        DMA("sp", cst.s(0, 512), I["consts"], [], [Dcst], Dcst)
        DMA("pool", cstb.s(0, 512), I["consts"], [], [Dcstb], Dcstb)
        identb = cstb.s(0, 128)
        blk32 = cst.s(128, 256)
        blkb = cstb.s(128, 256)
        ident32 = cst.s(0, 128)
        mfb, mbb = cstb.s(256, 384), cstb.s(384, 512)
        kc = sb("kc", 8, F32)
        Dkc = Dep("kc")
        MEMSET(kc.s(0, 1), EPS, [Dkc])
        MEMSET(kc.s(1, 2), 1.0, [Dkc])
        MEMSET(kc.s(2, 3), 0.0, [Dkc])
        ones32 = sb("ones32", 128, F32)
        Dones = Dep("ones")
        MEMSET(ones32.s(0, 128), 1.0, [Dones])
        onesb = sb("onesb", 128, BF16)
        MEMSET(onesb.s(0, 128), 1.0, [Dones])
        onesw = sb("onesw", 512, F32)
        MEMSET(onesw.s(0, 512), 1.0, [Dones])

        prm = sb("prm", 128, F32)
        Dprm = Dep("prm")
        PRM = ((0, "cvec", 16), (16, "b_modT", 48), (64, "n1g", 8), (72, "n2g", 8), (80, "balpha", 4),
               (84, "pvec", 3), (88, "cmask", 8))

        def ld_prm(e, s_):
            for (c0, key, n) in PRM:
                e.dma_start(out=prm.s(c0, c0 + n), in_=I[key]).then_inc(s_, 16)
        S.add("sp", ld_prm, [], [Dprm], dma=Dprm, ndma=len(PRM))
        walb = sb("walb", 512, BF16)
        Dwal = Dep("walb")
        DMA("pool", walb.s(0, 512, n=32), I["walpha"], [], [Dwal], Dwal)
        gq, gk, gn = prm.s(84, 85), prm.s(85, 86), prm.s(86, 87)

        NWS = 3
        WS = [sb("ws%d" % i, 4096, BF16) for i in range(NWS)]
        WD = [Dep("ws%d" % i) for i in range(NWS)]
        wctr = [0]

        STG_SPEC = [("w_modl", "w_mod", 1024, 2048, 4096, 128)]
        STG = {}
        stg_chunks = []
        for (sname, src_name, rows, c0_, ncols, rpc) in STG_SPEC:
            t_ = nc.dram_tensor("stg_" + sname, [rows, ncols], BF16).ap()
            deps_ = []
            for r0_ in range(0, rows, rpc):
                r1_ = min(rows, r0_ + rpc)
                d_ = Dep("stg_%s_%d" % (sname, r0_))
                deps_.append((r0_, r1_, d_))
                stg_chunks.append((t_[r0_:r1_, :], I[src_name][r0_:r1_, c0_:c0_ + ncols], d_))
            STG[sname] = (t_, deps_)
        use_stg = [False]

        def stager():
            for (o_, i_, d_) in stg_chunks:
                DMA("pool", o_, i_, [], [d_], d_, nobar=True)
                for _ in range(STG_GAP):
                    yield

        def loadw(name, r0, kt, c0, n):
            i = wctr[0] % len(WS)
            wctr[0] += 1
            reads = []
            if use_stg[0] and name in STG:
                t_, deps_ = STG[name]
                src = t_[r0:r0 + 128 * kt, c0:c0 + n].rearrange("(k p) n -> p k n", p=128)
                reads = [d_ for (a_, b_, d_) in deps_ if a_ < r0 + 128 * kt and b_ > r0]
            else:
                src = I[name][r0:r0 + 128 * kt, c0:c0 + n].rearrange("(k p) n -> p k n", p=128)
            DMA("pool", WS[i].v(0, [[n, kt], [1, n]]), src, reads, [WD[i]], WD[i])
            return WS[i], WD[i], n

        scb = sb("scb", 16, BF16)
        Dscb = Dep("scb")
        ACT(scb.v(0, [[2, 8]]), prm.s(0, 8), AF.Silu, [Dprm], [Dscb])
        ACT(scb.v(1, [[2, 8]]), prm.s(8, 16), AF.Silu, [Dprm], [Dscb])
        modv = sb("modv", 96, F32)
        Dmod = Dep("modv")
        mv = sb("mv", 64, F32)
        Dmv = Dep("mv")
        Dmv2 = Dep("mv2")
        mcol = lambda ch0, j: modv.v(2 * ch0 + j, [[2, 8]])
        gbc = sb("gbc", 2048, F32)
        Dgbc = Dep("gbc")
        dg = sb("dg", 128, F32)
        Ddg = Dep("dg")

        def mod_groups(g0, g1_, dmod):
            for g in range(g0, g1_):
                if use_stg[0] and g >= 4:
                    ws, wd, n = loadw("w_modl", 0, 8, (g - 4) * 512, 512)
                else:
                    ws, wd, n = loadw("w_mod", 0, 8, g * 512, 512)
                mms = []
                for jj in range(4):
                    ch = g * 4 + jj
                    for kt in range(8):
                        mms.append((P[0].s(2 * ch, 2 * ch + 2), ws.v(kt * 512 + jj * 128, [[1, 128]]), scb.s(2 * kt, 2 * kt + 2),
                                    kt == 0, kt == 7))
                PE(mms, [wd, Dscb], [DP[0]])
            c0, c1 = g0 * 4, g1_ * 4
            TT(modv.v(2 * c0, [[2, c1 - c0], [1, 2]]), P[0].v(2 * c0, [[2, c1 - c0], [1, 2]]), prm.v(16 + c0, [[1, c1 - c0], [0, 2]]),
               ALU.add, [DP[0], Dprm], [dmod])

        def mod_early():
            mod_groups(0, 4, Dmod)
            STT(mv.s(0, 8), mcol(8, 0), 1.0, prm.s(64, 72), ALU.add, ALU.mult, [Dmod, Dprm], [Dmv])
            CP(mv.s(8, 16), mcol(0, 0), [Dmod], [Dmv])
            STT(mv.s(16, 24), mcol(8, 1), 1.0, prm.s(64, 72), ALU.add, ALU.mult, [Dmod, Dprm], [Dmv])
            CP(mv.s(24, 32), mcol(0, 1), [Dmod], [Dmv])

        def mod_late():
            Dm2 = Dep("modv2")
            mod_groups(4, 12, Dm2)
            STT(mv.s(32, 40), mcol(32, 0), 1.0, prm.s(72, 80), ALU.add, ALU.mult, [Dm2, Dprm], [Dmv2])
            CP(mv.s(40, 48), mcol(24, 0), [Dm2], [Dmv2])
            CP(mv.s(48, 56), mcol(16, 0), [Dm2], [Dmv2])
            CP(mv.s(56, 64), mcol(40, 0), [Dm2], [Dmv2])
            for gi in range(2):
                dgt, Ddgt = xt[gi], Dxt[gi]
                TT(dgt.v(0, [[128, 8], [1, 128]]), cst.v(0, [[0, 8], [1, 128]]), mv.v(48 + 8 * gi, [[1, 8], [0, 128]]), ALU.mult,
                   [Dcst, Dmv2], [Ddgt])
                for hf in range(2):
                    b_ = 1 + hf
                    PE([(P[b_].s(0, 512), ones32.s(0, 128), dgt.s(hf * 512, hf * 512 + 512), True, True)], [Ddgt, Dones], [DP[b_]])
                    CP(gbc.s(gi * 1024 + hf * 512, gi * 1024 + hf * 512 + 512), P[b_].s(0, 512), [DP[b_]], [Dgbc],
                       eng="act" if hf == 0 else "dve")
            tap("mv", mv.s(0, 64), 64, F32, [Dmv, Dmv2])
            tap("gbc", gbc.s(0, 2048), 2048, F32, [Dgbc])

        mod_early()
        negb = sb("negb", 4, F32)
        Dnegb = Dep("negb")
        TS(negb.s(0, 4), prm.s(80, 84), -1.0, ALU.mult, [Dprm], [Dnegb])

        xt = [sb("xt%d" % i, 1024, F32) for i in range(2)]
        Dxt = [Dep("xt%d" % i) for i in range(2)]
        xs = [sb("xs%d" % i, 1024, BF16) for i in range(2)]
        Dxs = [Dep("xs%d" % i) for i in range(2)]
        nst = [sb("nst%d" % i, 4, F32) for i in range(2)]
        Dnst = [Dep("nst%d" % i) for i in range(2)]

        PTv = [PT, T(P[6].h.bitcast(BF16), 1024)]
        DPTv = [DPT, DP[6]]

        def interleave(*gens):
            gens = list(gens)
            while gens:
                for g_ in list(gens):
                    try:
                        next(g_)
                    except StopIteration:
                        gens.remove(g_)

        import os

        def norm_T_g(x_t, x_dep, gcol, shcol, dst, dst_deps, dcol, slot, dmv):
            xap = x_t.s(0, 1024) if isinstance(x_t, T) else x_t
            i = slot
            ns, dn = nst[i], Dnst[i]
            pt, dpt = PTv[i], DPTv[i]
            ACT(xs[i].s(0, 1024), xap, AF.Square, [x_dep], [dn, Dxs[i]], accum=ns.s(0, 1))
            yield
            ACT(ns.s(1, 2), ns.s(0, 1), AF.Ln, [dn, Dkc], [dn], scale=1.0 / 1024, bias=kc.s(0, 1))
            yield
            ACT(ns.s(2, 3), ns.s(1, 2), AF.Exp, [dn], [dn], scale=-0.5)
            yield
            TS(xs[i].s(0, 1024), xap, ns.s(2, 3), ALU.mult, [x_dep, dn], [Dxs[i]])
            yield
            PE([("T", pt.s(kt * 128, kt * 128 + 128), xs[i].s(kt * 128, kt * 128 + 128), identb) for kt in range(8)],
               [Dxs[i], Dcstb], [dpt])
            yield
            for kt in range(8):
                o = dst.s(kt * dst.F // 8 + dcol, kt * dst.F // 8 + dcol + 128)
                if slot == 0:
                    ACT(o, pt.s(kt * 128, kt * 128 + 128), AF.Identity, [dpt, dmv], [dst_deps[kt]],
                        scale=mv.s(gcol + kt, gcol + kt + 1), bias=mv.s(shcol + kt, shcol + kt + 1))
                else:
                    TS(o, pt.s(kt * 128, kt * 128 + 128), mv.s(gcol + kt, gcol + kt + 1), ALU.mult, [dpt, dmv], [dst_deps[kt]],
                       s2=mv.s(shcol + kt, shcol + kt + 1), op1=ALU.add)
                if kt % 2 == 1:
                    yield

        def norm_tiles(srcs, gcol, shcol, dst, dst_deps, dmv):
            import os
            STEP = 1 if os.environ.get("NOIL") else 2
            for j in range(0, len(srcs), STEP):
                gs = []
                for slot, (xa_, xd_, dc_, ti_) in enumerate(srcs[j:j + STEP]):
                    gs.append(norm_T_g(xa_, xd_, gcol, shcol, dst, dst_deps[ti_ * 8:(ti_ + 1) * 8], dc_, slot, dmv))
                interleave(*gs)

        xctr = [0]

        def load_x(src_rows):
            i = xctr[0] % 2
            xctr[0] += 1
            DMA("sp", xt[i].s(0, 1024), src_rows, [], [Dxt[i]], Dxt[i])
            return xt[i], Dxt[i]

        hTc = sb("hTc", 8 * CTX, BF16)
        DhTc = [Dep("hTc%d" % i_) for i_ in range(16)]
        hT = sb("hT", 8 * 1024, BF16)
        DhT = [Dep("hT%d" % i_) for i_ in range(64)]
        hT2 = sb("hT2", 8 * 1024, BF16)
        DhT2 = [Dep("hT2_%d" % i_) for i_ in range(64)]
        hTb, DhTb = [hT, hT2], [DhT, DhT2]
        o_loc = sb("o_loc", 4 * TOK, BF16)
        Doloc = [Dep("oloc%d" % a) for a in range(NSEG)]
        qsd = sb("qsd", 4 * TOK, BF16)
        Dqsd = [Dep("qsd%d" % a) for a in range(NSEG)]
        NSUM = NSEG + 1
        Sentbf = sb("Sentbf", NSEG * 2 * 256, BF16)
        DSent = Dep("Sentbf")
        xsc = ExitStack()
        Useg = sb("Useg", NSUM * 2 * 256, F32, xsc)
        Aseg = sb("Aseg", NSUM * 2 * 2, F32, xsc)
        Dsum = [Dep("sum%d" % a) for a in range(NSUM)]

        srcs = []
        for tci in range(2):
            x_t, x_d = load_x(I["xc"][tci * 128:(tci + 1) * 128, :])
            srcs.append((x_t, x_d, tci * 128, tci))
        norm_tiles(srcs, 16, 24, hTc, DhTc, Dmv)
        tap("hTc", hTc.s(0, 8 * CTX), 8 * CTX, BF16, DhTc)
        if stop == "mod":
            S.add("sp", None, reads=tapd)
            S.emit(st)
            return nc

        pa = ExitStack()
        ropeC = sb("ropeC", SA, F32, pa)
        ropeS = sb("ropeS", SA, F32, pa)
        DropeC, DropeS = Dep("ropeC"), Dep("ropeS")
        qr = sb("qr", SA, F32, pa)
        kr = sb("kr", SA, F32, pa)
        Dqr, Dkr = Dep("qr"), Dep("kr")
        tmpf = sb("tmpf", SA, F32, pa)
        Dtmpf = Dep("tmpf")
        vg = sb("vg", 2 * 4 * 512, BF16, pa)
        Dvg = [Dep("vg0"), Dep("vg1")]
        zl = sb("zl", SA, BF16, pa)
        Dzl = Dep("zl")
        la = [sb("la%d" % i_, SA, F32, pa) for i_ in range(2)]
        Dla = [Dep("la0"), Dep("la1")]
        pbuf = [sb("pbuf%d" % i_, SA + 1, F32, pa) for i_ in range(2)]
        Dpb = [Dep("pbuf0"), Dep("pbuf1")]
        for i_ in range(2):
            MEMSET(pbuf[i_].s(0, 1), 0.0, [Dpb[i_]])
        bt = [sb("bt%d" % i_, SA, F32, pa) for i_ in range(2)]
        Dbt = [Dep("bt0"), Dep("bt1")]
        E2 = [sb("E2_%d" % i_, SA, BF16, pa) for i_ in range(2)]
        DE2 = [Dep(), Dep()]
        E1, DE1 = la, Dla
        E3, DE3 = bt, Dbt
        qdec = sb("qdec", 2 * 4 * SA, BF16, pa)
        kinv = sb("kinv", 2 * 4 * SA, BF16, pa)
        Dqdec = [[Dep("qdec%d" % i) for i in range(4)] for _ in range(2)]
        Dkinv = [[Dep("kinv%d" % i) for i in range(4)] for _ in range(2)]
        kend = [sb("kend%d" % i_, SA, BF16, pa) for i_ in range(2)]
        Dkend = [Dep("kend0"), Dep("kend1")]
        kendtm = sb("kendtm", 2 * 4 * 512, BF16, pa)
        Dkendtm = [[Dep("kendtm%d" % i_) for i_ in range(4)] for _ in range(2)]
        decs = sb("decs", 2 * 4 * 8, F32, pa)
        Ddecs = [[Dep("decs%d" % i_) for i_ in range(4)] for _ in range(2)]
        dtmp = [sb("dtmp%d" % i_, 8, F32, pa) for i_ in range(2)]
        Ddtmp = [Dep("dtmp0"), Dep("dtmp1")]
        Sst = sb("Sst", 2 * 256, F32, pa)
        DSst = [Dep("Sst0"), Dep("Sst1")]
        Stmp = sb("Stmp", 2 * 256, F32, pa)
        DStmp = [Dep("Stmp0"), Dep("Stmp1")]
        Sbf = sb("Sbf", 2 * 8 * 256, BF16, pa)
        DSbf = Dep("Sbf")
        Abf = [sb("Abf%d" % i, 256, BF16, pa) for i in range(2)]
        DAbf = [Dep("Abf0"), Dep("Abf1")]
        onesA = onesw

        def roundrobin(gens):
            gens = list(gens)
            while gens:
                for g_ in list(gens):
                    try:
                        next(g_)
                        yield
                    except StopIteration:
                        gens.remove(g_)

        def pa_front(N, hsrc, hcol0, hdep, hstride, tok0, sidx, need_q, pp):
            NCH = N // 64
            NT = N // 128
            hk = lambda kt: hsrc.s(kt * hstride + hcol0, kt * hstride + hcol0 + N)
            DMA("sp", ropeC.s(0, N), I["ropeC"][:, tok0:tok0 + N], [], [DropeC], DropeC)
            DMA("sp", ropeS.s(0, N), I["ropeS"][:, tok0:tok0 + N], [], [DropeS], DropeS)
            ws, wd, n = loadw("wv_g", 0, 8, 0, 512)
            for tt in range(NT):
                b = 5 + (tt % 2)
                PE([(P[b].s(0, 512), hsrc.s(kt * hstride + hcol0 + tt * 128, kt * hstride + hcol0 + tt * 128 + 128),
                     ws.s(kt * 512, kt * 512 + 512), kt == 0, kt == 7) for kt in range(8)], [wd] + hdep, [DP[b]])
                yield
                CP(vg.s(pp * 2048 + tt * 512, pp * 2048 + tt * 512 + 512), P[b].s(0, 512), [DP[b]], [Dvg[pp]], eng="act")
                yield
            ws, wd, n = loadw("w_lr", 0, 8, 0, 32)
            PE([(P[5].s(0, N, n=32), ws.v(kt * 32, [[1, 32]]), hk(kt), kt == 0, kt == 7) for kt in range(8)],
               [wd] + hdep, [DP[5]])
            yield
            CP(zl.s(0, N, n=32), P[5].s(0, N, n=32), [DP[5]], [Dzl], eng="act")
            yield
            wsg, wdg, n = loadw("wg1", 0, 8, 0, 512)
            wsk, wdk, n = loadw("wg1", 0, 8, 512, 512)

            def chain(hp, dr):
                dh = dr * 2 + hp
                la_, pb_, bt_, e1, e2, e3, dt_, ke_ = la[dr], pbuf[dr], bt[dr], E1[dr], E2[dr], E3[dr], dtmp[dr], kend[dr]
                zb = 5 + dr
                PE([(P[zb].s(0, N), walb.v(dh * 128, [[1, 128]], n=32), zl.s(0, N, n=32), True, True)],
                   [Dwal, Dzl], [DP[zb]])
                yield
                ACT(la_.s(0, N), P[zb].s(0, N), AF.Exp, [DP[zb], Dnegb], [Dla[dr]], scale=-1.0, bias=negb.s(dh, dh + 1))
                yield
                ACT(la_.s(0, N), la_.s(0, N), AF.Ln, [Dkc], [Dla[dr]], bias=kc.s(1, 2))
                yield
                TS(la_.s(0, N), la_.s(0, N), -1.0 / 16.0, ALU.mult, [], [Dla[dr]], s2=-1.0, op1=ALU.max)
                yield
                S.add("dve", lambda e: e.tensor_tensor_scan(out=pb_.s(1, N + 1), data0=onesA.s(0, N), data1=la_.s(0, N),
                                                            initial=0.0, op0=ALU.mult, op1=ALU.add),
                      [Dla[dr], Dones], [Dpb[dr]])
                yield
                if dr == 0:
                    TT(bt_.v(0, [[64, NCH], [1, 64]]), pb_.v(1, [[64, NCH], [1, 64]]), pb_.v(0, [[64, NCH], [0, 64]]),
                       ALU.subtract, [Dpb[dr]], [Dbt[dr]])
                else:
                    TT(bt_.v(0, [[64, NCH], [1, 64]]), pb_.v(64, [[64, NCH], [0, 64]]), pb_.v(0, [[64, NCH], [1, 64]]),
                       ALU.subtract, [Dpb[dr]], [Dbt[dr]])
                yield
                if hp == 0 and sidx == 0 and dr == 0:
                    tap("bt", bt_.s(0, N), N, F32, [Dbt[dr]])
                ACT(e1.s(0, N), bt_.s(0, N), AF.Exp, [Dbt[dr], Dla[dr]], [DE1[dr]])
                yield
                ACT(e2.s(0, N), bt_.s(0, N), AF.Exp, [Dbt[dr]], [DE2[dr]], scale=-1.0)
                yield
                if dr == 0:
                    ACT(e3.s(0, N), pb_.s(1, N + 1), AF.Exp, [Dpb[dr]], [DE3[dr]])
                else:
                    ACT(e3.s(0, N), pb_.s(0, N), AF.Exp, [Dpb[dr]], [DE3[dr]], scale=-1.0, bias=pb_.s(N, N + 1))
                yield
                if need_q:
                    STT(qdec.s((pp * 4 + dh) * SA, (pp * 4 + dh) * SA + N), qr.s(0, N), 0.125, e1.s(0, N), ALU.mult, ALU.mult, [Dqr, DE1[dr]],
                        [Dqdec[pp][dh]])
                    yield
                    STT(qsd.s(dh * TOK + tok0, dh * TOK + tok0 + N), qr.s(0, N), 0.125, e3.s(0, N), ALU.mult, ALU.mult,
                        [Dqr, DE3[dr]], [Dqsd[sidx]])
                    yield
                TT(kinv.s((pp * 4 + dh) * SA, (pp * 4 + dh) * SA + N), kr.s(0, N), e2.s(0, N), ALU.mult, [Dkr, DE2[dr]], [Dkinv[pp][dh]])
                yield
                TT(dt_.s(0, NCH), pb_.v(64, [[64, NCH]]), pb_.v(0, [[64, NCH]]), ALU.subtract, [Dpb[dr]], [Ddtmp[dr]])
                yield
                ACT(decs.s(pp * 32 + dh * 8, pp * 32 + dh * 8 + NCH), dt_.s(0, NCH), AF.Exp, [Ddtmp[dr]], [Ddecs[pp][dh]])
                yield
                ACT(Aseg.s((sidx * 2 + dr) * 2 + hp, (sidx * 2 + dr) * 2 + hp + 1), pb_.s(N, N + 1), AF.Exp, [Dpb[dr]],
                    [Dsum[sidx]])
                yield
                TT(ke_.v(0, [[64, NCH], [1, 64]]), kinv.v((pp * 4 + dh) * SA, [[64, NCH], [1, 64]]),
                   decs.v(pp * 32 + dh * 8, [[1, NCH], [0, 64]]), ALU.mult, [Dkinv[pp][dh], Ddecs[pp][dh]], [Dkend[dr]])
                yield
                pt, dpt = T(P[zb].h.bitcast(BF16), 1024), DP[zb]
                PE([("T", pt.s(tt * 128, tt * 128 + 128), ke_.s(tt * 128, tt * 128 + 128), identb) for tt in range(NT)],
                   [Dkend[dr], Dcstb], [dpt])
                yield
                CP(kendtm.v(pp * 2048 + dh * 128, [[512, NT], [1, 128]]), pt.v(0, [[128, NT], [1, 128]]), [dpt], [Dkendtm[pp][dh]],
                   eng="act" if dr == 0 else "dve")
                yield

            for hp in range(2):
                Cs = ropeC.s(0, N)
                Ss = ropeS.s(0, N)
                for (wsx, wdx, dst, ddst, bk, need) in ((wsg, wdg, qr, Dqr, 3, need_q), (wsk, wdk, kr, Dkr, 3, True)):
                    if not need:
                        continue
                    PE([(P[bk].s(0, N), wsx.v(kt * 512 + hp * 128, [[1, 128]]), hk(kt), kt == 0, kt == 7) for kt in range(8)],
                       [wdx] + hdep, [DP[bk]])
                    PE([(P[bk + 1].s(0, N), wsx.v(kt * 512 + 256 + hp * 128, [[1, 128]]), hk(kt), kt == 0, kt == 7)
                        for kt in range(8)], [wdx] + hdep, [DP[bk + 1]])
                    yield
                    TT(tmpf.s(0, N), P[bk].s(0, N), Cs, ALU.mult, [DP[bk], DropeC], [Dtmpf])
                    yield
                    TT(dst.s(0, N), P[bk + 1].s(0, N), Ss, ALU.mult, [DP[bk + 1], DropeS], [ddst])
                    yield
                    TT(dst.s(0, N), dst.s(0, N), tmpf.s(0, N), ALU.add, [Dtmpf], [ddst])
                    yield
                if hp == 0 and sidx == 0:
                    tap("qr", qr.s(0, N), N, F32, [Dqr])
                    tap("kr", kr.s(0, N), N, F32, [Dkr])
                for _ in roundrobin([chain(hp, 0), chain(hp, 1)]):
                    yield

        def pa_back(N, tok0, sidx, need_q, pp):
            NCH = N // 64
            NT = N // 128
            for dr in range(2):
                MEMSET(Sst.s(dr * 256, dr * 256 + 256), 0.0, [DSst[dr]])
            for nstep in range(NCH):
                for dr in range(2):
                    cn = nstep if dr == 0 else NCH - 1 - nstep
                    tt, half = divmod(cn, 2)
                    bk = 0
                    mms = []
                    for hp in range(2):
                        for hh in range(2):
                            h = 2 * hp + hh
                            mms.append((P[bk].s(dr * 256 + hp * 128, dr * 256 + hp * 128 + 128, p0=64 * hh, n=64),
                                        kendtm.s(pp * 2048 + tt * 512 + (dr * 2 + hp) * 128 + 64 * hh, pp * 2048 + tt * 512 + (dr * 2 + hp) * 128 + 64 * hh + 64,
                                                 p0=64 * half, n=64),
                                        vg.s(pp * 2048 + tt * 512 + h * 128, pp * 2048 + tt * 512 + h * 128 + 128, p0=64 * half, n=64), True, True))
                    PE(mms, Dkendtm[pp] + [Dvg[pp]], [DP[bk]])
                    yield
                    TT(Stmp.v(dr * 256, [[128, 2], [1, 128]]), Sst.v(dr * 256, [[128, 2], [1, 128]]),
                       decs.v(pp * 32 + dr * 16 + cn, [[8, 2], [0, 128]]), ALU.mult, [DSst[dr]] + Ddecs[pp], [DStmp[dr]])
                    TT(Sst.s(dr * 256, dr * 256 + 256), Stmp.s(dr * 256, dr * 256 + 256), P[bk].s(dr * 256, dr * 256 + 256),
                       ALU.add, [DStmp[dr], DP[bk]], [DSst[dr]])
                    yield
                    nxt = cn + 1 if dr == 0 else cn - 1
                    if 0 <= nxt < NCH and need_q:
                        CP(Sbf.s((dr * 8 + nxt) * 256, (dr * 8 + nxt) * 256 + 256), Sst.s(dr * 256, dr * 256 + 256),
                           [DSst[dr]], [DSbf], eng="act")
            for dr in range(2):
                CP(Useg.s((sidx * 2 + dr) * 256, (sidx * 2 + dr) * 256 + 256), Sst.s(dr * 256, dr * 256 + 256), [DSst[dr]],
                   [Dsum[sidx]])
            if not need_q:
                return
            for h in range(4):
                hp, hh = divmod(h, 2)
                ob = 2
                for tt in range(NT):
                    cs = slice(tt * 128, tt * 128 + 128)
                    ab = 1
                    ao = (tt % 2) * 256
                    mms = []
                    for dr in range(2):
                        dh = dr * 2 + hp
                        mms.append((P[ab].s(ao + dr * 128, ao + dr * 128 + 128),
                                    kinv.s((pp * 4 + dh) * SA + tt * 128, (pp * 4 + dh) * SA + tt * 128 + 128, p0=64 * hh, n=64),
                                    qdec.s((pp * 4 + dh) * SA + tt * 128, (pp * 4 + dh) * SA + tt * 128 + 128, p0=64 * hh, n=64), True, True))
                    PE(mms, [Dkinv[pp][hp], Dkinv[pp][2 + hp], Dqdec[pp][hp], Dqdec[pp][2 + hp]], [DP[ab]])
                    yield
                    ai = tt % 2
                    TT(Abf[ai].s(0, 128), P[ab].s(ao, ao + 128), mfb, ALU.mult, [DP[ab], Dcstb], [DAbf[ai]])
                    TT(Abf[ai].s(128, 256), P[ab].s(ao + 128, ao + 256), mbb, ALU.mult, [DP[ab], Dcstb], [DAbf[ai]])
                    mms = [(P[ob].s(tt * 128, tt * 128 + 128), vg.s(pp * 2048 + tt * 512 + h * 128, pp * 2048 + tt * 512 + h * 128 + 128),
                            Abf[ai].s(0, 128), True, False),
                           (P[ob].s(tt * 128, tt * 128 + 128), vg.s(pp * 2048 + tt * 512 + h * 128, pp * 2048 + tt * 512 + h * 128 + 128),
                            Abf[ai].s(128, 256), False, False)]
                    for cn in (2 * tt, 2 * tt + 1):
                        for dr in range(2):
                            if (dr == 0 and cn == 0) or (dr == 1 and cn == NCH - 1):
                                continue
                            dh = dr * 2 + hp
                            mms.append((P[ob].s(cn * 64, cn * 64 + 64),
                                        Sbf.s((dr * 8 + cn) * 256 + hp * 128, (dr * 8 + cn) * 256 + hp * 128 + 128, p0=64 * hh, n=64),
                                        qdec.s((pp * 4 + dh) * SA + cn * 64, (pp * 4 + dh) * SA + cn * 64 + 64, p0=64 * hh, n=64), False, False))
                    mms[-1] = mms[-1][:4] + (True,)
                    PE(mms, [Dvg[pp], DAbf[ai], DSbf, Dqdec[pp][hp], Dqdec[pp][2 + hp]], [DP[ob]])
                    yield
                CP(o_loc.s(h * TOK + tok0, h * TOK + tok0 + N), P[ob].s(0, N), [DP[ob]], [Doloc[sidx]], eng="act")

        def drain(g_):
            for _ in g_:
                pass

        def prep_a(a):
            hb = a % 2
            for t2 in range(SA // 128):
                x_t, x_d = load_x(I["xa"][a * SA + t2 * 128: a * SA + t2 * 128 + 128, :])
                ti_ = hb * 4 + t2
                for _ in norm_T_g(x_t, x_d, 0, 8, hT, DhT[ti_ * 8:(ti_ + 1) * 8], hb * 512 + t2 * 128, 0, Dmv):
                    yield

        drain(roundrobin([pa_front(CTX, hTc, 0, DhTc, CTX, TOK, NSEG, False, 0), prep_a(0)]))

        def fr(a):
            hb = a % 2
            return pa_front(SA, hT, hb * 512, DhT[hb * 32:hb * 32 + 32], 1024, a * SA, a, True, (a + 1) % 2)

        bg = stager()

        def drain_bg(gens):
            for _ in roundrobin(gens):
                try:
                    next(bg)
                except StopIteration:
                    pass

        gens = [pa_back(CTX, TOK, NSEG, False, 0), fr(0)]
        if NSEG > 1:
            gens.append(prep_a(1))
        drain_bg(gens)
        for a in range(NSEG):
            gens = [pa_back(SA, a * SA, a, True, (a + 1) % 2)]
            if a + 1 < NSEG:
                gens.append(fr(a + 1))
            if a + 2 < NSEG:
                gens.append(prep_a(a + 2))
            drain_bg(gens)
        drain(bg)
        use_stg[0] = True
        tap("oloc", o_loc.s(0, 4 * TOK), 4 * TOK, BF16, Doloc)
        tap("qsd", qsd.s(0, 4 * TOK), 4 * TOK, BF16, Dqsd)
        tap("Useg", Useg.s(0, NSUM * 512), NSUM * 512, F32, Dsum)
        tap("Aseg", Aseg.s(0, NSUM * 4), NSUM * 4, F32, Dsum)
        S.barrier()
        pa.close()
        if stop in ("a0", "a"):
            xsc.close()
            S.add("sp", None, reads=tapd)
            S.emit(st)
            return nc

        snd = sb("snd", 520, F32, xsc)
        gath = sb("gath", 4 * 520, F32, xsc)
        Dsnd, Dgath = Dep("snd"), Dep("gath")
        Dccin, Dccout = Dep("ccin"), Dep("ccout")
        U_ = lambda a, d: Useg.s((a * 2 + d) * 256, (a * 2 + d) * 256 + 256)
        U3 = lambda a, d: Useg.v((a * 2 + d) * 256, [[128, 2], [1, 128]])
        Ab = lambda a, d: Aseg.v((a * 2 + d) * 2, [[1, 2], [0, 128]])
        A2 = lambda a, d: Aseg.s((a * 2 + d) * 2, (a * 2 + d) * 2 + 2)
        MEMSET(snd.s(512, 520), 0.0, [Dsnd])
        CP(snd.s(0, 256), U_(0, 0), Dsum, [Dsnd])
        CP(snd.s(512, 514), A2(0, 0), Dsum, [Dsnd])
        for a in range(1, NSEG):
            TT(snd.v(0, [[128, 2], [1, 128]]), snd.v(0, [[128, 2], [1, 128]]), Ab(a, 0), ALU.mult, Dsum, [Dsnd])
            TT(snd.s(0, 256), snd.s(0, 256), U_(a, 0), ALU.add, Dsum, [Dsnd])
            TT(snd.s(512, 514), snd.s(512, 514), A2(a, 0), ALU.mult, Dsum, [Dsnd])
        CP(snd.s(256, 512), U_(NSEG - 1, 1), Dsum, [Dsnd])
        CP(snd.s(514, 516), A2(NSEG - 1, 1), Dsum, [Dsnd])
        for a in range(NSEG - 2, -1, -1):
            TT(snd.v(256, [[128, 2], [1, 128]]), snd.v(256, [[128, 2], [1, 128]]), Ab(a, 1), ALU.mult, Dsum, [Dsnd])
            TT(snd.s(256, 512), snd.s(256, 512), U_(a, 1), ALU.add, Dsum, [Dsnd])
            TT(snd.s(514, 516), snd.s(514, 516), A2(a, 1), ALU.mult, Dsum, [Dsnd])
        DMA("sp", cc_in.ap(), snd.s(0, 520), [Dsnd], [Dccin], Dccin)
        S.add("pool", lambda e, s_: e.collective_compute("AllGather", ALU.bypass, replica_groups=[[0, 1, 2, 3], [4, 5, 6, 7]],
                                                         ins=[cc_in.ap().opt()], outs=[cc_out.ap().opt()]).then_inc(s_, 1),
              reads=[Dccin], writes=[Dccout], dma=Dccout, inc=1)
        DMA("sp", gath.v(0, [[520, 4], [1, 520]]), cc_out.ap().rearrange("(r p) c -> p r c", p=128), [Dccout], [Dgath], Dgath)
        mod_late()
        Sc = sb("Sc", 2 * 256, F32, xsc)
        Sct2 = sb("Sct", 2 * 256, F32, xsc)
        DSc, DSct = [Dep("Sc0"), Dep("Sc1")], [Dep("Sct0"), Dep("Sct1")]
        DSent2 = [Dep("Sent0"), Dep("Sent1")]

        def compose(d):
            Sd = Sc.s(d * 256, d * 256 + 256)
            Sd3 = Sc.v(d * 256, [[128, 2], [1, 128]])
            St = Sct2.s(d * 256, d * 256 + 256)
            St3 = Sct2.v(d * 256, [[128, 2], [1, 128]])
            CP(Sd, U_(NSEG, d), Dsum, [DSc[d]])
            yield
            order = range(4) if d == 0 else range(3, -1, -1)
            for p in order:
                TT(St3, Sd3, gath.v(p * 520 + 512 + 2 * d, [[1, 2], [0, 128]]), ALU.mult, [DSc[d], Dgath], [DSct[d]])
                yield
                TT(St, St, gath.s(p * 520 + d * 256, p * 520 + d * 256 + 256), ALU.add, [Dgath], [DSct[d]])
                yield
                TT(St, St, Sd, ALU.subtract, [DSc[d]], [DSct[d]])
                yield
                STT(Sd, St, prm.s(88 + 4 * d + p, 88 + 4 * d + p + 1), Sd, ALU.mult, ALU.add, [DSct[d], Dprm], [DSc[d]])
                yield
            segs = range(NSEG) if d == 0 else range(NSEG - 1, -1, -1)
            for a in segs:
                CP(Sentbf.s((a * 2 + d) * 256, (a * 2 + d) * 256 + 256), Sd, [DSc[d]], [DSent2[d]], eng="act")
                yield
                TT(Sd3, Sd3, Ab(a, d), ALU.mult, Dsum, [DSc[d]])
                yield
                TT(Sd, Sd, U_(a, d), ALU.add, Dsum, [DSc[d]])
                yield

        drain(roundrobin([compose(0), compose(1)]))
        tap("Sent", Sentbf.s(0, NSEG * 512), NSEG * 512, BF16, DSent2)
        S.barrier()
        xsc.close()
        if stop == "x":
            S.add("sp", None, reads=tapd)
            S.emit(st)
            return nc

        pb = ExitStack()
        WS.append(sb("ws3", 4096, BF16, pb))
        WD.append(Dep("ws3"))
        kctx = sb("kctx", 4 * CTX, BF16, pb)
        vctx = sb("vctx", 2 * 520, BF16, pb)
        Dvctx = Dep("vctx")
        gq8 = sb("gq8", 1, F32, pb)
        Dgq8 = Dep("gq8")
        TS(gq8.s(0, 1), gq, 0.125, ALU.mult, [Dprm], [Dgq8])
        PTf = T(PT.h.bitcast(F32), 512)

        def make_qk_norm(stack, nslot=2):
            sqb = [sb("sqb%d" % i_, 512, BF16, stack) for i_ in range(nslot)]
            rsb = [sb("rsb%d" % i_, 512, F32, stack) for i_ in range(nslot)]
            Dsqb, Drsb = [Dep() for _ in range(nslot)], [Dep() for _ in range(nslot)]
            if nslot == 2:
                banks = [(P[0], DP[0], P[2], DP[2]), (P[1], DP[1], P[3], DP[3])]
            else:
                banks = [(P[0], DP[0], P[4], DP[4]), (P[1], DP[1], P[5], DP[5]), (P[2], DP[2], P[6], DP[6]),
                         (P[3], DP[3], PTf, DPT)]

            def qk_norm_g(proj_fn, slot, N, g_ap, g_dep, dst_ap, dst_dep):
                pb_, dpb_, sb_, dsb_ = banks[slot]
                proj_fn(pb_, dpb_)
                yield
                ACT(sqb[slot].s(0, N), pb_.s(0, N), AF.Square, [dpb_], [Dsqb[slot]])
                yield
                PE([(sb_.s(0, N), blkb, sqb[slot].s(0, N), True, True)], [Dsqb[slot], Dcstb], [dsb_])
                yield
                ACT(rsb[slot].s(0, N), sb_.s(0, N), AF.Ln, [dsb_, Dkc], [Drsb[slot]], scale=1.0 / 64, bias=kc.s(0, 1))
                yield
                ACT(rsb[slot].s(0, N), rsb[slot].s(0, N), AF.Exp, [], [Drsb[slot]], scale=-0.5)
                yield
                STT(dst_ap, pb_.s(0, N), g_ap, rsb[slot].s(0, N), ALU.mult, ALU.mult, [dpb_, Drsb[slot], g_dep], [dst_dep])
                yield
            return qk_norm_g

        def run_groups(gens, k):
            for j in range(0, len(gens), k):
                drain(roundrobin(gens[j:j + k]))

        def run_pairs(gens):
            for j in range(0, len(gens), 2):
                drain(roundrobin(gens[j:j + 2]))

        MEMSET(vctx.s(0, 2 * 520), 1.0, [Dvctx])
        scs = ExitStack()
        qk_norm_g = make_qk_norm(scs, 4)
        ws, wd, n = loadw("wk_na", 0, 8, 0, 512)
        gens = []
        Dkctx_l = [Dep("kctx%d" % i_) for i_ in range(4)]
        for hp in range(4):
            def pj(pb_, dpb_, hp=hp, ws=ws, wd=wd):
                PE([(pb_.s(0, CTX), ws.v(kt * 512 + hp * 128, [[1, 128]]), hTc.s(kt * CTX, kt * CTX + CTX), kt == 0, kt == 7)
                    for kt in range(8)], [wd] + DhTc, [dpb_])
            gens.append(qk_norm_g(pj, hp % 4, CTX, gk, Dprm, kctx.s(hp * CTX, hp * CTX + CTX), Dkctx_l[hp]))
        run_groups(gens, 4)
        ws, wd, n = loadw("wv_na", 0, 8, 0, 512)
        for tci in range(2):
            b = 4 + tci
            PE([(P[b].s(0, 512), hTc.s(kt * CTX + tci * 128, kt * CTX + tci * 128 + 128), ws.s(kt * 512, kt * 512 + 512),
                 kt == 0, kt == 7) for kt in range(8)], [wd] + DhTc, [DP[b]])
            CP(vctx.v(tci * 520, [[65, 8], [1, 64]]), P[b].v(0, [[64, 8], [1, 64]]), [DP[b]], [Dvctx], eng="act")
        S.barrier()
        scs.close()
        tap("kctx", kctx.s(0, 4 * CTX), 4 * CTX, BF16, Dkctx_l)
        tap("vctx", vctx.s(0, 1040), 1040, BF16, [Dvctx])

        ost = [sb("ost%d" % i, 512, F32, pb) for i in range(2)]
        Dost = [Dep("ost0"), Dep("ost1")]
        Dout = [Dep("out0"), Dep("out1")]
        octr = [0]
        SPEC = {(0, 0): 1, (0, 1): 2, (NSEG - 1, 2): 3, (NSEG - 1, 3): 4}

        def halo_prep(sn):
            hTn, DhTn = hTb[sn % 2], DhTb[sn % 2]
            for tt in range(0, 8, 2):
                gs = []
                for slot, t2 in enumerate((tt, tt + 1)):
                    x_t, x_d = load_x(I["xh"][sn * 1024 + t2 * 128: sn * 1024 + t2 * 128 + 128, :])
                    gs.append(norm_T_g(x_t, x_d, 0, 8, hTn, DhTn[t2 * 8:(t2 + 1) * 8], t2 * 128, slot, Dmv))
                for _ in roundrobin(gs):
                    yield

        for s_i in range(NSEG):
            sg = ExitStack()
            o_naT = sb("o_naT", 4 * 512, BF16, sg)
            Do_naT = Dep("o_naT")
            hT, DhT = hTb[s_i % 2], DhTb[s_i % 2]
            if s_i == 0:
                drain(halo_prep(0))
            hown = lambda kt, hT=hT: hT.s(kt * 1024 + 256, kt * 1024 + 768)
            b1 = ExitStack()
            tabI = sb("tabI", 5120, BF16, b1)
            tabS = sb("tabS", 5120, BF16, b1)
            DtabI, DtabS = Dep("tabI"), Dep("tabS")
            DMA("pool", tabI.s(0, 5120), I["btab"][0], [], [DtabI], DtabI, ab=True)
            kn = sb("kn", 4 * 1024, BF16, b1)
            qn = sb("qn", 4 * 512, BF16, b1)
            vaug = sb("vaug", 8 * 520, BF16, b1)
            Dkn = [Dep("kn%d" % i_) for i_ in range(8)]
            Dqn = [Dep("qn%d" % i_) for i_ in range(4)]
            Dvaug = Dep("vaug")
            ptA = [sb("ptA%d" % i, 512, BF16, b1) for i in range(2)]
            ptB = [sb("ptB%d" % i, 384, BF16, b1) for i in range(2)]
            DptA = [Dep("ptA0"), Dep("ptA1")]
            DptB = [Dep("ptB0"), Dep("ptB1")]
            otm = sb("otm", 512, BF16, b1)
            Dotm = Dep("otm")
            rcp = sb("rcp", 8, F32, b1)
            Drcp = Dep("rcp")
            MEMSET(vaug.s(0, 8 * 520), 1.0, [Dvaug])
            qk_norm_g = make_qk_norm(b1, 4)
            ws, wd, n = loadw("wq_na", 0, 8, 0, 512)
            gens = []
            for hp in range(4):
                def pj(pb_, dpb_, hp=hp, ws=ws, wd=wd):
                    PE([(pb_.s(0, 512), ws.v(kt * 512 + hp * 128, [[1, 128]]), hown(kt), kt == 0, kt == 7) for kt in range(8)],
                       [wd] + DhT, [dpb_])
                gens.append(qk_norm_g(pj, hp % 4, 512, gq8.s(0, 1), Dgq8, qn.s(hp * 512, hp * 512 + 512), Dqn[hp]))
            ws, wd, n = loadw("wk_na", 0, 8, 0, 512)
            for hp in range(4):
                for tg in range(2):
                    def pj(pb_, dpb_, hp=hp, tg=tg, ws=ws, wd=wd):
                        PE([(pb_.s(0, 512), ws.v(kt * 512 + hp * 128, [[1, 128]]),
                             hT.s(kt * 1024 + tg * 512, kt * 1024 + tg * 512 + 512), kt == 0, kt == 7) for kt in range(8)],
                           [wd] + DhT, [dpb_])
                    gens.append(qk_norm_g(pj, (hp * 2 + tg) % 4, 512, gk, Dprm, kn.s(hp * 1024 + tg * 512, hp * 1024 + tg * 512 + 512),
                                          Dkn[hp * 2 + tg]))
            run_groups(gens, 4)
            ws, wd, n = loadw("wv_na", 0, 8, 0, 512)
            for tt in range(8):
                b = 4 + tt % 2
                PE([(P[b].s(0, 512), hT.s(kt * 1024 + tt * 128, kt * 1024 + tt * 128 + 128), ws.s(kt * 512, kt * 512 + 512),
                     kt == 0, kt == 7) for kt in range(8)], [wd] + DhT, [DP[b]])
                CP(vaug.v(tt * 520, [[65, 8], [1, 64]]), P[b].v(0, [[64, 8], [1, 64]]), [DP[b]], [Dvaug], eng="act")
            if s_i == 0:
                tap("qn", qn.s(0, 2048), 2048, BF16, Dqn)
                tap("kn", kn.s(0, 4096), 4096, BF16, Dkn)
                tap("vaug", vaug.s(0, 8 * 520), 8 * 520, BF16, [Dvaug])
            for i in range(4):
                if stop == "b1a" or (stop in ("b1b", "b1c", "b1d", "b1e") and i == 1):
                    break
                sp_ = SPEC.get((s_i, i))
                if sp_ is None:
                    tab, Dtab = tabI, DtabI
                else:
                    DMA("pool", tabS.s(0, 5120), I["btab"][sp_], [], [DtabS], DtabS, ab=True)
                    tab, Dtab = tabS, DtabS

                def ST(h, i=i, tab=tab, Dtab=Dtab):
                    hp, hh = divmod(h, 2)
                    ba, bb_ = (0, 1) if h % 2 == 0 else (2, 3)
                    qap = qn.s(hp * 512 + i * 128, hp * 512 + i * 128 + 128, p0=64 * hh, n=64)
                    mms = []
                    for t_ in range(4):
                        o_ = P[ba].s(t_ * 128, t_ * 128 + 128)
                        mms.append((o_, kn.s(hp * 1024 + (i + t_) * 128, hp * 1024 + (i + t_) * 128 + 128, p0=64 * hh, n=64), qap, True, False))
                        mms.append((o_, identb, tab.s(h * 640 + t_ * 128, h * 640 + t_ * 128 + 128), False, True))
                    PE(mms, Dkn + Dqn + [Dtab, Dcstb], [DP[ba]])
                    mms = []
                    o_ = P[bb_].s(0, 128)
                    mms.append((o_, kn.s(hp * 1024 + (i + 4) * 128, hp * 1024 + (i + 4) * 128 + 128, p0=64 * hh, n=64), qap, True, False))
                    mms.append((o_, identb, tab.s(h * 640 + 512, h * 640 + 640), False, True))
                    for c_ in range(2):
                        mms.append((P[bb_].s(128 + c_ * 128, 256 + c_ * 128),
                                    kctx.s(hp * CTX + c_ * 128, hp * CTX + c_ * 128 + 128, p0=64 * hh, n=64), qap, True, True))
                    PE(mms, Dkn + Dqn + [Dtab, Dcstb] + Dkctx_l, [DP[bb_]])

                def EXPV(h, i=i):
                    j = h % 2
                    ba, bb_ = (0, 1) if h % 2 == 0 else (2, 3)
                    ACT(ptA[j].s(0, 512), P[ba].s(0, 512), AF.Exp, [DP[ba]], [DptA[j]])
                    ACT(ptB[j].s(0, 384), P[bb_].s(0, 384), AF.Exp, [DP[bb_]], [DptB[j]])
                    ob = 4 + h // 4
                    oc = (h % 4) * 65
                    o_ = P[ob].s(oc, oc + 65)
                    mms = []
                    for t_ in range(4):
                        mms.append((o_, ptA[j].s(t_ * 128, t_ * 128 + 128), vaug.s((i + t_) * 520 + h * 65, (i + t_) * 520 + h * 65 + 65),
                                    t_ == 0, False))
                    mms.append((o_, ptB[j].s(0, 128), vaug.s((i + 4) * 520 + h * 65, (i + 4) * 520 + h * 65 + 65), False, False))
                    for c_ in range(2):
                        mms.append((o_, ptB[j].s(128 + c_ * 128, 256 + c_ * 128), vctx.s(c_ * 520 + h * 65, c_ * 520 + h * 65 + 65),
                                    False, c_ == 1))
                    PE(mms, [DptA[j], DptB[j], Dvaug, Dvctx], [DP[ob]])

                if stop == "b1e":
                    break
                ST(0)
                if stop == "b1c":
                    break
                if stop == "b1d":
                    EXPV(0)
                    break
                for h in range(8):
                    if h + 1 < 8:
                        ST(h + 1)
                    EXPV(h)
                    if h % 4 == 3:
                        ob = 4 + h // 4
                        g4 = h // 4
                        S.add("dve", lambda e, ob=ob, g4=g4: e.reciprocal(out=rcp.s(g4 * 4, g4 * 4 + 4), in_=P[ob].v(64, [[65, 4]])),
                              [DP[ob]], [Drcp])
                        TT(otm.v(g4 * 256, [[64, 4], [1, 64]]), P[ob].v(0, [[65, 4], [1, 64]]), rcp.v(g4 * 4, [[1, 4], [0, 64]]),
                           ALU.mult, [DP[ob], Drcp], [Dotm])
                PE([("T", PT.s(k_ * 128, k_ * 128 + 128), otm.s(k_ * 128, k_ * 128 + 128), identb) for k_ in range(4)],
                   [Dotm, Dcstb], [DPT])
                CP(o_naT.v(i * 128, [[512, 4], [1, 128]]), PT.v(0, [[128, 4], [1, 128]]), [DPT], [Do_naT])
            if s_i == 0:
                tap("o_naT", o_naT.s(0, 2048), 2048, BF16, [Do_naT])
            S.barrier()
            b1.close()
            if stop in ("b1", "b1a", "b1b", "b1c", "b1d", "b1e"):
                sg.close()
                break
            sg2 = ExitStack()
            x1 = sb("x1", 4 * 1024, F32, sg2)
            Dx1 = [Dep("x1_%d" % t_) for t_ in range(4)]
            h2T = sb("h2T", 8 * 512, BF16, sg2)
            Dh2T = [Dep("h2T%d" % i_) for i_ in range(32)]
            b2 = ExitStack()
            sog = [sb("sog%d" % i_, 512, F32, b2) for i_ in range(2)]
            ogt = [sb("ogt%d" % i_, 512, F32, b2) for i_ in range(2)]
            gt2 = [sb("gt2_%d" % i_, 512, F32, b2) for i_ in range(2)]
            gsq = [sb("gsq%d" % i_, 512, BF16, b2) for i_ in range(2)]
            Dgsq = [Dep(), Dep()]
            Dsog, Dogt, Dgt2 = [Dep(), Dep()], [Dep(), Dep()], [Dep(), Dep()]
            o_gT = sb("o_gT", 4 * 512, BF16, b2)
            Do_gT = [Dep("o_gT%d" % i_) for i_ in range(4)]
            sig = [sb("sig%d" % i, 512, F32, b2) for i in range(2)]
            Dsig = [Dep("sig0"), Dep("sig1")]
            ytmp = sb("ytmp", 4 * 512, F32, b2)
            Dytmp = Dep("ytmp")
            yT = sb("yT", 8 * 512, BF16, b2)
            DyT = Dep("yT")
            mtmp, Dmtmp = ogt[0], Dogt[0]
            for tt in range(4):
                DMA("sp", x1.s(tt * 1024, tt * 1024 + 1024), I["xa"][s_i * 512 + tt * 128: s_i * 512 + tt * 128 + 128, :], [], [Dx1[tt]],
                    Dx1[tt], ab=True)
            wso, wdo, n = loadw("w_og", 0, 8, 0, 512)

            def gla_fin(h, sl):
                hp, hh = divmod(h, 2)
                b0, b1_, b2_ = (0, 1, 2) if sl == 0 else (3, 4, 5)
                sog_, ogt_, gt2_ = sog[sl], ogt[sl], gt2[sl]
                PE([(P[b0].s(0, 512), wso.v(kt * 512 + h * 128, [[1, 128]]), hown(kt), kt == 0, kt == 7) for kt in range(8)],
                   [wdo] + DhT, [DP[b0]])
                yield
                ACT(sog_.s(0, 512), P[b0].s(0, 512), AF.Silu, [DP[b0]], [Dsog[sl]])
                yield
                PE([(P[b1_].s(0, 512), Sentbf.s((s_i * 2 + 0) * 256 + hp * 128, (s_i * 2 + 0) * 256 + hp * 128 + 128, p0=64 * hh, n=64),
                     qsd.s((0 * 2 + hp) * TOK + s_i * 512, (0 * 2 + hp) * TOK + s_i * 512 + 512, p0=64 * hh, n=64), True, False),
                    (P[b1_].s(0, 512), Sentbf.s((s_i * 2 + 1) * 256 + hp * 128, (s_i * 2 + 1) * 256 + hp * 128 + 128, p0=64 * hh, n=64),
                     qsd.s((1 * 2 + hp) * TOK + s_i * 512, (1 * 2 + hp) * TOK + s_i * 512 + 512, p0=64 * hh, n=64), False, True)],
                   DSent2 + [Dqsd[s_i]], [DP[b1_]])
                yield
                TT(ogt_.s(0, 512), P[b1_].s(0, 512), o_loc.s(h * TOK + s_i * 512, h * TOK + s_i * 512 + 512), ALU.add,
                   [DP[b1_], Doloc[s_i]], [Dogt[sl]])
                yield
                if s_i == 0 and h == 0:
                    tap("ogt", ogt_.s(0, 512), 512, F32, [Dogt[sl]])
                ACT(gsq[sl].s(0, 512), ogt_.s(0, 512), AF.Square, [Dogt[sl]], [Dgsq[sl]])
                yield
                PE([(P[b2_].s(0, 512), onesb.s(0, 128), gsq[sl].s(0, 512), True, True)], [Dgsq[sl], Dones], [DP[b2_]])
                yield
                ACT(gt2_.s(0, 512), P[b2_].s(0, 512), AF.Ln, [DP[b2_], Dkc], [Dgt2[sl]], scale=1.0 / 128, bias=kc.s(0, 1))
                yield
                ACT(gt2_.s(0, 512), gt2_.s(0, 512), AF.Exp, [], [Dgt2[sl]], scale=-0.5)
                yield
                STT(ogt_.s(0, 512), ogt_.s(0, 512), gn, gt2_.s(0, 512), ALU.mult, ALU.mult, [Dgt2[sl], Dprm], [Dogt[sl]])
                yield
                TT(o_gT.s(h * 512, h * 512 + 512), ogt_.s(0, 512), sog_.s(0, 512), ALU.mult, [Dogt[sl], Dsog[sl]], [Do_gT[h]])
                yield

            run_groups([gla_fin(h, h % 2) for h in range(4)], 2)
            if s_i == 0:
                tap("o_gT", o_gT.s(0, 2048), 2048, BF16, Do_gT)
            for half in range(2):
                for br in range(2):
                    wsb, wdb, n = loadw("w_br_na" if br == 0 else "w_br_gla", 0, 4, half * 512, 512)
                    wsm, wdm, n = loadw("w_ma" if br == 0 else "w_mb", 0, 8, half * 512, 512)
                    src, dsrc = (o_naT, [Do_naT]) if br == 0 else (o_gT, Do_gT)
                    for cj in range(4):
                        c = half * 4 + cj
                        pb_, mb_ = (0, 1) if cj % 2 == 0 else (2, 3)
                        PE([(P[pb_].s(0, 512), wsb.v(k_ * 512 + cj * 128, [[1, 128]]), src.s(k_ * 512, k_ * 512 + 512), k_ == 0, k_ == 3)
                            for k_ in range(4)], [wdb] + dsrc, [DP[pb_]])
                        PE([(P[mb_].s(0, 512), wsm.v(kt * 512 + cj * 128, [[1, 128]]), hown(kt), kt == 0, kt == 7) for kt in range(8)],
                           [wdm] + DhT, [DP[mb_]])
                        j = cj % 2
                        ACT(sig[j].s(0, 512), P[mb_].s(0, 512), AF.Sigmoid, [DP[mb_]], [Dsig[j]])
                        if br == 0:
                            TT(ytmp.s(cj * 512, cj * 512 + 512), P[pb_].s(0, 512), sig[j].s(0, 512), ALU.mult, [DP[pb_], Dsig[j]],
                               [Dytmp])
                        else:
                            TT(mtmp.s(0, 512), P[pb_].s(0, 512), sig[j].s(0, 512), ALU.mult, [DP[pb_], Dsig[j]], [Dmtmp])
                            TT(yT.s(c * 512, c * 512 + 512), mtmp.s(0, 512), ytmp.s(cj * 512, cj * 512 + 512), ALU.add,
                               [Dmtmp, Dytmp], [DyT])
            if s_i == 0:
                tap("yT", yT.s(0, 4096), 4096, BF16, [DyT])
            for half in range(2):
                ws, wd, n = loadw("w_out", 0, 8, half * 512, 512)
                for tt in range(4):
                    b = 4 + tt % 2
                    PE([(P[b].s(0, 512), yT.s(c * 512 + tt * 128, c * 512 + tt * 128 + 128), ws.s(c * 512, c * 512 + 512), c == 0, c == 7)
                        for c in range(8)], [wd, DyT], [DP[b]])
                    TT(mtmp.s(0, 512), P[b].s(0, 512), gbc.s(half * 512, half * 512 + 512), ALU.mult, [DP[b], Dgbc], [Dmtmp])
                    TT(x1.s(tt * 1024 + half * 512, tt * 1024 + half * 512 + 512), x1.s(tt * 1024 + half * 512, tt * 1024 + half * 512 + 512),
                       mtmp.s(0, 512), ALU.add, [Dmtmp], [Dx1[tt]])
            norm_tiles([(x1.s(tt * 1024, tt * 1024 + 1024), Dx1[tt], tt * 128, tt) for tt in range(4)], 32, 40, h2T, Dh2T, Dmv2)
            if s_i == 0:
                tap("x1", x1.s(0, 4096), 4096, F32, Dx1)
                tap("h2T", h2T.s(0, 4096), 4096, BF16, Dh2T)
            S.barrier()
            b2.close()
            b3 = ExitStack()
            actT = sb("actT", NF * 512, BF16, b3)
            Dact = Dep("actT")
            sgl = [sb("sgl%d" % i, 512, F32, b3) for i in range(2)]
            Dsgl = [Dep("sgl0"), Dep("sgl1")]
            def ffn_gu():
              for f2 in range(NF // 2):
                  i_ = wctr[0] % len(WS)
                  wctr[0] += 1
                  wsf, wdf = WS[i_], WD[i_]

                  def ld(e, s_, f2=f2, wsf=wsf):
                      e.dma_start(out=wsf.v(0, [[512, 8], [1, 256]]),
                                  in_=I["w_gate"][:, f2 * 256:f2 * 256 + 256].rearrange("(k p) n -> p k n", p=128)).then_inc(s_, 16)
                      e.dma_start(out=wsf.v(256, [[512, 8], [1, 256]]),
                                  in_=I["w_up"][:, f2 * 256:f2 * 256 + 256].rearrange("(k p) n -> p k n", p=128)).then_inc(s_, 16)
                  S.add("pool", ld, [], [wdf], dma=wdf, ndma=2)
                  for ff in range(2):
                      f = 2 * f2 + ff
                      bg, bu = (0, 1) if ff == 0 else (2, 3)
                      PE([(P[bg].s(0, 512), wsf.v(kt * 512 + ff * 128, [[1, 128]]), h2T.s(kt * 512, kt * 512 + 512), kt == 0, kt == 7)
                          for kt in range(8)], [wdf] + Dh2T, [DP[bg]])
                      yield
                      PE([(P[bu].s(0, 512), wsf.v(kt * 512 + 256 + ff * 128, [[1, 128]]), h2T.s(kt * 512, kt * 512 + 512), kt == 0, kt == 7)
                          for kt in range(8)], [wdf] + Dh2T, [DP[bu]])
                      yield
                      ACT(sgl[ff].s(0, 512), P[bg].s(0, 512), AF.Silu, [DP[bg]], [Dsgl[ff]])
                      yield
                      TT(actT.s(f * 512, f * 512 + 512), P[bu].s(0, 512), sgl[ff].s(0, 512), ALU.mult, [DP[bu], Dsgl[ff]], [Dact])
                      yield
            gens = [ffn_gu()]
            if s_i + 1 < NSEG:
                gens.append(halo_prep(s_i + 1))
            drain(roundrobin(gens))
            for half in range(2):
                banks = (0, 1, 2, 3) if half == 0 else (4, 5, 6, 3)
                nf4 = (NF + 3) // 4
                for f4 in range(nf4):
                    kk = min(4, NF - 4 * f4)
                    ws, wd, n = loadw("w_down", f4 * 512, kk, half * 512, 512)
                    for tt in range(4):
                        PE([(P[banks[tt]].s(0, 512), actT.s((4 * f4 + k_) * 512 + tt * 128, (4 * f4 + k_) * 512 + tt * 128 + 128),
                             ws.s(k_ * 512, k_ * 512 + 512), (f4 == 0 and k_ == 0), (f4 == nf4 - 1 and k_ == kk - 1)) for k_ in range(kk)],
                           [wd, Dact], [DP[banks[tt]]])
                for tt in range(4):
                    oi = octr[0] % 2
                    octr[0] += 1
                    b = banks[tt]
                    TT(ost[oi].s(0, 512), P[b].s(0, 512), gbc.s(1024 + half * 512, 1024 + half * 512 + 512), ALU.mult,
                       [DP[b], Dgbc, Dout[oi]], [Dost[oi]])
                    TT(ost[oi].s(0, 512), ost[oi].s(0, 512), x1.s(tt * 1024 + half * 512, tt * 1024 + half * 512 + 512), ALU.add,
                       [Dx1[tt]], [Dost[oi]])
                    DMA("sp", out[s_i * 512 + tt * 128: s_i * 512 + tt * 128 + 128, half * 512: half * 512 + 512], ost[oi].s(0, 512),
                        [Dost[oi]], [Dout[oi]], Dout[oi], nobar=True)
            S.barrier()
            b3.close()
            sg2.close()
            sg.close()
            if stop == "b3":
                break
        S.add("sp", None, reads=Dout + tapd)
        pb.close()
        S.emit(st)
    return nc


_NC_CACHE = {}


def kernel(**inputs):
    maps = prep_inputs(**inputs)
    if "nc" not in _NC_CACHE:
        _NC_CACHE["nc"] = build()
    nc = _NC_CACHE["nc"]
    res = run_bass_kernel_spmd(nc, maps, core_ids=list(range(NCORE)))
    out = np.empty((2, SEQ, D), np.float32)
    for core in range(NCORE):
        bb, r = divmod(core, 4)
        out[bb, r * TOK:(r + 1) * TOK] = np.asarray(res.results[core]["out"], dtype=np.float32)
    return out
```

```python
import numpy as np
from contextlib import ExitStack
import concourse.bass as bass
import concourse.mybir as mybir
from concourse.bass_utils import run_bass_kernel_spmd

F32 = mybir.dt.float32
BF16 = mybir.dt.bfloat16
AF = mybir.ActivationFunctionType
ALU = mybir.AluOpType

D = 1024
SEQ = 8192
NCORE = 8
TOK = 2048
SA = 512
WAW_RELAX = True
STG_GAP = 80
NSEG = TOK // SA
CTX = 256
DFF = 2816
NF = DFF // 128
EPS = 1e-6
NEG = -1e30


class Dep:
    __slots__ = ("name", "w", "rs", "sem", "semcnt")

    def __init__(self, name=""):
        self.name = name
        self.w = None
        self.rs = []
        self.sem = None
        self.semcnt = 0


class Op:
    __slots__ = ("eng", "fn", "deps", "dma", "signal", "ev", "idx", "inc")

    def __init__(self, eng, fn, dma, inc):
        self.eng = eng
        self.fn = fn
        self.deps = []
        self.dma = dma
        self.inc = inc
        self.signal = False
        self.ev = None


ENGS = ("pe", "act", "dve", "pool", "sp")


class Sched:
    def __init__(self, nc, same_engine_sync=True):
        self.nc = nc
        self.ops = []
        self.same_engine_sync = same_engine_sync
        self.last = {e: None for e in ENGS}
        self.pending_dma = []
        self.bar_evs = []

    def add(self, eng, fn, reads=(), writes=(), dma=None, ndma=1, inc=16, after_barrier=False, nobar=False):
        deps = []
        if after_barrier:
            deps.extend(getattr(self, "bar_evs", []))
        for r in reads:
            if r.w is not None:
                deps.append(r.w)
        for w in writes:
            if w.w is not None:
                deps.append(w.w)
            deps.extend(w.rs)
        op = Op(eng, fn, dma, ndma * inc)
        raw = set(id(r.w) for r in reads if r.w is not None)
        seen = set()
        for d in deps:
            if id(d) in seen:
                continue
            seen.add(id(d))
            if d.dma is None and d.eng == eng and (eng == "pe" or not self.same_engine_sync):
                continue
            if WAW_RELAX and dma is None and d.dma is None and d.eng == eng and id(d) not in raw:
                continue
            op.deps.append(d)
        for r in reads:
            r.rs.append(op)
        for w in writes:
            w.w = op
            w.rs = []
        self.ops.append(op)
        if dma is None:
            self.last[eng] = op
        elif not nobar:
            self.pending_dma.append(op)
        return op

    def barrier(self):
        evs = [o for o in self.last.values() if o is not None] + list(self.pending_dma)
        self.pending_dma = []
        self.bar_evs = evs
        for e in ("pe", "act", "dve"):
            op = Op(e, None, None, 0)
            op.deps = [d for d in evs if not (d.dma is None and d.eng == e and e == "pe")]
            self.ops.append(op)

    def emit(self, stack):
        nc = self.nc
        for op in self.ops:
            for d in op.deps:
                d.signal = True
        esem = {}
        for e in ("pe", "act", "dve", "pool"):
            esem[e] = stack.enter_context(nc.semaphore("sem_" + e))
        ecnt = {e: 0 for e in esem}
        nsem = 0
        for op in self.ops:
            if op.dma is not None:
                d = op.dma
                if d.sem is None:
                    d.sem = stack.enter_context(nc.semaphore("dsem%d" % nsem))
                    nsem += 1
                d.semcnt += op.inc
                op.ev = (d.sem, d.semcnt)
            elif op.signal and op.fn is not None:
                ecnt[op.eng] += 1
                op.ev = (esem[op.eng], ecnt[op.eng])
        self.nsem = nsem
        per = {e: [o for o in self.ops if o.eng == e] for e in ENGS}
        block = stack.enter_context(nc.Block())

        def run(engobj, ops):
            waited = {}
            for op in ops:
                for d in op.deps:
                    if d.ev is None:
                        continue
                    sem, val = d.ev
                    k = id(sem)
                    if waited.get(k, 0) >= val:
                        continue
                    waited[k] = val
                    engobj.wait_ge(sem, val)
                if op.fn is None:
                    continue
                if op.dma is not None:
                    op.fn(engobj, op.ev[0])
                else:
                    inst = op.fn(engobj)
                    if op.signal:
                        inst.then_inc(op.ev[0], 1)

        @block.tensor
        def _(e):
            run(e, per["pe"])

        @block.scalar
        def _(e):
            run(e, per["act"])

        @block.vector
        def _(e):
            run(e, per["dve"])

        @block.gpsimd
        def _(e):
            run(e, per["pool"])

        @block.sync
        def _(e):
            run(e, per["sp"])


class T:
    def __init__(self, h, F, base=0):
        self.h = h
        self.F = F
        self.base = base

    def v(self, c0, dims, p0=0, n=128):
        return bass.AP(self.h, p0 * self.F + self.base + c0, [[self.F, n]] + [list(d) for d in dims])

    def s(self, c0, c1, p0=0, n=128):
        return bass.AP(self.h, p0 * self.F + self.base + c0, [[self.F, n], [1, c1 - c0]])


def _gmap(Rs, b):
    g = Rs - 4 + b
    if g < 0:
        return {-4: 5, -3: 6, -2: 7, -1: 7}[g]
    if g > 127:
        return 120 if g == 128 else 121
    return g


def _bias_table(rpb, Rs, i):
    tab = np.full((128, 8, 5, 128), NEG, np.float32)
    cq = np.arange(64)
    ck = np.arange(64)
    col_start = np.clip(cq - 8, 0, 48)
    in_win = (ck[None, :] >= col_start[:, None]) & (ck[None, :] < col_start[:, None] + 16)
    dc = np.clip(ck[None, :] - cq[:, None], -15, 15) + 15
    seen = set()
    for kb in range(9):
        b = 2 * i + kb
        g = _gmap(Rs, b)
        first = g not in seen
        seen.add(g)
        t, ph = divmod(kb * 64, 128)
        for ql in range(2):
            rq = Rs + 2 * i + ql
            rs = min(max(rq - 4, 0), 120)
            if not (first and rs <= g < rs + 8):
                continue
            dr = g - rq + 7
            vals = rpb[:, dr, :][:, dc]
            vals = np.where(in_win[None], vals, NEG)
            tab[ph:ph + 64, :, t, ql * 64:(ql + 1) * 64] = np.transpose(vals, (2, 0, 1))
    return tab.reshape(128, 8 * 5 * 128)


def _fm(v):
    return np.ascontiguousarray(v.reshape(-1, 128).T)


def prep_inputs(x, c, ctx, c_ctx, w_mod, b_mod, norm1_g, norm2_g, w_in, na_q_norm_g, na_k_norm_g, na_rpb,
                gla_w_alpha, gla_b_alpha, gla_norm_g, w_branch_na, w_branch_gla, w_out,
                w_ffn_gate, w_ffn_up, w_ffn_down):
    f = lambda a: np.ascontiguousarray(np.asarray(a, dtype=np.float32))
    x, c, ctx, c_ctx = f(x), f(c), f(ctx), f(c_ctx)
    w_in0 = f(w_in)[0]
    offs = np.cumsum([0, 512, 512, 512, 256, 256, 512, 512, 32, 1024, 1024])
    sl = lambda k: w_in0[:, offs[k]:offs[k + 1]]
    wq_na, wk_na, wv_na, wq_g, wk_g, wv_g, w_og, w_lr, w_ma, w_mb = [sl(k) for k in range(10)]
    d = np.arange(256)
    dd = d % 64
    swp = (d // 64) * 64 + np.where(dd % 32 < 16, dd + 16, dd - 16)
    wg1 = np.ascontiguousarray(np.concatenate([wq_g, wq_g[:, swp], wk_g, wk_g[:, swp]], axis=1))
    wal = f(gla_w_alpha)[0]
    walpha = np.zeros((32, 512), np.float32)
    walpha[0:16, 0:256] = wal[0]
    walpha[16:32, 256:512] = wal[1]
    bal = f(gla_b_alpha)[0]
    balpha = np.ascontiguousarray(np.stack([bal[0, 0:128], bal[0, 128:256], bal[1, 0:128], bal[1, 128:256]], axis=1))
    rpb = f(na_rpb)[0]
    gq = np.tile(f(na_q_norm_g)[0], 2).reshape(128, 1)
    gk = np.tile(f(na_k_norm_g)[0], 2).reshape(128, 1)
    gn = f(gla_norm_g)[0].reshape(128, 1)
    pvec = np.ascontiguousarray(np.concatenate([gq, gk, gn], axis=1))
    ident = np.eye(128, dtype=np.float32)
    blk = np.zeros((128, 128), np.float32)
    blk[:64, :64] = 1
    blk[64:, 64:] = 1
    j = np.arange(128)[:, None]
    i = np.arange(128)[None, :]
    same = (j // 64) == (i // 64)
    mf = (same & (j <= i)).astype(np.float32)
    mb = (same & (j >= i)).astype(np.float32)
    consts = np.ascontiguousarray(np.concatenate([ident, blk, mf, mb], axis=1))
    freqs = (10000.0 ** (-np.arange(16, dtype=np.float32) / 16)).astype(np.float32)
    tab_int = _bias_table(rpb, 40, 0)
    shared = {
        "w_mod": f(w_mod)[0], "b_modT": _fm(f(b_mod)[0]),
        "n1g": _fm(f(norm1_g)[0]), "n2g": _fm(f(norm2_g)[0]),
        "wq_na": np.ascontiguousarray(wq_na), "wk_na": np.ascontiguousarray(wk_na),
        "wv_na": np.ascontiguousarray(wv_na), "wg1": wg1, "wv_g": np.ascontiguousarray(wv_g),
        "w_og": np.ascontiguousarray(w_og), "w_lr": np.ascontiguousarray(w_lr),
        "w_ma": np.ascontiguousarray(w_ma), "w_mb": np.ascontiguousarray(w_mb),
        "walpha": walpha, "balpha": balpha, "pvec": pvec,
        "w_br_na": f(w_branch_na)[0], "w_br_gla": f(w_branch_gla)[0], "w_out": f(w_out)[0],
        "w_gate": f(w_ffn_gate)[0], "w_up": f(w_ffn_up)[0], "w_down": f(w_ffn_down)[0],
        "consts": consts,
    }
    maps = []
    for core in range(NCORE):
        bb, r = divmod(core, 4)
        R0 = 32 * r
        xa = x[bb, R0 * 64:(R0 + 32) * 64]
        rows = []
        for s in range(NSEG):
            Rs = R0 + 8 * s
            for b in range(16):
                g = _gmap(Rs, b)
                rows.append(x[bb, g * 64:(g + 1) * 64])
        xh = np.ascontiguousarray(np.concatenate(rows, axis=0))
        cvec = np.ascontiguousarray(np.concatenate([_fm(c[bb]), _fm(c_ctx)], axis=1))
        tabs = [tab_int,
                _bias_table(rpb, R0, 0) if r == 0 else tab_int,
                _bias_table(rpb, R0, 1) if r == 0 else tab_int,
                _bias_table(rpb, R0 + 24, 2) if r == 3 else tab_int,
                _bias_table(rpb, R0 + 24, 3) if r == 3 else tab_int]
        btab = np.ascontiguousarray(np.stack(tabs, axis=0))
        t = np.arange(TOK)
        rowp = (R0 + t // 64).astype(np.float32)
        colp = (t % 64).astype(np.float32)
        dd = np.arange(128) % 64
        fr = freqs[dd % 16][:, None]
        pos = np.where((dd < 32)[:, None], rowp[None, :], colp[None, :]).astype(np.float32)
        ang = (pos * fr).astype(np.float32)
        ropeC = np.ones((128, TOK + CTX), np.float32)
        ropeS = np.zeros((128, TOK + CTX), np.float32)
        ropeC[:, :TOK] = np.cos(ang)
        sgn = np.where((dd % 32) < 16, -1.0, 1.0).astype(np.float32)[:, None]
        ropeS[:, :TOK] = np.sin(ang) * sgn
        cmask = np.zeros((128, 8), np.float32)
        for p in range(4):
            cmask[:, p] = 1.0 if p < r else 0.0
            cmask[:, 4 + p] = 1.0 if p > r else 0.0
        m = dict(shared)
        m.update({"xa": np.ascontiguousarray(xa), "xh": xh, "xc": np.ascontiguousarray(ctx[bb]), "cvec": cvec,
                  "btab": btab, "ropeC": ropeC, "ropeS": ropeS, "cmask": cmask})
        maps.append(m)
    return maps


IN_SHAPES = {
    "xa": [TOK, D], "xh": [NSEG * 1024, D], "xc": [CTX, D], "cvec": [128, 16],
    "w_mod": [D, 6 * D], "b_modT": [128, 48], "n1g": [128, 8], "n2g": [128, 8],
    "wq_na": [D, 512], "wk_na": [D, 512], "wv_na": [D, 512], "wg1": [D, 1024], "wv_g": [D, 512],
    "w_og": [D, 512], "w_lr": [D, 32], "w_ma": [D, D], "w_mb": [D, D],
    "walpha": [32, 512], "balpha": [128, 4], "pvec": [128, 3],
    "w_br_na": [512, D], "w_br_gla": [512, D], "w_out": [D, D],
    "w_gate": [D, DFF], "w_up": [D, DFF], "w_down": [DFF, D],
    "consts": [128, 512], "btab": [5, 128, 5120], "ropeC": [128, TOK + CTX], "ropeS": [128, TOK + CTX],
    "cmask": [128, 8],
}


def build(stop=None, taps=None):
    nc = bass.Bass("TRN2", target_bir_lowering=False)
    I = {k: nc.dram_tensor(k, shp, F32, kind="ExternalInput").ap() for k, shp in IN_SHAPES.items()}
    out = nc.dram_tensor("out", [TOK, D], F32, kind="ExternalOutput").ap()
    cc_in = nc.dram_tensor("cc_in", [128, 520], F32)
    cc_out = nc.dram_tensor("cc_out", [512, 520], F32)
    taps = taps if taps is not None else {}
    tapd = []
    st = ExitStack()
    with st:
        S = Sched(nc)

        uniq = [0]

        def sb(name, F, dt, stack=None):
            uniq[0] += 1
            h = (stack or st).enter_context(nc.sbuf_tensor("s%d_%s" % (uniq[0], name), [128, F], dt))
            return T(h, F)

        def DMA(eng, o, i, reads, writes, dep, n=1, ab=False, nobar=False):
            S.add(eng, lambda e, s: e.dma_start(out=o, in_=i).then_inc(s, 16), reads=reads, writes=writes, dma=dep,
                  after_barrier=ab, nobar=nobar)

        def PE(mms, reads, writes):
            def fn(e):
                r = None
                for m in mms:
                    if m[0] == "T":
                        r = e.transpose(m[1], m[2], m[3])
                    else:
                        r = e.matmul(m[0], lhsT=m[1], rhs=m[2], start=m[3], stop=m[4])
                return r
            S.add("pe", fn, reads, writes)

        def ACT(o, i, func, reads, writes, scale=None, bias=None, accum=None):
            kw = {}
            if scale is not None:
                kw["scale"] = scale
            if bias is not None:
                kw["bias"] = bias
            if accum is not None:
                kw["accum_out"] = accum
            S.add("act", lambda e: e.activation(out=o, in_=i, func=func, **kw), reads, writes)

        def TT(o, a, b, op, reads, writes, eng="dve"):
            S.add(eng, lambda e: e.tensor_tensor(out=o, in0=a, in1=b, op=op), reads, writes)

        def TS(o, a, s1, op0, reads, writes, s2=None, op1=None, eng="dve"):
            if op1 is None:
                S.add(eng, lambda e: e.tensor_scalar(out=o, in0=a, scalar1=s1, scalar2=None, op0=op0), reads, writes)
            else:
                S.add(eng, lambda e: e.tensor_scalar(out=o, in0=a, scalar1=s1, scalar2=s2, op0=op0, op1=op1), reads, writes)

        def STT(o, a, sc, b, op0, op1, reads, writes):
            S.add("dve", lambda e: e.scalar_tensor_tensor(out=o, in0=a, scalar=sc, in1=b, op0=op0, op1=op1), reads, writes)

        def CP(o, i, reads, writes, eng="dve"):
            if eng == "act":
                S.add("act", lambda e: e.copy(out=o, in_=i), reads, writes)
            else:
                S.add(eng, lambda e: e.tensor_copy(out=o, in_=i), reads, writes)

        def MEMSET(o, val, writes, eng="dve"):
            S.add(eng, lambda e: e.memset(o, val), (), writes)

        def tap(name, t, F, dt, deps):
            if name not in taps:
                return
            d = nc.dram_tensor("tap_" + name, [128, F], dt, kind="ExternalOutput").ap()
            dep = Dep("tap_" + name)
            DMA("sp", d, t, deps, [dep], dep)
            tapd.append(dep)

        PD = [st.enter_context(nc.psum_tensor("psd%d" % i, [128, 1024], F32)) for i in range(2)]
        P = [T(PD[0], 1024, 0), T(PD[0], 1024, 512), T(PD[1], 1024, 0), T(PD[1], 1024, 512)]
        P += [T(st.enter_context(nc.psum_tensor("ps%d" % i, [128, 512], F32)), 512) for i in range(4, 7)]
        DP = [Dep("ps%d" % i) for i in range(7)]
        PT = T(st.enter_context(nc.psum_tensor("pst", [128, 1024], BF16)), 1024)
        DPT = Dep("pst")

        cst = sb("cst", 512, F32)
        cstb = sb("cstb", 512, BF16)
        Dcst, Dcstb = Dep("cst"), Dep("cstb")
        DMA("sp", cst.s(0, 512), I["consts"], [], [Dcst], Dcst)
        DMA("pool", cstb.s(0, 512), I["consts"], [], [Dcstb], Dcstb)
        identb = cstb.s(0, 128)
        blk32 = cst.s(128, 256)
        blkb = cstb.s(128, 256)
        ident32 = cst.s(0, 128)
        mfb, mbb = cstb.s(256, 384), cstb.s(384, 512)
        kc = sb("kc", 8, F32)
        Dkc = Dep("kc")
        MEMSET(kc.s(0, 1), EPS, [Dkc])
        MEMSET(kc.s(1, 2), 1.0, [Dkc])
        MEMSET(kc.s(2, 3), 0.0, [Dkc])
        ones32 = sb("ones32", 128, F32)
        Dones = Dep("ones")
        MEMSET(ones32.s(0, 128), 1.0, [Dones])
        onesb = sb("onesb", 128, BF16)
        MEMSET(onesb.s(0, 128), 1.0, [Dones])
        onesw = sb("onesw", 512, F32)
        MEMSET(onesw.s(0, 512), 1.0, [Dones])

        prm = sb("prm", 128, F32)
        Dprm = Dep("prm")
        for (c0, key, n) in ((0, "cvec", 16), (16, "b_modT", 48), (64, "n1g", 8), (72, "n2g", 8), (80, "balpha", 4),
                             (84, "pvec", 3), (88, "cmask", 8)):
            DMA("sp", prm.s(c0, c0 + n), I[key], [], [Dprm], Dprm)
        walb = sb("walb", 512, BF16)
        Dwal = Dep("walb")
        DMA("pool", walb.s(0, 512, n=32), I["walpha"], [], [Dwal], Dwal)
        gq, gk, gn = prm.s(84, 85), prm.s(85, 86), prm.s(86, 87)

        NWS = 3
        WS = [sb("ws%d" % i, 4096, BF16) for i in range(NWS)]
        WD = [Dep("ws%d" % i) for i in range(NWS)]
        wctr = [0]

        STG_SPEC = [("w_modl", "w_mod", 1024, 2048, 4096, 128)]
        STG = {}
        stg_chunks = []
        for (sname, src_name, rows, c0_, ncols, rpc) in STG_SPEC:
            t_ = nc.dram_tensor("stg_" + sname, [rows, ncols], BF16).ap()
            deps_ = []
            for r0_ in range(0, rows, rpc):
                r1_ = min(rows, r0_ + rpc)
                d_ = Dep("stg_%s_%d" % (sname, r0_))
                deps_.append((r0_, r1_, d_))
                stg_chunks.append((t_[r0_:r1_, :], I[src_name][r0_:r1_, c0_:c0_ + ncols], d_))
            STG[sname] = (t_, deps_)
        use_stg = [False]

        def stager():
            for (o_, i_, d_) in stg_chunks:
                DMA("pool", o_, i_, [], [d_], d_, nobar=True)
                for _ in range(STG_GAP):
                    yield

        def loadw(name, r0, kt, c0, n):
            i = wctr[0] % len(WS)
            wctr[0] += 1
            reads = []
            if use_stg[0] and name in STG:
                t_, deps_ = STG[name]
                src = t_[r0:r0 + 128 * kt, c0:c0 + n].rearrange("(k p) n -> p k n", p=128)
                reads = [d_ for (a_, b_, d_) in deps_ if a_ < r0 + 128 * kt and b_ > r0]
            else:
                src = I[name][r0:r0 + 128 * kt, c0:c0 + n].rearrange("(k p) n -> p k n", p=128)
            DMA("pool", WS[i].v(0, [[n, kt], [1, n]]), src, reads, [WD[i]], WD[i])
            return WS[i], WD[i], n

        scb = sb("scb", 16, BF16)
        Dscb = Dep("scb")
        ACT(scb.v(0, [[2, 8]]), prm.s(0, 8), AF.Silu, [Dprm], [Dscb])
        ACT(scb.v(1, [[2, 8]]), prm.s(8, 16), AF.Silu, [Dprm], [Dscb])
        modv = sb("modv", 96, F32)
        Dmod = Dep("modv")
        mv = sb("mv", 64, F32)
        Dmv = Dep("mv")
        Dmv2 = Dep("mv2")
        mcol = lambda ch0, j: modv.v(2 * ch0 + j, [[2, 8]])
        gbc = sb("gbc", 2048, F32)
        Dgbc = Dep("gbc")
        dg = sb("dg", 128, F32)
        Ddg = Dep("dg")

        def mod_groups(g0, g1_, dmod):
            for g in range(g0, g1_):
                if use_stg[0] and g >= 4:
                    ws, wd, n = loadw("w_modl", 0, 8, (g - 4) * 512, 512)
                else:
                    ws, wd, n = loadw("w_mod", 0, 8, g * 512, 512)
                mms = []
                for jj in range(4):
                    ch = g * 4 + jj
                    for kt in range(8):
                        mms.append((P[0].s(2 * ch, 2 * ch + 2), ws.v(kt * 512 + jj * 128, [[1, 128]]), scb.s(2 * kt, 2 * kt + 2),
                                    kt == 0, kt == 7))
                PE(mms, [wd, Dscb], [DP[0]])
            c0, c1 = g0 * 4, g1_ * 4
            TT(modv.v(2 * c0, [[2, c1 - c0], [1, 2]]), P[0].v(2 * c0, [[2, c1 - c0], [1, 2]]), prm.v(16 + c0, [[1, c1 - c0], [0, 2]]),
               ALU.add, [DP[0], Dprm], [dmod])

        def mod_early():
            mod_groups(0, 4, Dmod)
            STT(mv.s(0, 8), mcol(8, 0), 1.0, prm.s(64, 72), ALU.add, ALU.mult, [Dmod, Dprm], [Dmv])
            CP(mv.s(8, 16), mcol(0, 0), [Dmod], [Dmv])
            STT(mv.s(16, 24), mcol(8, 1), 1.0, prm.s(64, 72), ALU.add, ALU.mult, [Dmod, Dprm], [Dmv])
            CP(mv.s(24, 32), mcol(0, 1), [Dmod], [Dmv])

        def mod_late():
            Dm2 = Dep("modv2")
            mod_groups(4, 12, Dm2)
            STT(mv.s(32, 40), mcol(32, 0), 1.0, prm.s(72, 80), ALU.add, ALU.mult, [Dm2, Dprm], [Dmv2])
            CP(mv.s(40, 48), mcol(24, 0), [Dm2], [Dmv2])
            CP(mv.s(48, 56), mcol(16, 0), [Dm2], [Dmv2])
            CP(mv.s(56, 64), mcol(40, 0), [Dm2], [Dmv2])
            for gi in range(2):
                dgt, Ddgt = xt[gi], Dxt[gi]
                TT(dgt.v(0, [[128, 8], [1, 128]]), cst.v(0, [[0, 8], [1, 128]]), mv.v(48 + 8 * gi, [[1, 8], [0, 128]]), ALU.mult,
                   [Dcst, Dmv2], [Ddgt])
                for hf in range(2):
                    b_ = 1 + hf
                    PE([(P[b_].s(0, 512), ones32.s(0, 128), dgt.s(hf * 512, hf * 512 + 512), True, True)], [Ddgt, Dones], [DP[b_]])
                    CP(gbc.s(gi * 1024 + hf * 512, gi * 1024 + hf * 512 + 512), P[b_].s(0, 512), [DP[b_]], [Dgbc],
                       eng="act" if hf == 0 else "dve")
            tap("mv", mv.s(0, 64), 64, F32, [Dmv, Dmv2])
            tap("gbc", gbc.s(0, 2048), 2048, F32, [Dgbc])

        mod_early()
        negb = sb("negb", 4, F32)
        Dnegb = Dep("negb")
        TS(negb.s(0, 4), prm.s(80, 84), -1.0, ALU.mult, [Dprm], [Dnegb])

        xt = [sb("xt%d" % i, 1024, F32) for i in range(2)]
        Dxt = [Dep("xt%d" % i) for i in range(2)]
        xs = [sb("xs%d" % i, 1024, BF16) for i in range(2)]
        Dxs = [Dep("xs%d" % i) for i in range(2)]
        nst = [sb("nst%d" % i, 4, F32) for i in range(2)]
        Dnst = [Dep("nst%d" % i) for i in range(2)]

        PTv = [PT, T(P[6].h.bitcast(BF16), 1024)]
        DPTv = [DPT, DP[6]]

        def interleave(*gens):
            gens = list(gens)
            while gens:
                for g_ in list(gens):
                    try:
                        next(g_)
                    except StopIteration:
                        gens.remove(g_)

        import os

        def norm_T_g(x_t, x_dep, gcol, shcol, dst, dst_deps, dcol, slot, dmv):
            xap = x_t.s(0, 1024) if isinstance(x_t, T) else x_t
            i = slot
            ns, dn = nst[i], Dnst[i]
            pt, dpt = PTv[i], DPTv[i]
            ACT(xs[i].s(0, 1024), xap, AF.Square, [x_dep], [dn, Dxs[i]], accum=ns.s(0, 1))
            yield
            ACT(ns.s(1, 2), ns.s(0, 1), AF.Ln, [dn, Dkc], [dn], scale=1.0 / 1024, bias=kc.s(0, 1))
            yield
            ACT(ns.s(2, 3), ns.s(1, 2), AF.Exp, [dn], [dn], scale=-0.5)
            yield
            TS(xs[i].s(0, 1024), xap, ns.s(2, 3), ALU.mult, [x_dep, dn], [Dxs[i]])
            yield
            PE([("T", pt.s(kt * 128, kt * 128 + 128), xs[i].s(kt * 128, kt * 128 + 128), identb) for kt in range(8)],
               [Dxs[i], Dcstb], [dpt])
            yield
            for kt in range(8):
                o = dst.s(kt * dst.F // 8 + dcol, kt * dst.F // 8 + dcol + 128)
                if slot == 0:
                    ACT(o, pt.s(kt * 128, kt * 128 + 128), AF.Identity, [dpt, dmv], [dst_deps[kt]],
                        scale=mv.s(gcol + kt, gcol + kt + 1), bias=mv.s(shcol + kt, shcol + kt + 1))
                else:
                    TS(o, pt.s(kt * 128, kt * 128 + 128), mv.s(gcol + kt, gcol + kt + 1), ALU.mult, [dpt, dmv], [dst_deps[kt]],
                       s2=mv.s(shcol + kt, shcol + kt + 1), op1=ALU.add)
                if kt % 2 == 1:
                    yield

        def norm_tiles(srcs, gcol, shcol, dst, dst_deps, dmv):
            import os
            STEP = 1 if os.environ.get("NOIL") else 2
            for j in range(0, len(srcs), STEP):
                gs = []
                for slot, (xa_, xd_, dc_, ti_) in enumerate(srcs[j:j + STEP]):
                    gs.append(norm_T_g(xa_, xd_, gcol, shcol, dst, dst_deps[ti_ * 8:(ti_ + 1) * 8], dc_, slot, dmv))
                interleave(*gs)

        xctr = [0]

        def load_x(src_rows):
            i = xctr[0] % 2
            xctr[0] += 1
            DMA("sp", xt[i].s(0, 1024), src_rows, [], [Dxt[i]], Dxt[i])
            return xt[i], Dxt[i]

        hTc = sb("hTc", 8 * CTX, BF16)
        DhTc = [Dep("hTc%d" % i_) for i_ in range(16)]
        hT = sb("hT", 8 * 1024, BF16)
        DhT = [Dep("hT%d" % i_) for i_ in range(64)]
        hT2 = sb("hT2", 8 * 1024, BF16)
        DhT2 = [Dep("hT2_%d" % i_) for i_ in range(64)]
        hTb, DhTb = [hT, hT2], [DhT, DhT2]
        o_loc = sb("o_loc", 4 * TOK, BF16)
        Doloc = [Dep("oloc%d" % a) for a in range(NSEG)]
        qsd = sb("qsd", 4 * TOK, BF16)
        Dqsd = [Dep("qsd%d" % a) for a in range(NSEG)]
        NSUM = NSEG + 1
        Sentbf = sb("Sentbf", NSEG * 2 * 256, BF16)
        DSent = Dep("Sentbf")
        xsc = ExitStack()
        Useg = sb("Useg", NSUM * 2 * 256, F32, xsc)
        Aseg = sb("Aseg", NSUM * 2 * 2, F32, xsc)
        Dsum = [Dep("sum%d" % a) for a in range(NSUM)]

        srcs = []
        for tci in range(2):
            x_t, x_d = load_x(I["xc"][tci * 128:(tci + 1) * 128, :])
            srcs.append((x_t, x_d, tci * 128, tci))
        norm_tiles(srcs, 16, 24, hTc, DhTc, Dmv)
        tap("hTc", hTc.s(0, 8 * CTX), 8 * CTX, BF16, DhTc)
        if stop == "mod":
            S.add("sp", None, reads=tapd)
            S.emit(st)
            return nc

        pa = ExitStack()
        ropeC = sb("ropeC", SA, F32, pa)
        ropeS = sb("ropeS", SA, F32, pa)
        DropeC, DropeS = Dep("ropeC"), Dep("ropeS")
        qr = sb("qr", SA, F32, pa)
        kr = sb("kr", SA, F32, pa)
        Dqr, Dkr = Dep("qr"), Dep("kr")
        tmpf = sb("tmpf", SA, F32, pa)
        Dtmpf = Dep("tmpf")
        vg = sb("vg", 2 * 4 * 512, BF16, pa)
        Dvg = [Dep("vg0"), Dep("vg1")]
        zl = sb("zl", SA, BF16, pa)
        Dzl = Dep("zl")
        la = [sb("la%d" % i_, SA, F32, pa) for i_ in range(2)]
        Dla = [Dep("la0"), Dep("la1")]
        pbuf = [sb("pbuf%d" % i_, SA + 1, F32, pa) for i_ in range(2)]
        Dpb = [Dep("pbuf0"), Dep("pbuf1")]
        for i_ in range(2):
            MEMSET(pbuf[i_].s(0, 1), 0.0, [Dpb[i_]])
        bt = [sb("bt%d" % i_, SA, F32, pa) for i_ in range(2)]
        Dbt = [Dep("bt0"), Dep("bt1")]
        E2 = [sb("E2_%d" % i_, SA, BF16, pa) for i_ in range(2)]
        DE2 = [Dep(), Dep()]
        E1, DE1 = la, Dla
        E3, DE3 = bt, Dbt
        qdec = sb("qdec", 2 * 4 * SA, BF16, pa)
        kinv = sb("kinv", 2 * 4 * SA, BF16, pa)
        Dqdec = [[Dep("qdec%d" % i) for i in range(4)] for _ in range(2)]
        Dkinv = [[Dep("kinv%d" % i) for i in range(4)] for _ in range(2)]
        kend = [sb("kend%d" % i_, SA, BF16, pa) for i_ in range(2)]
        Dkend = [Dep("kend0"), Dep("kend1")]
        kendtm = sb("kendtm", 2 * 4 * 512, BF16, pa)
        Dkendtm = [[Dep("kendtm%d" % i_) for i_ in range(4)] for _ in range(2)]
        decs = sb("decs", 2 * 4 * 8, F32, pa)
        Ddecs = [[Dep("decs%d" % i_) for i_ in range(4)] for _ in range(2)]
        dtmp = [sb("dtmp%d" % i_, 8, F32, pa) for i_ in range(2)]
        Ddtmp = [Dep("dtmp0"), Dep("dtmp1")]
        Sst = sb("Sst", 2 * 256, F32, pa)
        DSst = [Dep("Sst0"), Dep("Sst1")]
        Stmp = sb("Stmp", 2 * 256, F32, pa)
        DStmp = [Dep("Stmp0"), Dep("Stmp1")]
        Sbf = sb("Sbf", 2 * 8 * 256, BF16, pa)
        DSbf = Dep("Sbf")
        Abf = [sb("Abf%d" % i, 256, BF16, pa) for i in range(2)]
        DAbf = [Dep("Abf0"), Dep("Abf1")]
        onesA = onesw

        def roundrobin(gens):
            gens = list(gens)
            while gens:
                for g_ in list(gens):
                    try:
                        next(g_)
                        yield
                    except StopIteration:
                        gens.remove(g_)

        def pa_front(N, hsrc, hcol0, hdep, hstride, tok0, sidx, need_q, pp):
            NCH = N // 64
            NT = N // 128
            hk = lambda kt: hsrc.s(kt * hstride + hcol0, kt * hstride + hcol0 + N)
            DMA("sp", ropeC.s(0, N), I["ropeC"][:, tok0:tok0 + N], [], [DropeC], DropeC)
            DMA("sp", ropeS.s(0, N), I["ropeS"][:, tok0:tok0 + N], [], [DropeS], DropeS)
            ws, wd, n = loadw("wv_g", 0, 8, 0, 512)
            for tt in range(NT):
                b = 5 + (tt % 2)
                PE([(P[b].s(0, 512), hsrc.s(kt * hstride + hcol0 + tt * 128, kt * hstride + hcol0 + tt * 128 + 128),
                     ws.s(kt * 512, kt * 512 + 512), kt == 0, kt == 7) for kt in range(8)], [wd] + hdep, [DP[b]])
                yield
                CP(vg.s(pp * 2048 + tt * 512, pp * 2048 + tt * 512 + 512), P[b].s(0, 512), [DP[b]], [Dvg[pp]], eng="act")
                yield
            ws, wd, n = loadw("w_lr", 0, 8, 0, 32)
            PE([(P[5].s(0, N, n=32), ws.v(kt * 32, [[1, 32]]), hk(kt), kt == 0, kt == 7) for kt in range(8)],
               [wd] + hdep, [DP[5]])
            yield
            CP(zl.s(0, N, n=32), P[5].s(0, N, n=32), [DP[5]], [Dzl], eng="act")
            yield
            wsg, wdg, n = loadw("wg1", 0, 8, 0, 512)
            wsk, wdk, n = loadw("wg1", 0, 8, 512, 512)

            def chain(hp, dr):
                dh = dr * 2 + hp
                la_, pb_, bt_, e1, e2, e3, dt_, ke_ = la[dr], pbuf[dr], bt[dr], E1[dr], E2[dr], E3[dr], dtmp[dr], kend[dr]
                zb = 5 + dr
                PE([(P[zb].s(0, N), walb.v(dh * 128, [[1, 128]], n=32), zl.s(0, N, n=32), True, True)],
                   [Dwal, Dzl], [DP[zb]])
                yield
                ACT(la_.s(0, N), P[zb].s(0, N), AF.Exp, [DP[zb], Dnegb], [Dla[dr]], scale=-1.0, bias=negb.s(dh, dh + 1))
                yield
                ACT(la_.s(0, N), la_.s(0, N), AF.Ln, [Dkc], [Dla[dr]], bias=kc.s(1, 2))
                yield
                TS(la_.s(0, N), la_.s(0, N), -1.0 / 16.0, ALU.mult, [], [Dla[dr]], s2=-1.0, op1=ALU.max)
                yield
                S.add("dve", lambda e: e.tensor_tensor_scan(out=pb_.s(1, N + 1), data0=onesA.s(0, N), data1=la_.s(0, N),
                                                            initial=0.0, op0=ALU.mult, op1=ALU.add),
                      [Dla[dr], Dones], [Dpb[dr]])
                yield
                if dr == 0:
                    TT(bt_.v(0, [[64, NCH], [1, 64]]), pb_.v(1, [[64, NCH], [1, 64]]), pb_.v(0, [[64, NCH], [0, 64]]),
                       ALU.subtract, [Dpb[dr]], [Dbt[dr]])
                else:
                    TT(bt_.v(0, [[64, NCH], [1, 64]]), pb_.v(64, [[64, NCH], [0, 64]]), pb_.v(0, [[64, NCH], [1, 64]]),
                       ALU.subtract, [Dpb[dr]], [Dbt[dr]])
                yield
                if hp == 0 and sidx == 0 and dr == 0:
                    tap("bt", bt_.s(0, N), N, F32, [Dbt[dr]])
                ACT(e1.s(0, N), bt_.s(0, N), AF.Exp, [Dbt[dr], Dla[dr]], [DE1[dr]])
                yield
                ACT(e2.s(0, N), bt_.s(0, N), AF.Exp, [Dbt[dr]], [DE2[dr]], scale=-1.0)
                yield
                if dr == 0:
                    ACT(e3.s(0, N), pb_.s(1, N + 1), AF.Exp, [Dpb[dr]], [DE3[dr]])
                else:
                    ACT(e3.s(0, N), pb_.s(0, N), AF.Exp, [Dpb[dr]], [DE3[dr]], scale=-1.0, bias=pb_.s(N, N + 1))
                yield
                if need_q:
                    STT(qdec.s((pp * 4 + dh) * SA, (pp * 4 + dh) * SA + N), qr.s(0, N), 0.125, e1.s(0, N), ALU.mult, ALU.mult, [Dqr, DE1[dr]],
                        [Dqdec[pp][dh]])
                    yield
                    STT(qsd.s(dh * TOK + tok0, dh * TOK + tok0 + N), qr.s(0, N), 0.125, e3.s(0, N), ALU.mult, ALU.mult,
                        [Dqr, DE3[dr]], [Dqsd[sidx]])
                    yield
                TT(kinv.s((pp * 4 + dh) * SA, (pp * 4 + dh) * SA + N), kr.s(0, N), e2.s(0, N), ALU.mult, [Dkr, DE2[dr]], [Dkinv[pp][dh]])
                yield
                TT(dt_.s(0, NCH), pb_.v(64, [[64, NCH]]), pb_.v(0, [[64, NCH]]), ALU.subtract, [Dpb[dr]], [Ddtmp[dr]])
                yield
                ACT(decs.s(pp * 32 + dh * 8, pp * 32 + dh * 8 + NCH), dt_.s(0, NCH), AF.Exp, [Ddtmp[dr]], [Ddecs[pp][dh]])
                yield
                ACT(Aseg.s((sidx * 2 + dr) * 2 + hp, (sidx * 2 + dr) * 2 + hp + 1), pb_.s(N, N + 1), AF.Exp, [Dpb[dr]],
                    [Dsum[sidx]])
                yield
                TT(ke_.v(0, [[64, NCH], [1, 64]]), kinv.v((pp * 4 + dh) * SA, [[64, NCH], [1, 64]]),
                   decs.v(pp * 32 + dh * 8, [[1, NCH], [0, 64]]), ALU.mult, [Dkinv[pp][dh], Ddecs[pp][dh]], [Dkend[dr]])
                yield
                pt, dpt = T(P[zb].h.bitcast(BF16), 1024), DP[zb]
                PE([("T", pt.s(tt * 128, tt * 128 + 128), ke_.s(tt * 128, tt * 128 + 128), identb) for tt in range(NT)],
                   [Dkend[dr], Dcstb], [dpt])
                yield
                CP(kendtm.v(pp * 2048 + dh * 128, [[512, NT], [1, 128]]), pt.v(0, [[128, NT], [1, 128]]), [dpt], [Dkendtm[pp][dh]],
                   eng="act" if dr == 0 else "dve")
                yield

            for hp in range(2):
                Cs = ropeC.s(0, N)
                Ss = ropeS.s(0, N)
                for (wsx, wdx, dst, ddst, bk, need) in ((wsg, wdg, qr, Dqr, 3, need_q), (wsk, wdk, kr, Dkr, 3, True)):
                    if not need:
                        continue
                    PE([(P[bk].s(0, N), wsx.v(kt * 512 + hp * 128, [[1, 128]]), hk(kt), kt == 0, kt == 7) for kt in range(8)],
                       [wdx] + hdep, [DP[bk]])
                    PE([(P[bk + 1].s(0, N), wsx.v(kt * 512 + 256 + hp * 128, [[1, 128]]), hk(kt), kt == 0, kt == 7)
                        for kt in range(8)], [wdx] + hdep, [DP[bk + 1]])
                    yield
                    TT(tmpf.s(0, N), P[bk].s(0, N), Cs, ALU.mult, [DP[bk], DropeC], [Dtmpf])
                    yield
                    TT(dst.s(0, N), P[bk + 1].s(0, N), Ss, ALU.mult, [DP[bk + 1], DropeS], [ddst])
                    yield
                    TT(dst.s(0, N), dst.s(0, N), tmpf.s(0, N), ALU.add, [Dtmpf], [ddst])
                    yield
                if hp == 0 and sidx == 0:
                    tap("qr", qr.s(0, N), N, F32, [Dqr])
                    tap("kr", kr.s(0, N), N, F32, [Dkr])
                for _ in roundrobin([chain(hp, 0), chain(hp, 1)]):
                    yield

        def pa_back(N, tok0, sidx, need_q, pp):
            NCH = N // 64
            NT = N // 128
            for dr in range(2):
                MEMSET(Sst.s(dr * 256, dr * 256 + 256), 0.0, [DSst[dr]])
            for nstep in range(NCH):
                for dr in range(2):
                    cn = nstep if dr == 0 else NCH - 1 - nstep
                    tt, half = divmod(cn, 2)
                    bk = 0
                    mms = []
                    for hp in range(2):
                        for hh in range(2):
                            h = 2 * hp + hh
                            mms.append((P[bk].s(dr * 256 + hp * 128, dr * 256 + hp * 128 + 128, p0=64 * hh, n=64),
                                        kendtm.s(pp * 2048 + tt * 512 + (dr * 2 + hp) * 128 + 64 * hh, pp * 2048 + tt * 512 + (dr * 2 + hp) * 128 + 64 * hh + 64,
                                                 p0=64 * half, n=64),
                                        vg.s(pp * 2048 + tt * 512 + h * 128, pp * 2048 + tt * 512 + h * 128 + 128, p0=64 * half, n=64), True, True))
                    PE(mms, Dkendtm[pp] + [Dvg[pp]], [DP[bk]])
                    yield
                    TT(Stmp.v(dr * 256, [[128, 2], [1, 128]]), Sst.v(dr * 256, [[128, 2], [1, 128]]),
                       decs.v(pp * 32 + dr * 16 + cn, [[8, 2], [0, 128]]), ALU.mult, [DSst[dr]] + Ddecs[pp], [DStmp[dr]])
                    TT(Sst.s(dr * 256, dr * 256 + 256), Stmp.s(dr * 256, dr * 256 + 256), P[bk].s(dr * 256, dr * 256 + 256),
                       ALU.add, [DStmp[dr], DP[bk]], [DSst[dr]])
                    yield
                    nxt = cn + 1 if dr == 0 else cn - 1
                    if 0 <= nxt < NCH and need_q:
                        CP(Sbf.s((dr * 8 + nxt) * 256, (dr * 8 + nxt) * 256 + 256), Sst.s(dr * 256, dr * 256 + 256),
                           [DSst[dr]], [DSbf], eng="act")
            for dr in range(2):
                CP(Useg.s((sidx * 2 + dr) * 256, (sidx * 2 + dr) * 256 + 256), Sst.s(dr * 256, dr * 256 + 256), [DSst[dr]],
                   [Dsum[sidx]])
            if not need_q:
                return
            for h in range(4):
                hp, hh = divmod(h, 2)
                ob = 2
                for tt in range(NT):
                    cs = slice(tt * 128, tt * 128 + 128)
                    ab = 1
                    ao = (tt % 2) * 256
                    mms = []
                    for dr in range(2):
                        dh = dr * 2 + hp
                        mms.append((P[ab].s(ao + dr * 128, ao + dr * 128 + 128),
                                    kinv.s((pp * 4 + dh) * SA + tt * 128, (pp * 4 + dh) * SA + tt * 128 + 128, p0=64 * hh, n=64),
                                    qdec.s((pp * 4 + dh) * SA + tt * 128, (pp * 4 + dh) * SA + tt * 128 + 128, p0=64 * hh, n=64), True, True))
                    PE(mms, [Dkinv[pp][hp], Dkinv[pp][2 + hp], Dqdec[pp][hp], Dqdec[pp][2 + hp]], [DP[ab]])
                    yield
                    ai = tt % 2
                    TT(Abf[ai].s(0, 128), P[ab].s(ao, ao + 128), mfb, ALU.mult, [DP[ab], Dcstb], [DAbf[ai]])
                    TT(Abf[ai].s(128, 256), P[ab].s(ao + 128, ao + 256), mbb, ALU.mult, [DP[ab], Dcstb], [DAbf[ai]])
                    mms = [(P[ob].s(tt * 128, tt * 128 + 128), vg.s(pp * 2048 + tt * 512 + h * 128, pp * 2048 + tt * 512 + h * 128 + 128),
                            Abf[ai].s(0, 128), True, False),
                           (P[ob].s(tt * 128, tt * 128 + 128), vg.s(pp * 2048 + tt * 512 + h * 128, pp * 2048 + tt * 512 + h * 128 + 128),
                            Abf[ai].s(128, 256), False, False)]
                    for cn in (2 * tt, 2 * tt + 1):
                        for dr in range(2):
                            if (dr == 0 and cn == 0) or (dr == 1 and cn == NCH - 1):
                                continue
                            dh = dr * 2 + hp
                            mms.append((P[ob].s(cn * 64, cn * 64 + 64),
                                        Sbf.s((dr * 8 + cn) * 256 + hp * 128, (dr * 8 + cn) * 256 + hp * 128 + 128, p0=64 * hh, n=64),
                                        qdec.s((pp * 4 + dh) * SA + cn * 64, (pp * 4 + dh) * SA + cn * 64 + 64, p0=64 * hh, n=64), False, False))
                    mms[-1] = mms[-1][:4] + (True,)
                    PE(mms, [Dvg[pp], DAbf[ai], DSbf, Dqdec[pp][hp], Dqdec[pp][2 + hp]], [DP[ob]])
                    yield
                CP(o_loc.s(h * TOK + tok0, h * TOK + tok0 + N), P[ob].s(0, N), [DP[ob]], [Doloc[sidx]], eng="act")

        def drain(g_):
            for _ in g_:
                pass

        def prep_a(a):
            hb = a % 2
            for t2 in range(SA // 128):
                x_t, x_d = load_x(I["xa"][a * SA + t2 * 128: a * SA + t2 * 128 + 128, :])
                ti_ = hb * 4 + t2
                for _ in norm_T_g(x_t, x_d, 0, 8, hT, DhT[ti_ * 8:(ti_ + 1) * 8], hb * 512 + t2 * 128, 0, Dmv):
                    yield

        drain(roundrobin([pa_front(CTX, hTc, 0, DhTc, CTX, TOK, NSEG, False, 0), prep_a(0)]))

        def fr(a):
            hb = a % 2
            return pa_front(SA, hT, hb * 512, DhT[hb * 32:hb * 32 + 32], 1024, a * SA, a, True, (a + 1) % 2)

        bg = stager()

        def drain_bg(gens):
            for _ in roundrobin(gens):
                try:
                    next(bg)
                except StopIteration:
                    pass

        gens = [pa_back(CTX, TOK, NSEG, False, 0), fr(0)]
        if NSEG > 1:
            gens.append(prep_a(1))
        drain_bg(gens)
        for a in range(NSEG):
            gens = [pa_back(SA, a * SA, a, True, (a + 1) % 2)]
            if a + 1 < NSEG:
                gens.append(fr(a + 1))
            if a + 2 < NSEG:
                gens.append(prep_a(a + 2))
            drain_bg(gens)
        drain(bg)
        use_stg[0] = True
        tap("oloc", o_loc.s(0, 4 * TOK), 4 * TOK, BF16, Doloc)
        tap("qsd", qsd.s(0, 4 * TOK), 4 * TOK, BF16, Dqsd)
        tap("Useg", Useg.s(0, NSUM * 512), NSUM * 512, F32, Dsum)
        tap("Aseg", Aseg.s(0, NSUM * 4), NSUM * 4, F32, Dsum)
        S.barrier()
        pa.close()
        if stop in ("a0", "a"):
            xsc.close()
            S.add("sp", None, reads=tapd)
            S.emit(st)
            return nc

        snd = sb("snd", 520, F32, xsc)
        gath = sb("gath", 4 * 520, F32, xsc)
        Dsnd, Dgath = Dep("snd"), Dep("gath")
        Dccin, Dccout = Dep("ccin"), Dep("ccout")
        U_ = lambda a, d: Useg.s((a * 2 + d) * 256, (a * 2 + d) * 256 + 256)
        U3 = lambda a, d: Useg.v((a * 2 + d) * 256, [[128, 2], [1, 128]])
        Ab = lambda a, d: Aseg.v((a * 2 + d) * 2, [[1, 2], [0, 128]])
        A2 = lambda a, d: Aseg.s((a * 2 + d) * 2, (a * 2 + d) * 2 + 2)
        MEMSET(snd.s(512, 520), 0.0, [Dsnd])
        CP(snd.s(0, 256), U_(0, 0), Dsum, [Dsnd])
        CP(snd.s(512, 514), A2(0, 0), Dsum, [Dsnd])
        for a in range(1, NSEG):
            TT(snd.v(0, [[128, 2], [1, 128]]), snd.v(0, [[128, 2], [1, 128]]), Ab(a, 0), ALU.mult, Dsum, [Dsnd])
            TT(snd.s(0, 256), snd.s(0, 256), U_(a, 0), ALU.add, Dsum, [Dsnd])
            TT(snd.s(512, 514), snd.s(512, 514), A2(a, 0), ALU.mult, Dsum, [Dsnd])
        CP(snd.s(256, 512), U_(NSEG - 1, 1), Dsum, [Dsnd])
        CP(snd.s(514, 516), A2(NSEG - 1, 1), Dsum, [Dsnd])
        for a in range(NSEG - 2, -1, -1):
            TT(snd.v(256, [[128, 2], [1, 128]]), snd.v(256, [[128, 2], [1, 128]]), Ab(a, 1), ALU.mult, Dsum, [Dsnd])
            TT(snd.s(256, 512), snd.s(256, 512), U_(a, 1), ALU.add, Dsum, [Dsnd])
            TT(snd.s(514, 516), snd.s(514, 516), A2(a, 1), ALU.mult, Dsum, [Dsnd])
        DMA("sp", cc_in.ap(), snd.s(0, 520), [Dsnd], [Dccin], Dccin)
        S.add("pool", lambda e, s_: e.collective_compute("AllGather", ALU.bypass, replica_groups=[[0, 1, 2, 3], [4, 5, 6, 7]],
                                                         ins=[cc_in.ap().opt()], outs=[cc_out.ap().opt()]).then_inc(s_, 1),
              reads=[Dccin], writes=[Dccout], dma=Dccout, inc=1)
        DMA("sp", gath.v(0, [[520, 4], [1, 520]]), cc_out.ap().rearrange("(r p) c -> p r c", p=128), [Dccout], [Dgath], Dgath)
        mod_late()
        Sc = sb("Sc", 2 * 256, F32, xsc)
        Sct2 = sb("Sct", 2 * 256, F32, xsc)
        DSc, DSct = [Dep("Sc0"), Dep("Sc1")], [Dep("Sct0"), Dep("Sct1")]
        DSent2 = [Dep("Sent0"), Dep("Sent1")]

        def compose(d):
            Sd = Sc.s(d * 256, d * 256 + 256)
            Sd3 = Sc.v(d * 256, [[128, 2], [1, 128]])
            St = Sct2.s(d * 256, d * 256 + 256)
            St3 = Sct2.v(d * 256, [[128, 2], [1, 128]])
            CP(Sd, U_(NSEG, d), Dsum, [DSc[d]])
            yield
            order = range(4) if d == 0 else range(3, -1, -1)
            for p in order:
                TT(St3, Sd3, gath.v(p * 520 + 512 + 2 * d, [[1, 2], [0, 128]]), ALU.mult, [DSc[d], Dgath], [DSct[d]])
                yield
                TT(St, St, gath.s(p * 520 + d * 256, p * 520 + d * 256 + 256), ALU.add, [Dgath], [DSct[d]])
                yield
                TT(St, St, Sd, ALU.subtract, [DSc[d]], [DSct[d]])
                yield
                STT(Sd, St, prm.s(88 + 4 * d + p, 88 + 4 * d + p + 1), Sd, ALU.mult, ALU.add, [DSct[d], Dprm], [DSc[d]])
                yield
            segs = range(NSEG) if d == 0 else range(NSEG - 1, -1, -1)
            for a in segs:
                CP(Sentbf.s((a * 2 + d) * 256, (a * 2 + d) * 256 + 256), Sd, [DSc[d]], [DSent2[d]], eng="act")
                yield
                TT(Sd3, Sd3, Ab(a, d), ALU.mult, Dsum, [DSc[d]])
                yield
                TT(Sd, Sd, U_(a, d), ALU.add, Dsum, [DSc[d]])
                yield

        drain(roundrobin([compose(0), compose(1)]))
        tap("Sent", Sentbf.s(0, NSEG * 512), NSEG * 512, BF16, DSent2)
        S.barrier()
        xsc.close()
        if stop == "x":
            S.add("sp", None, reads=tapd)
            S.emit(st)
            return nc

        pb = ExitStack()
        WS.append(sb("ws3", 4096, BF16, pb))
        WD.append(Dep("ws3"))
        kctx = sb("kctx", 4 * CTX, BF16, pb)
        vctx = sb("vctx", 2 * 520, BF16, pb)
        Dvctx = Dep("vctx")
        gq8 = sb("gq8", 1, F32, pb)
        Dgq8 = Dep("gq8")
        TS(gq8.s(0, 1), gq, 0.125, ALU.mult, [Dprm], [Dgq8])
        PTf = T(PT.h.bitcast(F32), 512)

        def make_qk_norm(stack, nslot=2):
            sqb = [sb("sqb%d" % i_, 512, BF16, stack) for i_ in range(nslot)]
            rsb = [sb("rsb%d" % i_, 512, F32, stack) for i_ in range(nslot)]
            Dsqb, Drsb = [Dep() for _ in range(nslot)], [Dep() for _ in range(nslot)]
            if nslot == 2:
                banks = [(P[0], DP[0], P[2], DP[2]), (P[1], DP[1], P[3], DP[3])]
            else:
                banks = [(P[0], DP[0], P[4], DP[4]), (P[1], DP[1], P[5], DP[5]), (P[2], DP[2], P[6], DP[6]),
                         (P[3], DP[3], PTf, DPT)]

            def qk_norm_g(proj_fn, slot, N, g_ap, g_dep, dst_ap, dst_dep):
                pb_, dpb_, sb_, dsb_ = banks[slot]
                proj_fn(pb_, dpb_)
                yield
                ACT(sqb[slot].s(0, N), pb_.s(0, N), AF.Square, [dpb_], [Dsqb[slot]])
                yield
                PE([(sb_.s(0, N), blkb, sqb[slot].s(0, N), True, True)], [Dsqb[slot], Dcstb], [dsb_])
                yield
                ACT(rsb[slot].s(0, N), sb_.s(0, N), AF.Ln, [dsb_, Dkc], [Drsb[slot]], scale=1.0 / 64, bias=kc.s(0, 1))
                yield
                ACT(rsb[slot].s(0, N), rsb[slot].s(0, N), AF.Exp, [], [Drsb[slot]], scale=-0.5)
                yield
                STT(dst_ap, pb_.s(0, N), g_ap, rsb[slot].s(0, N), ALU.mult, ALU.mult, [dpb_, Drsb[slot], g_dep], [dst_dep])
                yield
            return qk_norm_g

        def run_groups(gens, k):
            for j in range(0, len(gens), k):
                drain(roundrobin(gens[j:j + k]))

        def run_pairs(gens):
            for j in range(0, len(gens), 2):
                drain(roundrobin(gens[j:j + 2]))

        MEMSET(vctx.s(0, 2 * 520), 1.0, [Dvctx])
        scs = ExitStack()
        qk_norm_g = make_qk_norm(scs)
        ws, wd, n = loadw("wk_na", 0, 8, 0, 512)
        gens = []
        Dkctx_l = [Dep("kctx%d" % i_) for i_ in range(4)]
        for hp in range(4):
            def pj(pb_, dpb_, hp=hp, ws=ws, wd=wd):
                PE([(pb_.s(0, CTX), ws.v(kt * 512 + hp * 128, [[1, 128]]), hTc.s(kt * CTX, kt * CTX + CTX), kt == 0, kt == 7)
                    for kt in range(8)], [wd] + DhTc, [dpb_])
            gens.append(qk_norm_g(pj, hp % 2, CTX, gk, Dprm, kctx.s(hp * CTX, hp * CTX + CTX), Dkctx_l[hp]))
        run_groups(gens, 2)
        ws, wd, n = loadw("wv_na", 0, 8, 0, 512)
        for tci in range(2):
            b = 4 + tci
            PE([(P[b].s(0, 512), hTc.s(kt * CTX + tci * 128, kt * CTX + tci * 128 + 128), ws.s(kt * 512, kt * 512 + 512),
                 kt == 0, kt == 7) for kt in range(8)], [wd] + DhTc, [DP[b]])
            CP(vctx.v(tci * 520, [[65, 8], [1, 64]]), P[b].v(0, [[64, 8], [1, 64]]), [DP[b]], [Dvctx], eng="act")
        S.barrier()
        scs.close()
        tap("kctx", kctx.s(0, 4 * CTX), 4 * CTX, BF16, Dkctx_l)
        tap("vctx", vctx.s(0, 1040), 1040, BF16, [Dvctx])

        ost = [sb("ost%d" % i, 512, F32, pb) for i in range(2)]
        Dost = [Dep("ost0"), Dep("ost1")]
        Dout = [Dep("out0"), Dep("out1")]
        octr = [0]
        SPEC = {(0, 0): 1, (0, 1): 2, (NSEG - 1, 2): 3, (NSEG - 1, 3): 4}

        def halo_prep(sn):
            hTn, DhTn = hTb[sn % 2], DhTb[sn % 2]
            for tt in range(0, 8, 2):
                gs = []
                for slot, t2 in enumerate((tt, tt + 1)):
                    x_t, x_d = load_x(I["xh"][sn * 1024 + t2 * 128: sn * 1024 + t2 * 128 + 128, :])
                    gs.append(norm_T_g(x_t, x_d, 0, 8, hTn, DhTn[t2 * 8:(t2 + 1) * 8], t2 * 128, slot, Dmv))
                for _ in roundrobin(gs):
                    yield

        for s_i in range(NSEG):
            sg = ExitStack()
            o_naT = sb("o_naT", 4 * 512, BF16, sg)
            Do_naT = Dep("o_naT")
            hT, DhT = hTb[s_i % 2], DhTb[s_i % 2]
            if s_i == 0:
                drain(halo_prep(0))
            hown = lambda kt, hT=hT: hT.s(kt * 1024 + 256, kt * 1024 + 768)
            b1 = ExitStack()
            tabI = sb("tabI", 5120, BF16, b1)
            tabS = sb("tabS", 5120, BF16, b1)
            DtabI, DtabS = Dep("tabI"), Dep("tabS")
            DMA("pool", tabI.s(0, 5120), I["btab"][0], [], [DtabI], DtabI, ab=True)
            kn = sb("kn", 4 * 1024, BF16, b1)
            qn = sb("qn", 4 * 512, BF16, b1)
            vaug = sb("vaug", 8 * 520, BF16, b1)
            Dkn = [Dep("kn%d" % i_) for i_ in range(8)]
            Dqn = [Dep("qn%d" % i_) for i_ in range(4)]
            Dvaug = Dep("vaug")
            ptAB = [sb("ptAB%d" % i, 896, BF16, b1) for i in range(2)]
            ptA = [T(ptAB[i].h, 896, 0) for i in range(2)]
            ptB = [T(ptAB[i].h, 896, 512) for i in range(2)]
            DptA = [Dep("ptA0"), Dep("ptA1")]
            DptB = [Dep("ptB0"), Dep("ptB1")]
            otm = sb("otm", 512, BF16, b1)
            Dotm = Dep("otm")
            rcp = sb("rcp", 8, F32, b1)
            Drcp = Dep("rcp")
            MEMSET(vaug.s(0, 8 * 520), 1.0, [Dvaug])
            qk_norm_g = make_qk_norm(b1, 4)
            ws, wd, n = loadw("wq_na", 0, 8, 0, 512)
            gens = []
            for hp in range(4):
                def pj(pb_, dpb_, hp=hp, ws=ws, wd=wd):
                    PE([(pb_.s(0, 512), ws.v(kt * 512 + hp * 128, [[1, 128]]), hown(kt), kt == 0, kt == 7) for kt in range(8)],
                       [wd] + DhT, [dpb_])
                gens.append(qk_norm_g(pj, hp % 4, 512, gq8.s(0, 1), Dgq8, qn.s(hp * 512, hp * 512 + 512), Dqn[hp]))
            ws, wd, n = loadw("wk_na", 0, 8, 0, 512)
            for hp in range(4):
                for tg in range(2):
                    def pj(pb_, dpb_, hp=hp, tg=tg, ws=ws, wd=wd):
                        PE([(pb_.s(0, 512), ws.v(kt * 512 + hp * 128, [[1, 128]]),
                             hT.s(kt * 1024 + tg * 512, kt * 1024 + tg * 512 + 512), kt == 0, kt == 7) for kt in range(8)],
                           [wd] + DhT, [dpb_])
                    gens.append(qk_norm_g(pj, (hp * 2 + tg) % 4, 512, gk, Dprm, kn.s(hp * 1024 + tg * 512, hp * 1024 + tg * 512 + 512),
                                          Dkn[hp * 2 + tg]))
            run_groups(gens, 4)
            ws, wd, n = loadw("wv_na", 0, 8, 0, 512)
            for tt in range(8):
                b = 4 + tt % 2
                PE([(P[b].s(0, 512), hT.s(kt * 1024 + tt * 128, kt * 1024 + tt * 128 + 128), ws.s(kt * 512, kt * 512 + 512),
                     kt == 0, kt == 7) for kt in range(8)], [wd] + DhT, [DP[b]])
                CP(vaug.v(tt * 520, [[65, 8], [1, 64]]), P[b].v(0, [[64, 8], [1, 64]]), [DP[b]], [Dvaug], eng="act")
            if s_i == 0:
                tap("qn", qn.s(0, 2048), 2048, BF16, Dqn)
                tap("kn", kn.s(0, 4096), 4096, BF16, Dkn)
                tap("vaug", vaug.s(0, 8 * 520), 8 * 520, BF16, [Dvaug])
            for i in range(4):
                if stop == "b1a" or (stop in ("b1b", "b1c", "b1d", "b1e") and i == 1):
                    break
                sp_ = SPEC.get((s_i, i))
                if sp_ is None:
                    tab, Dtab = tabI, DtabI
                else:
                    DMA("pool", tabS.s(0, 5120), I["btab"][sp_], [], [DtabS], DtabS, ab=True)
                    tab, Dtab = tabS, DtabS

                def ST(h, i=i, tab=tab, Dtab=Dtab):
                    hp, hh = divmod(h, 2)
                    ba, bb_ = (0, 1) if h % 2 == 0 else (2, 3)
                    qap = qn.s(hp * 512 + i * 128, hp * 512 + i * 128 + 128, p0=64 * hh, n=64)
                    mms = []
                    for t_ in range(4):
                        o_ = P[ba].s(t_ * 128, t_ * 128 + 128)
                        mms.append((o_, kn.s(hp * 1024 + (i + t_) * 128, hp * 1024 + (i + t_) * 128 + 128, p0=64 * hh, n=64), qap, True, False))
                        mms.append((o_, identb, tab.s(h * 640 + t_ * 128, h * 640 + t_ * 128 + 128), False, True))
                    PE(mms, Dkn + Dqn + [Dtab, Dcstb], [DP[ba]])
                    mms = []
                    o_ = P[bb_].s(0, 128)
                    mms.append((o_, kn.s(hp * 1024 + (i + 4) * 128, hp * 1024 + (i + 4) * 128 + 128, p0=64 * hh, n=64), qap, True, False))
                    mms.append((o_, identb, tab.s(h * 640 + 512, h * 640 + 640), False, True))
                    for c_ in range(2):
                        mms.append((P[bb_].s(128 + c_ * 128, 256 + c_ * 128),
                                    kctx.s(hp * CTX + c_ * 128, hp * CTX + c_ * 128 + 128, p0=64 * hh, n=64), qap, True, True))
                    PE(mms, Dkn + Dqn + [Dtab, Dcstb] + Dkctx_l, [DP[bb_]])

                def EXPV(h, i=i):
                    j = h % 2
                    ba, bb_ = (0, 1) if h % 2 == 0 else (2, 3)
                    ACT(ptAB[j].s(0, 896), P[ba].s(0, 896), AF.Exp, [DP[ba], DP[bb_]], [DptA[j], DptB[j]])
                    ob = 4 + h // 4
                    oc = (h % 4) * 65
                    o_ = P[ob].s(oc, oc + 65)
                    mms = []
                    for t_ in range(4):
                        mms.append((o_, ptA[j].s(t_ * 128, t_ * 128 + 128), vaug.s((i + t_) * 520 + h * 65, (i + t_) * 520 + h * 65 + 65),
                                    t_ == 0, False))
                    mms.append((o_, ptB[j].s(0, 128), vaug.s((i + 4) * 520 + h * 65, (i + 4) * 520 + h * 65 + 65), False, False))
                    for c_ in range(2):
                        mms.append((o_, ptB[j].s(128 + c_ * 128, 256 + c_ * 128), vctx.s(c_ * 520 + h * 65, c_ * 520 + h * 65 + 65),
                                    False, c_ == 1))
                    PE(mms, [DptA[j], DptB[j], Dvaug, Dvctx], [DP[ob]])

                if stop == "b1e":
                    break
                ST(0)
                if stop == "b1c":
                    break
                if stop == "b1d":
                    EXPV(0)
                    break
                for h in range(8):
                    if h + 1 < 8:
                        ST(h + 1)
                    EXPV(h)
                    if h % 4 == 3:
                        ob = 4 + h // 4
                        g4 = h // 4
                        S.add("dve", lambda e, ob=ob, g4=g4: e.reciprocal(out=rcp.s(g4 * 4, g4 * 4 + 4), in_=P[ob].v(64, [[65, 4]])),
                              [DP[ob]], [Drcp])
                        TT(otm.v(g4 * 256, [[64, 4], [1, 64]]), P[ob].v(0, [[65, 4], [1, 64]]), rcp.v(g4 * 4, [[1, 4], [0, 64]]),
                           ALU.mult, [DP[ob], Drcp], [Dotm])
                PE([("T", PT.s(k_ * 128, k_ * 128 + 128), otm.s(k_ * 128, k_ * 128 + 128), identb) for k_ in range(4)],
                   [Dotm, Dcstb], [DPT])
                CP(o_naT.v(i * 128, [[512, 4], [1, 128]]), PT.v(0, [[128, 4], [1, 128]]), [DPT], [Do_naT])
            if s_i == 0:
                tap("o_naT", o_naT.s(0, 2048), 2048, BF16, [Do_naT])
            S.barrier()
            b1.close()
            if stop in ("b1", "b1a", "b1b", "b1c", "b1d", "b1e"):
                sg.close()
                break
            sg2 = ExitStack()
            x1 = sb("x1", 4 * 1024, F32, sg2)
            Dx1 = [Dep("x1_%d" % t_) for t_ in range(4)]
            h2T = sb("h2T", 8 * 512, BF16, sg2)
            Dh2T = [Dep("h2T%d" % i_) for i_ in range(32)]
            b2 = ExitStack()
            sog = [sb("sog%d" % i_, 512, F32, b2) for i_ in range(2)]
            ogt = [sb("ogt%d" % i_, 512, F32, b2) for i_ in range(2)]
            gt2 = [sb("gt2_%d" % i_, 512, F32, b2) for i_ in range(2)]
            gsq = [sb("gsq%d" % i_, 512, BF16, b2) for i_ in range(2)]
            Dgsq = [Dep(), Dep()]
            Dsog, Dogt, Dgt2 = [Dep(), Dep()], [Dep(), Dep()], [Dep(), Dep()]
            o_gT = sb("o_gT", 4 * 512, BF16, b2)
            Do_gT = [Dep("o_gT%d" % i_) for i_ in range(4)]
            sig = [sb("sig%d" % i, 512, F32, b2) for i in range(2)]
            Dsig = [Dep("sig0"), Dep("sig1")]
            ytmp = sb("ytmp", 4 * 512, F32, b2)
            Dytmp = Dep("ytmp")
            yT = sb("yT", 8 * 512, BF16, b2)
            DyT = Dep("yT")
            mtmp, Dmtmp = ogt[0], Dogt[0]
            for tt in range(4):
                DMA("sp", x1.s(tt * 1024, tt * 1024 + 1024), I["xa"][s_i * 512 + tt * 128: s_i * 512 + tt * 128 + 128, :], [], [Dx1[tt]],
                    Dx1[tt], ab=True)
            wso, wdo, n = loadw("w_og", 0, 8, 0, 512)

            def gla_fin(h, sl):
                hp, hh = divmod(h, 2)
                b0, b1_, b2_ = (0, 1, 2) if sl == 0 else (3, 4, 5)
                sog_, ogt_, gt2_ = sog[sl], ogt[sl], gt2[sl]
                PE([(P[b0].s(0, 512), wso.v(kt * 512 + h * 128, [[1, 128]]), hown(kt), kt == 0, kt == 7) for kt in range(8)],
                   [wdo] + DhT, [DP[b0]])
                yield
                ACT(sog_.s(0, 512), P[b0].s(0, 512), AF.Silu, [DP[b0]], [Dsog[sl]])
                yield
                PE([(P[b1_].s(0, 512), Sentbf.s((s_i * 2 + 0) * 256 + hp * 128, (s_i * 2 + 0) * 256 + hp * 128 + 128, p0=64 * hh, n=64),
                     qsd.s((0 * 2 + hp) * TOK + s_i * 512, (0 * 2 + hp) * TOK + s_i * 512 + 512, p0=64 * hh, n=64), True, False),
                    (P[b1_].s(0, 512), Sentbf.s((s_i * 2 + 1) * 256 + hp * 128, (s_i * 2 + 1) * 256 + hp * 128 + 128, p0=64 * hh, n=64),
                     qsd.s((1 * 2 + hp) * TOK + s_i * 512, (1 * 2 + hp) * TOK + s_i * 512 + 512, p0=64 * hh, n=64), False, True)],
                   DSent2 + [Dqsd[s_i]], [DP[b1_]])
                yield
                TT(ogt_.s(0, 512), P[b1_].s(0, 512), o_loc.s(h * TOK + s_i * 512, h * TOK + s_i * 512 + 512), ALU.add,
                   [DP[b1_], Doloc[s_i]], [Dogt[sl]])
                yield
                if s_i == 0 and h == 0:
                    tap("ogt", ogt_.s(0, 512), 512, F32, [Dogt[sl]])
                ACT(gsq[sl].s(0, 512), ogt_.s(0, 512), AF.Square, [Dogt[sl]], [Dgsq[sl]])
                yield
                PE([(P[b2_].s(0, 512), onesb.s(0, 128), gsq[sl].s(0, 512), True, True)], [Dgsq[sl], Dones], [DP[b2_]])
                yield
                ACT(gt2_.s(0, 512), P[b2_].s(0, 512), AF.Ln, [DP[b2_], Dkc], [Dgt2[sl]], scale=1.0 / 128, bias=kc.s(0, 1))
                yield
                ACT(gt2_.s(0, 512), gt2_.s(0, 512), AF.Exp, [], [Dgt2[sl]], scale=-0.5)
                yield
                STT(ogt_.s(0, 512), ogt_.s(0, 512), gn, gt2_.s(0, 512), ALU.mult, ALU.mult, [Dgt2[sl], Dprm], [Dogt[sl]])
                yield
                TT(o_gT.s(h * 512, h * 512 + 512), ogt_.s(0, 512), sog_.s(0, 512), ALU.mult, [Dogt[sl], Dsog[sl]], [Do_gT[h]])
                yield

            run_groups([gla_fin(h, h % 2) for h in range(4)], 2)
            if s_i == 0:
                tap("o_gT", o_gT.s(0, 2048), 2048, BF16, Do_gT)
            for half in range(2):
                for br in range(2):
                    wsb, wdb, n = loadw("w_br_na" if br == 0 else "w_br_gla", 0, 4, half * 512, 512)
                    wsm, wdm, n = loadw("w_ma" if br == 0 else "w_mb", 0, 8, half * 512, 512)
                    src, dsrc = (o_naT, [Do_naT]) if br == 0 else (o_gT, Do_gT)
                    for cj in range(4):
                        c = half * 4 + cj
                        pb_, mb_ = (0, 1) if cj % 2 == 0 else (2, 3)
                        PE([(P[pb_].s(0, 512), wsb.v(k_ * 512 + cj * 128, [[1, 128]]), src.s(k_ * 512, k_ * 512 + 512), k_ == 0, k_ == 3)
                            for k_ in range(4)], [wdb] + dsrc, [DP[pb_]])
                        PE([(P[mb_].s(0, 512), wsm.v(kt * 512 + cj * 128, [[1, 128]]), hown(kt), kt == 0, kt == 7) for kt in range(8)],
                           [wdm] + DhT, [DP[mb_]])
                        j = cj % 2
                        ACT(sig[j].s(0, 512), P[mb_].s(0, 512), AF.Sigmoid, [DP[mb_]], [Dsig[j]])
                        if br == 0:
                            TT(ytmp.s(cj * 512, cj * 512 + 512), P[pb_].s(0, 512), sig[j].s(0, 512), ALU.mult, [DP[pb_], Dsig[j]],
                               [Dytmp])
                        else:
                            TT(mtmp.s(0, 512), P[pb_].s(0, 512), sig[j].s(0, 512), ALU.mult, [DP[pb_], Dsig[j]], [Dmtmp])
                            TT(yT.s(c * 512, c * 512 + 512), mtmp.s(0, 512), ytmp.s(cj * 512, cj * 512 + 512), ALU.add,
                               [Dmtmp, Dytmp], [DyT])
            if s_i == 0:
                tap("yT", yT.s(0, 4096), 4096, BF16, [DyT])
            for half in range(2):
                ws, wd, n = loadw("w_out", 0, 8, half * 512, 512)
                for tt in range(4):
                    b = 4 + tt % 2
                    PE([(P[b].s(0, 512), yT.s(c * 512 + tt * 128, c * 512 + tt * 128 + 128), ws.s(c * 512, c * 512 + 512), c == 0, c == 7)
                        for c in range(8)], [wd, DyT], [DP[b]])
                    TT(mtmp.s(0, 512), P[b].s(0, 512), gbc.s(half * 512, half * 512 + 512), ALU.mult, [DP[b], Dgbc], [Dmtmp])
                    TT(x1.s(tt * 1024 + half * 512, tt * 1024 + half * 512 + 512), x1.s(tt * 1024 + half * 512, tt * 1024 + half * 512 + 512),
                       mtmp.s(0, 512), ALU.add, [Dmtmp], [Dx1[tt]])
            norm_tiles([(x1.s(tt * 1024, tt * 1024 + 1024), Dx1[tt], tt * 128, tt) for tt in range(4)], 32, 40, h2T, Dh2T, Dmv2)
            if s_i == 0:
                tap("x1", x1.s(0, 4096), 4096, F32, Dx1)
                tap("h2T", h2T.s(0, 4096), 4096, BF16, Dh2T)
            S.barrier()
            b2.close()
            b3 = ExitStack()
            actT = sb("actT", NF * 512, BF16, b3)
            Dact = Dep("actT")
            sgl = [sb("sgl%d" % i, 512, F32, b3) for i in range(2)]
            Dsgl = [Dep("sgl0"), Dep("sgl1")]
            def ffn_gu():
              for f2 in range(NF // 2):
                  i_ = wctr[0] % len(WS)
                  wctr[0] += 1
                  wsf, wdf = WS[i_], WD[i_]

                  def ld(e, s_, f2=f2, wsf=wsf):
                      e.dma_start(out=wsf.v(0, [[512, 8], [1, 256]]),
                                  in_=I["w_gate"][:, f2 * 256:f2 * 256 + 256].rearrange("(k p) n -> p k n", p=128)).then_inc(s_, 16)
                      e.dma_start(out=wsf.v(256, [[512, 8], [1, 256]]),
                                  in_=I["w_up"][:, f2 * 256:f2 * 256 + 256].rearrange("(k p) n -> p k n", p=128)).then_inc(s_, 16)
                  S.add("pool", ld, [], [wdf], dma=wdf, ndma=2)
                  for ff in range(2):
                      f = 2 * f2 + ff
                      bg, bu = (0, 1) if ff == 0 else (2, 3)
                      PE([(P[bg].s(0, 512), wsf.v(kt * 512 + ff * 128, [[1, 128]]), h2T.s(kt * 512, kt * 512 + 512), kt == 0, kt == 7)
                          for kt in range(8)], [wdf] + Dh2T, [DP[bg]])
                      yield
                      PE([(P[bu].s(0, 512), wsf.v(kt * 512 + 256 + ff * 128, [[1, 128]]), h2T.s(kt * 512, kt * 512 + 512), kt == 0, kt == 7)
                          for kt in range(8)], [wdf] + Dh2T, [DP[bu]])
                      yield
                      ACT(sgl[ff].s(0, 512), P[bg].s(0, 512), AF.Silu, [DP[bg]], [Dsgl[ff]])
                      yield
                      TT(actT.s(f * 512, f * 512 + 512), P[bu].s(0, 512), sgl[ff].s(0, 512), ALU.mult, [DP[bu], Dsgl[ff]], [Dact])
                      yield
            gens = [ffn_gu()]
            if s_i + 1 < NSEG:
                gens.append(halo_prep(s_i + 1))
            drain(roundrobin(gens))
            for half in range(2):
                banks = (0, 1, 2, 3) if half == 0 else (4, 5, 6, 3)
                nf4 = (NF + 3) // 4
                for f4 in range(nf4):
                    kk = min(4, NF - 4 * f4)
                    ws, wd, n = loadw("w_down", f4 * 512, kk, half * 512, 512)
                    for tt in range(4):
                        PE([(P[banks[tt]].s(0, 512), actT.s((4 * f4 + k_) * 512 + tt * 128, (4 * f4 + k_) * 512 + tt * 128 + 128),
                             ws.s(k_ * 512, k_ * 512 + 512), (f4 == 0 and k_ == 0), (f4 == nf4 - 1 and k_ == kk - 1)) for k_ in range(kk)],
                           [wd, Dact], [DP[banks[tt]]])
                for tt in range(4):
                    oi = octr[0] % 2
                    octr[0] += 1
                    b = banks[tt]
                    TT(ost[oi].s(0, 512), P[b].s(0, 512), gbc.s(1024 + half * 512, 1024 + half * 512 + 512), ALU.mult,
                       [DP[b], Dgbc, Dout[oi]], [Dost[oi]])
                    TT(ost[oi].s(0, 512), ost[oi].s(0, 512), x1.s(tt * 1024 + half * 512, tt * 1024 + half * 512 + 512), ALU.add,
                       [Dx1[tt]], [Dost[oi]])
                    DMA("sp", out[s_i * 512 + tt * 128: s_i * 512 + tt * 128 + 128, half * 512: half * 512 + 512], ost[oi].s(0, 512),
                        [Dost[oi]], [Dout[oi]], Dout[oi], nobar=True)
            S.barrier()
            b3.close()
            sg2.close()
            sg.close()
            if stop == "b3":
                break
        S.add("sp", None, reads=Dout + tapd)
        pb.close()
        S.emit(st)
    return nc


_NC_CACHE = {}


def kernel(**inputs):
    maps = prep_inputs(**inputs)
    if "nc" not in _NC_CACHE:
        _NC_CACHE["nc"] = build()
    nc = _NC_CACHE["nc"]
    res = run_bass_kernel_spmd(nc, maps, core_ids=list(range(NCORE)))
    out = np.empty((2, SEQ, D), np.float32)
    for core in range(NCORE):
        bb, r = divmod(core, 4)
        out[bb, r * TOK:(r + 1) * TOK] = np.asarray(res.results[core]["out"], dtype=np.float32)
    return out
```
